# Optimizing a Trainium2 kernel written in Bass

```python
import math
import jax
import jax.numpy as jnp
from jax import lax
import numpy as np

D_MODEL = 1024
BATCH = 16
SEQ = 2048
DEPTH = 4
DEC_BATCH = 2
DEC_SEQ = 16384
PAST_LEN = 128

N_MEM = 256
EPS = 1e-6
D_S5 = D_MODEL // 2
S5_GROUP = 16
S5_GROUPS = D_S5 // S5_GROUP
S5_STATE = 64
D_SSD = D_MODEL
SSD_HEAD_DIM = 64
SSD_HEADS = D_SSD // SSD_HEAD_DIM
SSD_GROUPS = 4
SSD_HPG = SSD_HEADS // SSD_GROUPS
SSD_STATE = 128
SSD_CONV = 5
SSD_CHUNK = 128
SSD_CONV_CH = D_SSD + 2 * SSD_GROUPS * SSD_STATE
XA_HEADS = 4
XA_HEAD_DIM = D_MODEL // XA_HEADS
D_FF = 4 * D_MODEL
IN_COLS = D_S5 + D_SSD + SSD_CONV_CH + 2 * SSD_HEADS + 2 * D_MODEL
SPLIT_IDX = (D_S5, D_S5 + D_SSD, D_S5 + D_SSD + SSD_CONV_CH, D_S5 + D_SSD + SSD_CONV_CH + 2 * SSD_HEADS)

kernel_name = 'hybrid_s5_ssd_gated_encoder'


def rmsnorm(x, g):
    xf = x.astype(jnp.float32)
    y = xf * lax.rsqrt(jnp.mean(xf * xf, axis=-1, keepdims=True) + EPS)
    return (y * g.astype(jnp.float32)).astype(x.dtype)


def _cplx_combine(e1, e2):
    a1r, a1i, b1r, b1i = e1
    a2r, a2i, b2r, b2i = e2
    return (a1r * a2r - a1i * a2i,
            a1r * a2i + a1i * a2r,
            a2r * b1r - a2i * b1i + b2r,
            a2r * b1i + a2i * b1r + b2i)


def s5_scan(u, lam_re, lam_im, log_dt, b_re, b_im, c_re, c_im):
    f32 = jnp.float32
    lam_re, lam_im = lam_re.astype(f32), lam_im.astype(f32)
    b_re, b_im, c_re, c_im = b_re.astype(f32), b_im.astype(f32), c_re.astype(f32), c_im.astype(f32)
    dt = jnp.exp(log_dt.astype(f32))[:, None]
    mag = jnp.exp(lam_re * dt)
    ar, ai = mag * jnp.cos(lam_im * dt), mag * jnp.sin(lam_im * dt)
    den = lam_re * lam_re + lam_im * lam_im
    fr = ((ar - 1.0) * lam_re + ai * lam_im) / den
    fi = (ai * lam_re - (ar - 1.0) * lam_im) / den
    bbr = fr[..., None] * b_re - fi[..., None] * b_im
    bbi = fr[..., None] * b_im + fi[..., None] * b_re
    xr = jnp.einsum('blgh,gph->blgp', u, bbr)
    xi = jnp.einsum('blgh,gph->blgp', u, bbi)
    shape = (1, u.shape[1]) + ar.shape
    a_r = jnp.broadcast_to(ar, shape)
    a_i = jnp.broadcast_to(ai, shape)
    _, _, hr, hi = lax.associative_scan(_cplx_combine, (a_r, a_i, xr, xi), axis=1)
    return jnp.einsum('blgp,ghp->blgh', hr, c_re) - jnp.einsum('blgp,ghp->blgh', hi, c_im)


def s5_mixer(u, lam_re, lam_im, log_dt, b_re, b_im, c_re, c_im, d, w_glu):
    bsz, L, _ = u.shape
    uf = u.astype(jnp.float32).reshape(bsz, L, S5_GROUPS, S5_GROUP)
    y_f = s5_scan(uf, lam_re[0], lam_im[0], log_dt[0], b_re[0], b_im[0], c_re[0], c_im[0])
    y_b = jnp.flip(s5_scan(jnp.flip(uf, 1), lam_re[1], lam_im[1], log_dt[1],
                           b_re[1], b_im[1], c_re[1], c_im[1]), 1)
    y = (y_f + y_b).reshape(bsz, L, D_S5).astype(u.dtype) + d * u
    y = jax.nn.gelu(y)
    return y * jax.nn.sigmoid(y @ w_glu)


def ssd_scan(x, dt, a, bm, cm):
    bsz, L = x.shape[:2]
    q = SSD_CHUNK
    nc = L // q
    x = x.reshape(bsz, nc, q, SSD_GROUPS, SSD_HPG, SSD_HEAD_DIM)
    dt = dt.reshape(bsz, nc, q, SSD_GROUPS, SSD_HPG)
    bm = bm.reshape(bsz, nc, q, SSD_GROUPS, SSD_STATE)
    cm = cm.reshape(bsz, nc, q, SSD_GROUPS, SSD_STATE)
    xdt = x * dt[..., None]
    a_cum = jnp.cumsum(dt * a, axis=2)
    lower = jnp.tril(jnp.ones((q, q), dtype=bool))[None, None, :, :, None, None]
    seg = a_cum[:, :, :, None] - a_cum[:, :, None, :]
    decay = jnp.exp(jnp.where(lower, seg, -jnp.inf))
    cb = jnp.einsum('bctgn,bcsgn->bctsg', cm, bm)
    y_diag = jnp.einsum('bctsgj,bcsgjp->bctgjp', cb[..., None] * decay, xdt)
    decay_to_end = jnp.exp(a_cum[:, :, -1:] - a_cum)
    states = jnp.einsum('bcsgn,bcsgj,bcsgjp->bcgjpn', bm, decay_to_end, xdt)
    chunk_decay = jnp.exp(a_cum[:, :, -1])

    def step(carry, inp):
        st, dec = inp
        return carry * dec[..., None, None] + st, carry

    init = jnp.zeros_like(states[:, 0])
    _, states_in = lax.scan(step, init, (jnp.moveaxis(states, 1, 0), jnp.moveaxis(chunk_decay, 1, 0)))
    states_in = jnp.moveaxis(states_in, 0, 1)
    y_off = jnp.einsum('bctgn,bcgjpn,bctgj->bctgjp', cm, states_in, jnp.exp(a_cum))
    return (y_diag + y_off).reshape(bsz, L, SSD_GROUPS, SSD_HPG, SSD_HEAD_DIM)


def ssd_mixer(z, xbc, dt_raw, conv_w, conv_b, a_log, dt_bias, d_skip, norm_w):
    f32 = jnp.float32
    bsz, L, _ = xbc.shape
    xbc = lax.conv_general_dilated(xbc, conv_w.astype(xbc.dtype)[:, None, :], window_strides=(1,),
                                   padding=((SSD_CONV // 2, SSD_CONV // 2),),
                                   dimension_numbers=('NWC', 'WIO', 'NWC'),
                                   feature_group_count=SSD_CONV_CH)
    xbc = jax.nn.silu(xbc + conv_b)
    xs = xbc[..., :D_SSD]
    bm = xbc[..., D_SSD:D_SSD + SSD_GROUPS * SSD_STATE]
    cm = xbc[..., D_SSD + SSD_GROUPS * SSD_STATE:]
    xh = xs.astype(f32).reshape(bsz, L, SSD_GROUPS, SSD_HPG, SSD_HEAD_DIM)
    bm = bm.astype(f32).reshape(bsz, L, SSD_GROUPS, SSD_STATE)
    cm = cm.astype(f32).reshape(bsz, L, SSD_GROUPS, SSD_STATE)
    dt = jax.nn.softplus(dt_raw.astype(f32).reshape(bsz, L, 2, SSD_GROUPS, SSD_HPG)
                         + dt_bias.astype(f32).reshape(2, SSD_GROUPS, SSD_HPG))
    a = -jnp.exp(a_log.astype(f32)).reshape(2, SSD_GROUPS, SSD_HPG)
    y_f = ssd_scan(xh, dt[:, :, 0], a[0], bm, cm)
    y_b = jnp.flip(ssd_scan(jnp.flip(xh, 1), jnp.flip(dt[:, :, 1], 1), a[1],
                            jnp.flip(bm, 1), jnp.flip(cm, 1)), 1)
    y = y_f + y_b + d_skip.astype(f32).reshape(SSD_GROUPS, SSD_HPG)[..., None] * xh
    y = y.reshape(bsz, L, D_SSD).astype(z.dtype)
    return rmsnorm(y * jax.nn.silu(z), norm_w)


def memory_attention(h, mem, w_q, w_k, w_v, w_o):
    bsz, L, _ = h.shape
    m = mem.shape[1]
    q = (h @ w_q).reshape(bsz, L, XA_HEADS, XA_HEAD_DIM)
    k = (mem @ w_k).reshape(bsz, m, XA_HEADS, XA_HEAD_DIM)
    v = (mem @ w_v).reshape(bsz, m, XA_HEADS, XA_HEAD_DIM)
    s = jnp.einsum('blhd,bmhd->bhlm', q, k).astype(jnp.float32) * (XA_HEAD_DIM ** -0.5)
    p = jax.nn.softmax(s, axis=-1).astype(v.dtype)
    o = jnp.einsum('bhlm,bmhd->blhd', p, v).reshape(bsz, L, D_MODEL)
    return o @ w_o


def encoder_trunk(x, mem, p):
    for i in range(DEPTH):
        n = rmsnorm(x, p['norm_mix'][i])
        proj = n @ p['w_in'][i]
        u, z, xbc, dt_raw, gates = jnp.split(proj, SPLIT_IDX, axis=-1)
        ya = s5_mixer(u, p['s5_lam_re'][i], p['s5_lam_im'][i], p['s5_log_dt'][i],
                      p['s5_b_re'][i], p['s5_b_im'][i], p['s5_c_re'][i], p['s5_c_im'][i],
                      p['s5_d'][i], p['s5_w_glu'][i])
        yb = ssd_mixer(z, xbc, dt_raw, p['ssd_conv_w'][i], p['ssd_conv_b'][i], p['ssd_a_log'][i],
                       p['ssd_dt_bias'][i], p['ssd_d'][i], p['ssd_norm'][i])
        g_a, g_b = jnp.split(gates, 2, axis=-1)
        merged = (jax.nn.sigmoid(g_a) * (ya @ p['w_branch_a'][i])
                  + jax.nn.sigmoid(g_b) * (yb @ p['w_branch_b'][i]))
        x = x + merged @ p['w_out'][i]
        x = x + memory_attention(rmsnorm(x, p['norm_xattn'][i]), rmsnorm(mem, p['norm_mem'][i]),
                                 p['w_q'][i], p['w_k'][i], p['w_v'][i], p['w_o'][i])
        h = rmsnorm(x, p['norm_mlp'][i]) @ p['w_up'][i]
        x = x + jnp.square(jax.nn.relu(h)) @ p['w_down'][i]
    return rmsnorm(x, p['norm_final'])


def setup_inputs(seed: int = 0) -> dict:
    key = jax.random.key(seed)
    ks = iter(jax.random.split(key, 48))
    f32 = jnp.float32

    def nrm(shape, scale):
        return jax.random.normal(next(ks), shape, f32) * scale

    def gain(shape):
        return 1.0 + nrm(shape, 0.02)

    def unif(shape, lo, hi):
        return jax.random.uniform(next(ks), shape, f32, lo, hi)

    D = D_MODEL
    x_prompt = nrm((BATCH, SEQ, D), 1.0)
    x_sample = nrm((DEC_BATCH, DEC_SEQ, D), 1.0)
    mem_prompt = nrm((BATCH, N_MEM, D), 1.0)
    mem_sample = nrm((DEC_BATCH, N_MEM, D), 1.0)
    norm_mix = gain((DEPTH, D))
    w_in = nrm((DEPTH, D, IN_COLS), D ** -0.5)
    s5_lam_re = -0.5 + nrm((DEPTH, 2, S5_GROUPS, S5_STATE), 0.01)
    s5_lam_im = (math.pi * jnp.arange(S5_STATE, dtype=f32)) + nrm((DEPTH, 2, S5_GROUPS, S5_STATE), 0.01)
    s5_log_dt = unif((DEPTH, 2, S5_GROUPS), math.log(1e-3), math.log(1e-1))
    s5_b_re = nrm((DEPTH, 2, S5_GROUPS, S5_STATE, S5_GROUP), (2 * S5_GROUP) ** -0.5)
    s5_b_im = nrm((DEPTH, 2, S5_GROUPS, S5_STATE, S5_GROUP), (2 * S5_GROUP) ** -0.5)
    s5_c_re = nrm((DEPTH, 2, S5_GROUPS, S5_GROUP, S5_STATE), S5_STATE ** -0.5)
    s5_c_im = nrm((DEPTH, 2, S5_GROUPS, S5_GROUP, S5_STATE), S5_STATE ** -0.5)
    s5_d = nrm((DEPTH, D_S5), 1.0)
    s5_w_glu = nrm((DEPTH, D_S5, D_S5), D_S5 ** -0.5)
    ssd_conv_w = nrm((DEPTH, SSD_CONV, SSD_CONV_CH), SSD_CONV ** -0.5)
    ssd_conv_b = nrm((DEPTH, SSD_CONV_CH), 0.02)
    ssd_a_log = jnp.log(unif((DEPTH, 2, SSD_HEADS), 1.0, 16.0))
    dt0 = jnp.exp(unif((DEPTH, 2, SSD_HEADS), math.log(1e-3), math.log(1e-1)))
    ssd_dt_bias = dt0 + jnp.log(-jnp.expm1(-dt0))
    ssd_d = gain((DEPTH, SSD_HEADS))
    ssd_norm = gain((DEPTH, D_SSD))
    w_branch_a = nrm((DEPTH, D_S5, D), D_S5 ** -0.5)
    w_branch_b = nrm((DEPTH, D_SSD, D), D_SSD ** -0.5)
    w_out = nrm((DEPTH, D, D), D ** -0.5)
    norm_xattn = gain((DEPTH, D))
    norm_mem = gain((DEPTH, D))
    w_q = nrm((DEPTH, D, D), D ** -0.5)
    w_k = nrm((DEPTH, D, D), D ** -0.5)
    w_v = nrm((DEPTH, D, D), D ** -0.5)
    w_o = nrm((DEPTH, D, D), D ** -0.5)
    norm_mlp = gain((DEPTH, D))
    w_up = nrm((DEPTH, D, D_FF), D ** -0.5)
    w_down = nrm((DEPTH, D_FF, D), D_FF ** -0.5)
    norm_final = gain((D,))
    return {'x_prompt': x_prompt, 'x_sample': x_sample, 'mem_prompt': mem_prompt, 'mem_sample': mem_sample,
            'norm_mix': norm_mix, 'w_in': w_in,
            's5_lam_re': s5_lam_re, 's5_lam_im': s5_lam_im, 's5_log_dt': s5_log_dt,
            's5_b_re': s5_b_re, 's5_b_im': s5_b_im, 's5_c_re': s5_c_re, 's5_c_im': s5_c_im,
            's5_d': s5_d, 's5_w_glu': s5_w_glu,
            'ssd_conv_w': ssd_conv_w, 'ssd_conv_b': ssd_conv_b, 'ssd_a_log': ssd_a_log,
            'ssd_dt_bias': ssd_dt_bias, 'ssd_d': ssd_d, 'ssd_norm': ssd_norm,
            'w_branch_a': w_branch_a, 'w_branch_b': w_branch_b, 'w_out': w_out,
            'norm_xattn': norm_xattn, 'norm_mem': norm_mem, 'w_q': w_q, 'w_k': w_k, 'w_v': w_v, 'w_o': w_o,
            'norm_mlp': norm_mlp, 'w_up': w_up, 'w_down': w_down, 'norm_final': norm_final}


def reference(x_prompt, x_sample, mem_prompt, mem_sample, norm_mix, w_in,
              s5_lam_re, s5_lam_im, s5_log_dt, s5_b_re, s5_b_im, s5_c_re, s5_c_im, s5_d, s5_w_glu,
              ssd_conv_w, ssd_conv_b, ssd_a_log, ssd_dt_bias, ssd_d, ssd_norm,
              w_branch_a, w_branch_b, w_out,
              norm_xattn, norm_mem, w_q, w_k, w_v, w_o,
              norm_mlp, w_up, w_down, norm_final):
    p = dict(norm_mix=norm_mix, w_in=w_in,
             s5_lam_re=s5_lam_re, s5_lam_im=s5_lam_im, s5_log_dt=s5_log_dt,
             s5_b_re=s5_b_re, s5_b_im=s5_b_im, s5_c_re=s5_c_re, s5_c_im=s5_c_im,
             s5_d=s5_d, s5_w_glu=s5_w_glu,
             ssd_conv_w=ssd_conv_w, ssd_conv_b=ssd_conv_b, ssd_a_log=ssd_a_log,
             ssd_dt_bias=ssd_dt_bias, ssd_d=ssd_d, ssd_norm=ssd_norm,
             w_branch_a=w_branch_a, w_branch_b=w_branch_b, w_out=w_out,
             norm_xattn=norm_xattn, norm_mem=norm_mem, w_q=w_q, w_k=w_k, w_v=w_v, w_o=w_o,
             norm_mlp=norm_mlp, w_up=w_up, w_down=w_down, norm_final=norm_final)
    y_prompt = encoder_trunk(x_prompt, mem_prompt, p)
    y_sample = encoder_trunk(x_sample, mem_sample, p)
    return (y_prompt, y_sample)
```

```python
import contextlib
import math
import numpy as np
import concourse.bass as bass
import concourse.mybir as mybir
from concourse.bass_utils import run_bass_kernel_spmd

F32 = mybir.dt.float32
BF16 = mybir.dt.bfloat16
I32 = mybir.dt.int32
AF = mybir.ActivationFunctionType
ALU = mybir.AluOpType

ENGS = ["tensor", "vector", "scalar", "gpsimd", "sync"]
D = 1024
NMEM = 256
TT = 512
NEG = -30000.0


class Buf:
    def __init__(self, t, name):
        self.t = t
        self.n = name

    def __getitem__(self, k):
        return self.t[k]


class APBuf:
    def __init__(self, ap, name):
        self.ap = ap
        self.n = name

    def __getitem__(self, k):
        return self.ap[k]


def _names(xs):
    out = []
    for x in xs:
        if x is None:
            continue
        out.append(x if isinstance(x, str) else x.n)
    return out


class Prog:
    def __init__(self, nc, same_engine_sync=False, n_dma_sems=8):
        self.nc = nc
        self.ops = {e: [] for e in ENGS}
        self.cnt = {e: 0 for e in ENGS}
        self.dma_i = {e: 0 for e in ENGS}
        self.n_dma_sems = n_dma_sems
        self.synced = {}
        self.last_w = {}
        self.reads = {}
        self.same_engine_sync = same_engine_sync
        self.sems = {}
        self.ctx = []
        self.latest = {}
        self.n_ops = 0
        self.bigset = {e: set() for e in ENGS}

    def sem(self, key):
        if key not in self.sems:
            cm = self.nc.semaphore("s_" + "_".join(str(k) for k in key))
            self.sems[key] = cm.__enter__()
            self.ctx.append(cm)
        return self.sems[key]

    def _deps(self, eng, reads, writes):
        need = {}

        def add(k, v):
            if v > need.get(k, 0):
                need[k] = v
        for r in list(reads) + list(writes):
            lw = self.last_w.get(r)
            if lw is not None:
                add(*lw)
        for w in writes:
            for k, v in self.reads.get(w, {}).items():
                add(k, v)
        out = []
        for k, v in need.items():
            if k == ("c", eng) and (eng == "tensor" or not self.same_engine_sync or v in self.bigset[eng]):
                continue
            if self.synced.get((eng, k), 0) >= v:
                continue
            self.synced[(eng, k)] = v
            out.append((k, v))
        return out

    def _record(self, key, val, reads, writes):
        self.latest[key] = val
        for r in reads:
            self.reads.setdefault(r, {})[key] = val
        for w in writes:
            self.last_w[w] = (key, val)
            self.reads[w] = {}

    def op(self, eng, fn, reads=(), writes=(), big=False):
        reads, writes = _names(reads), _names(writes)
        waits = self._deps(eng, reads, writes)
        self.cnt[eng] += 1
        if big:
            self.bigset[eng].add(self.cnt[eng])
        key = ("c", eng)
        self.sem(key)
        e = getattr(self.nc, eng)
        for k, v in waits:
            e.wait_ge(self.sem(k), v)
        fn(e).then_inc(self.sems[key], 1)
        self._record(key, self.cnt[eng], reads, writes)
        self.n_ops += 1

    def dma(self, eng, fn, reads=(), writes=()):
        reads, writes = _names(reads), _names(writes)
        waits = self._deps(eng, reads, writes)
        i = self.dma_i[eng]
        self.dma_i[eng] += 1
        key = ("d", eng, i % self.n_dma_sems)
        val = 16 * (i // self.n_dma_sems + 1)
        self.sem(key)
        e = getattr(self.nc, eng)
        for k, v in waits:
            e.wait_ge(self.sem(k), v)
        fn(e).then_inc(self.sems[key], 16)
        self._record(key, val, reads, writes)
        self.n_ops += 1

    def barrier(self, engines=("tensor", "vector", "scalar", "gpsimd", "sync")):
        snap = dict(self.latest)
        for e in engines:
            waits = []
            for k, v in snap.items():
                if k == ("c", e):
                    continue
                if self.synced.get((e, k), 0) >= v:
                    continue
                self.synced[(e, k)] = v
                waits.append((k, v))
            eh = getattr(self.nc, e)
            for k, v in waits:
                eh.wait_ge(self.sem(k), v)

    def emit(self):
        for cm in reversed(self.ctx):
            cm.__exit__(None, None, None)


class Cfg:
    def __init__(self, depth=4, seq_lens=(2048, 2048, 16384), stop_after=None, same_engine_sync=True, debug=None):
        self.debug = dict(debug or {})
        self.depth = depth
        self.seq_lens = tuple(seq_lens)
        self.offs = [int(sum(seq_lens[:i])) for i in range(len(seq_lens))]
        self.Ltot = int(sum(seq_lens))
        self.nseq = len(seq_lens)
        self.stop_after = stop_after
        self.same_engine_sync = same_engine_sync
        for L in seq_lens:
            assert L % TT == 0


W_SPECS = [
    ("w_in", 1024, 5664), ("s5_w_glu", 512, 512), ("w_branch_a", 512, 1024), ("w_branch_b", 1024, 1024),
    ("w_out", 1024, 1024), ("w_q", 1024, 1024), ("w_k", 1024, 1024), ("w_v", 1024, 1024), ("w_o", 1024, 1024),
    ("w_up", 1024, 4096), ("w_down", 4096, 1024)]

C_U, C_Z, C_X, C_DT, C_G = 0, 512, 1536, 3584, 3616


def host_consts():
    c = {}
    r = np.arange(128)
    c["c_ident"] = np.eye(128, dtype=np.float32)
    tri = np.zeros((4, 128, 128), np.float32)
    tri[0] = (r[:, None] <= r[None, :])
    tri[1] = (r[:, None] > r[None, :])
    tri[2] = (r[:, None] >= r[None, :])
    tri[3] = (r[:, None] < r[None, :])
    c["c_tri"] = np.ascontiguousarray(tri.transpose(1, 0, 2))
    mk = np.zeros((2, 128, 4, 128), np.float32)
    mk[0] = np.where(r[:, None] > r[None, :], NEG, 0.0)[:, None, :]
    mk[1] = np.where(r[:, None] < r[None, :], NEG, 0.0)[:, None, :]
    c["c_mask"] = np.ascontiguousarray(mk.transpose(1, 0, 2, 3)).reshape(128, 2, 512)
    sel = np.zeros((16, 16, 128), np.float32)
    for h in range(16):
        sel[h, h, :] = 1.0
    c["c_sel"] = sel
    sel2 = np.zeros((64, 16), np.float32)
    for h in range(16):
        sel2[h, h] = 1.0
        sel2[32 + h, h] = 1.0
    c["c_sel2"] = sel2
    nsel = np.zeros((64, 16, 128), np.float32)
    for h in range(16):
        nsel[h, h, :] = -1.0
        nsel[32 + h, h, :] = -1.0
    c["c_nsel"] = nsel.reshape(64, 2048)
    s_idx = np.tile(np.arange(8), 16)
    m5 = np.zeros((128, 2, 128), np.float32)
    m5[:, 0, :] = (s_idx[None, :] >= s_idx[:, None])
    m5[:, 1, :] = (s_idx[None, :] <= s_idx[:, None])
    c["c_m5"] = m5
    kv = np.zeros((2, 4, 8), np.float32)
    s = np.arange(8, dtype=np.float32)
    kv[0, 0] = -s; kv[0, 1] = s; kv[0, 2] = 7 - s; kv[0, 3] = s + 1
    kv[1, 0] = s; kv[1, 1] = -s; kv[1, 2] = s; kv[1, 3] = 8 - s
    c["c_kv"] = np.broadcast_to(kv.reshape(1, 64), (128, 64)).copy()
    jv = np.arange(65, dtype=np.float32)
    c["c_jv"] = np.broadcast_to(jv.reshape(1, 65), (128, 65)).copy()
    return c


def build(cfg):
    nc = bass.Bass("TRN2", target_bir_lowering=False)
    P = Prog(nc, same_engine_sync=cfg.same_engine_sync)
    DEPTH, Ltot, nseq = cfg.depth, cfg.Ltot, cfg.nseq
    NT_TILES = Ltot // TT

    def dram(name, shape, dt, kind="Internal"):
        if kind == "Internal" and name in cfg.debug.get("dump", ()):
            kind = "ExternalOutput"
        return nc.dram_tensor(name, list(shape), dt, kind=kind).ap()

    xT_in = dram("xT", [D, Ltot], F32, "ExternalInput")
    memT_in = dram("memT", [D, nseq * NMEM], F32, "ExternalInput")
    yT_out = dram("yT", [D, Ltot], F32, "ExternalOutput")
    Wd = {n: dram(n, [DEPTH, K, N], F32, "ExternalInput") for n, K, N in W_SPECS}
    vecs = dram("vecs", [DEPTH, 128, 64], F32, "ExternalInput")
    nfin = dram("nfin", [128, 8], F32, "ExternalInput")
    convw = dram("convw", [DEPTH, 128, 16 * 6], F32, "ExternalInput")
    hpar = dram("hpar", [DEPTH, 16, 4], F32, "ExternalInput")
    s5lam = dram("s5lam", [DEPTH, 2, 128, 48], F32, "ExternalInput")
    s5b = dram("s5b", [DEPTH, 2, 128, 2 * 16 * 16], F32, "ExternalInput")
    s5c = dram("s5c", [DEPTH, 2, 128, 2 * 16 * 16], F32, "ExternalInput")
    cst = {k: dram(k, v.shape, F32, "ExternalInput") for k, v in host_consts().items()}

    XT = dram("XT", [D, Ltot], F32)
    PT = dram("PT", [5120, Ltot], BF16)
    PC = dram("PC", [2048, Ltot], BF16)
    US = dram("US", [4096, Ltot // 8], BF16)
    YS = dram("YS", [4096, Ltot // 8], BF16)
    DTs = dram("DTs", [32, Ltot], F32)
    YF = dram("YF", [D, Ltot], F32)
    slab_ids = {}
    n_slabs = 0
    for l in range(DEPTH):
        for n, K, N in W_SPECS:
            if n == "w_in":
                blocks = [("u", C_U, 512), ("z0", C_Z, 512), ("z1", C_Z + 512, 512)]
                blocks += [("x%d" % i, C_X + 512 * i, 512) for i in range(4)]
                blocks += [("g%d" % i, C_G + 512 * i, 512) for i in range(4)]
                blocks += [("dt", C_DT, 32)]
                for bn, c0, cw in blocks:
                    slab_ids[(l, n, bn)] = (n_slabs, 8, c0, cw); n_slabs += 1
            elif n == "w_down":
                for i in range(8):
                    slab_ids[(l, n, i)] = (n_slabs, 32, 128 * i, 128); n_slabs += 1
            else:
                for i in range(N // 512):
                    slab_ids[(l, n, i)] = (n_slabs, K // 128, 512 * i, 512); n_slabs += 1
    WS = dram("WS", [n_slabs, 128, 4096], BF16)

    with contextlib.ExitStack() as glob:
        uniq = [0]

        def sbuf(st, name, shape, dt):
            uniq[0] += 1
            nm = "%s_%d" % (name, uniq[0])
            return Buf(st.enter_context(nc.sbuf_tensor(nm, list(shape), dt)), nm)

        psb = [Buf(glob.enter_context(nc.psum_tensor("ps%d" % i, [128, 512], F32)), "ps%d" % i) for i in range(8)]
        ps_i = [0]

        def psum():
            b = psb[ps_i[0] % 8]
            ps_i[0] += 1
            return b

        def dump_sb(name, buf, ap2d, ncols, dt):
            if name not in cfg.debug.get("dumpsb", ()):
                return
            if name in dumped:
                return
            dumped.add(name)
            dd = nc.dram_tensor("dbg_" + name, [ap2d.shape[0], ncols], dt, kind="ExternalOutput").ap()
            P.dma("gpsimd", lambda e: e.dma_start(out=dd, in_=ap2d), reads=[buf], writes=["dbg_" + name])
        dumped = set()

        ident_f = sbuf(glob, "ident_f", [128, 128], F32)
        ident_b = sbuf(glob, "ident_b", [128, 128], BF16)
        onesM = sbuf(glob, "onesM", [128, 128], BF16)
        ones1 = sbuf(glob, "ones1", [128, 128], BF16)
        onesF = sbuf(glob, "onesF", [128, 128], F32)
        tri = sbuf(glob, "tri", [128, 4, 128], F32)
        maskneg = sbuf(glob, "maskneg", [128, 2, 512], BF16)
        m5 = sbuf(glob, "m5", [128, 2, 128], F32)
        kvc = sbuf(glob, "kvc", [128, 64], F32)
        jvc = sbuf(glob, "jvc", [128, 65], F32)
        vec_sb = sbuf(glob, "vec_sb", [128, DEPTH, 64], F32)
        nfin_sb = sbuf(glob, "nfin_sb", [128, 8], F32)
        convw_sb = sbuf(glob, "convw_sb", [128, DEPTH, 96], F32)
        hpar_sb = sbuf(glob, "hpar_sb", [16, DEPTH, 4], F32)
        hder = sbuf(glob, "hder", [16, DEPTH, 4], F32)

        def ld(dst, src, eng="sync"):
            P.dma(eng, lambda e: e.dma_start(out=dst[:], in_=src), writes=[dst])

        with contextlib.ExitStack() as st0:
            tmpf = sbuf(st0, "tmpf", [128, 1024], F32)
            ld(ident_f, cst["c_ident"])
            ld(tri, cst["c_tri"])
            ld(m5, cst["c_m5"])
            ld(kvc, cst["c_kv"])
            ld(jvc, cst["c_jv"])
            ld(nfin_sb, nfin)
            P.dma("sync", lambda e: e.dma_start(out=vec_sb[:], in_=vecs.rearrange("l p c -> p l c")), writes=[vec_sb])
            P.dma("sync", lambda e: e.dma_start(out=convw_sb[:], in_=convw.rearrange("l p c -> p l c")), writes=[convw_sb])
            P.dma("sync", lambda e: e.dma_start(out=hpar_sb[:], in_=hpar.rearrange("l p c -> p l c")), writes=[hpar_sb])
            P.dma("sync", lambda e: e.dma_start(out=tmpf[:], in_=cst["c_mask"].rearrange("p a b -> p (a b)")), writes=[tmpf])
            P.op("vector", lambda e: e.tensor_copy(out=maskneg[:].rearrange("p a b -> p (a b)"), in_=tmpf[:]), [tmpf], [maskneg])
            P.op("vector", lambda e: e.tensor_copy(out=ident_b[:], in_=ident_f[:]), [ident_f], [ident_b])
            P.op("gpsimd", lambda e: e.memset(onesM[:], 1.0 / 1024.0), [], [onesM])
            P.op("gpsimd", lambda e: e.memset(ones1[:], 1.0), [], [ones1])
            P.op("gpsimd", lambda e: e.memset(onesF[:], 1.0), [], [onesF])
            P.op("scalar", lambda e: e.activation(out=hder[:], in_=hpar_sb[:], func=AF.Exp), [hpar_sb], [hder])
            P.op("vector", lambda e: e.tensor_scalar(out=hder[:], in0=hder[:], scalar1=-1.0, scalar2=None, op0=ALU.mult), [hder], [hder])
            P.barrier()

        with contextlib.ExitStack() as st1:
            wf = [sbuf(st1, "wf%d" % i, [128, 4096], F32) for i in range(2)]
            wb = [sbuf(st1, "wb%d" % i, [128, 4096], BF16) for i in range(3)]
            k = 0
            for (l, n, bn), (sid, nk, c0, cw) in slab_ids.items():
                f, b = wf[k % 2], wb[k % 3]
                src = Wd[n][l, :, c0:c0 + cw].rearrange("(kt p) c -> p kt c", p=128)
                ne = nk * cw
                P.dma("sync", lambda e, f=f, src=src, nk=nk, ne=ne: e.dma_start(
                    out=f[:, 0:ne].rearrange("p (kt c) -> p kt c", kt=nk), in_=src), writes=[f])
                ceng = ["vector", "scalar", "gpsimd"][k % 3]
                if ceng == "scalar":
                    P.op("scalar", lambda e, f=f, b=b, ne=ne: e.copy(out=b[:, 0:ne], in_=f[:, 0:ne]), [f], [b])
                else:
                    P.op(ceng, lambda e, f=f, b=b, ne=ne: e.tensor_copy(out=b[:, 0:ne], in_=f[:, 0:ne]), [f], [b])
                P.dma("gpsimd", lambda e, b=b, sid=sid, ne=ne: e.dma_start(out=WS[sid, :, 0:ne], in_=b[:, 0:ne]),
                      reads=[b], writes=["WS%d" % sid])
                k += 1
            P.barrier()

        NRING = 3
        wring = [sbuf(glob, "wr%d" % i, [128, 4096], BF16) for i in range(NRING)]
        wr_i = [0]

        def slab(key):
            sid, nk, c0, cw = slab_ids[key]
            b = wring[wr_i[0] % NRING]
            wr_i[0] += 1
            ne = nk * cw
            P.dma("sync", lambda e: e.dma_start(out=b[:, 0:ne], in_=WS[sid, :, 0:ne]), reads=["WS%d" % sid], writes=[b])
            return b, nk, cw

        def slab_view(b, nk, cw):
            return b[:, 0:nk * cw].rearrange("p (kt c) -> p kt c", kt=nk)

        def OPB(eng, fn, r=(), w=()):
            P.op(eng, fn, r, w, big=True)

        ev_i = [0]

        def evac_eng():
            ev_i[0] += 1
            return "scalar" if ev_i[0] % 2 else "vector"

        def copy_op(eng, out_ap, in_ap, reads, writes, big=True):
            if eng == "scalar":
                P.op("scalar", lambda e: e.copy(out=out_ap, in_=in_ap), reads, writes, big=big)
            else:
                P.op(eng, lambda e: e.tensor_copy(out=out_ap, in_=in_ap), reads, writes, big=big)

        def sn(buf, k):
            return "%s:%d" % (buf.n, k)

        def subs(buf, n):
            return [sn(buf, k) for k in range(n)]

        def linear_fm(act, nk, keys, ntok, evac, sub=False):
            m = 0
            for key in keys:
                b, snk, cw = slab(key)
                sv = slab_view(b, snk, cw)
                for mi in range(cw // 128):
                    ps = psum()
                    for kt in range(nk):
                        P.op("tensor", lambda e, ps=ps, sv=sv, kt=kt, mi=mi: e.matmul(
                            ps[:, 0:ntok], lhsT=sv[:, kt, mi * 128:(mi + 1) * 128], rhs=act[:, kt, 0:ntok],
                            start=(kt == 0), stop=(kt == nk - 1)), [b, sn(act, kt) if sub else act], [ps])
                    evac(m, ps)
                    m += 1

        def rmsnorm_fm(st_bufs, x, nk, ntok, gain_ap_fn, out, sub=False):
            sq, rs = st_bufs
            if not sub:
                OPB("scalar", lambda e: e.activation(out=sq[:, 0:nk, 0:ntok], in_=x[:, 0:nk, 0:ntok], func=AF.Square), [x], [sq])
            else:
                for kt in range(nk):
                    if kt % 3 == 2:
                        OPB("gpsimd", lambda e, kt=kt: e.tensor_tensor(out=sq[:, kt, 0:ntok], in0=x[:, kt, 0:ntok], in1=x[:, kt, 0:ntok], op=ALU.mult), [sn(x, kt)], [sn(sq, kt)])
                    else:
                        OPB("scalar", lambda e, kt=kt: e.activation(out=sq[:, kt, 0:ntok], in_=x[:, kt, 0:ntok], func=AF.Square), [sn(x, kt)], [sn(sq, kt)])
            ps = psum()
            for kt in range(nk):
                P.op("tensor", lambda e, kt=kt: e.matmul(ps[:, 0:ntok], lhsT=onesM[:], rhs=sq[:, kt, 0:ntok],
                                                         start=(kt == 0), stop=(kt == nk - 1)), [onesM, sn(sq, kt) if sub else sq], [ps])
            OPB("scalar", lambda e: e.activation(out=rs[:, 0:ntok], in_=ps[:, 0:ntok], func=AF.Sqrt, bias=1e-6, scale=1.0), [ps], [rs])
            if not USE_DIV:
                OPB("vector", lambda e: e.reciprocal(out=rs[:, 0:ntok], in_=rs[:, 0:ntok]), [rs], [rs])
            for kt in range(nk):
                OPB("vector", lambda e, kt=kt: e.scalar_tensor_tensor(
                    out=out[:, kt, 0:ntok], in0=x[:, kt, 0:ntok], scalar=gain_ap_fn(kt), in1=rs[:, 0:ntok],
                    op0=ALU.mult, op1=(ALU.divide if USE_DIV else ALU.mult)), [sn(x, kt) if sub else x, rs], [sn(out, kt) if sub else out])

        USE_DIV = bool(cfg.debug.get("use_div", 0))

        def vcol(l, c):
            return vec_sb[:, l, c:c + 1]

        dbg = cfg.debug if hasattr(cfg, "debug") else {}

        def xsrc(l):
            return xT_in if l == 0 else XT

        def tile_list():
            out = []
            for s in range(nseq):
                for t in range(cfg.seq_lens[s] // TT):
                    out.append((s, t, cfg.offs[s] + t * TT))
            return out

        def phase_TL(l):
            with contextlib.ExitStack() as st:
                Xs = [sbuf(st, "X%d" % i, [128, 8, TT], F32) for i in range(2)]
                Xc = [Xs[0]]
                NTb = sbuf(st, "NTb", [128, 8, TT], BF16)
                SQ = sbuf(st, "SQ", [128, 8, TT], BF16)
                RS = sbuf(st, "RS", [128, TT], F32)
                STG = [sbuf(st, "STG%d" % i, [128, 4, TT], BF16) for i in range(2)]
                DTG = sbuf(st, "DTG", [32, TT], F32)
                if l > 0:
                    QT = sbuf(st, "QT", [128, 8, TT], BF16)
                    OT = sbuf(st, "OT", [128, 8, TT], BF16)
                    HUP = sbuf(st, "HUP", [128, 32, TT], BF16)
                    ET = [sbuf(st, "ET%d" % i, [128, 2, TT], BF16) for i in range(2)]
                    RD = [sbuf(st, "RD%d" % i, [128, TT], F32) for i in range(2)]
                    RL = [sbuf(st, "RL%d" % i, [128, TT], BF16) for i in range(2)]
                    KT = sbuf(st, "KT", [128, 8, NMEM], BF16)
                    VT = sbuf(st, "VT", [128, 2, D], BF16)
                    MEMX = sbuf(st, "MEMX", [128, 8, NMEM], F32)
                    MN = sbuf(st, "MN", [128, 8, NMEM], BF16)
                    SQm = sbuf(st, "SQm", [128, 8, NMEM], BF16)
                if l == DEPTH:
                    YO = sbuf(st, "YO", [128, 8, TT], F32)
                stg_i = [0]

                def kv_for_seq(ll, s):
                    P.dma("sync", lambda e: e.dma_start(
                        out=MEMX[:], in_=memT_in[:, s * NMEM:(s + 1) * NMEM].rearrange("(kt p) m -> p kt m", p=128)), writes=[MEMX])
                    rmsnorm_fm((SQm, RS), MEMX, 8, NMEM, lambda kt: vcol(ll, 16 + kt), MN)

                    def ev_k(m, ps):
                        copy_op(evac_eng(), KT[:, m, :], ps[:, 0:NMEM], [ps], [KT])
                    linear_fm(MN, 8, [(ll, "w_k", 0), (ll, "w_k", 1)], NMEM, ev_k)
                    for i in range(2):
                        b, snk, cw = slab((ll, "w_v", i))
                        sv = slab_view(b, snk, cw)
                        for mt in range(2):
                            ps = psum()
                            for kt in range(8):
                                P.op("tensor", lambda e, ps=ps, sv=sv, kt=kt, mt=mt: e.matmul(
                                    ps[:, 0:512], lhsT=MN[:, kt, mt * 128:(mt + 1) * 128], rhs=sv[:, kt, :],
                                    start=(kt == 0), stop=(kt == 7)), [b, MN], [ps])
                            copy_op(evac_eng(), VT[:, mt, i * 512:(i + 1) * 512], ps[:, 0:512], [ps], [VT])

                def add_to_X(m, ps):
                    OPB("vector", lambda e: e.tensor_tensor(out=Xc[0][:, m, :], in0=Xc[0][:, m, :], in1=ps[:, 0:TT], op=ALU.add), [sn(Xc[0], m), ps], [sn(Xc[0], m)])

                def xattn(ll):
                    rmsnorm_fm((SQ, RS), Xc[0], 8, TT, lambda kt: vcol(ll, 8 + kt), NTb, sub=True)

                    def ev_q(m, ps):
                        copy_op(evac_eng(), QT[:, m, :], ps[:, 0:TT], [ps], [sn(QT, m)])
                    linear_fm(NTb, 8, [(ll, "w_q", 0), (ll, "w_q", 1)], TT, ev_q, sub=True)
                    for hd in range(4):
                        E = ET[hd % 2]
                        Rd = RD[hd % 2]
                        for mt in range(2):
                            ps = psum()
                            for dk in range(2):
                                P.op("tensor", lambda e, ps=ps, dk=dk, mt=mt, hd=hd: e.matmul(
                                    ps[:, 0:TT], lhsT=KT[:, 2 * hd + dk, mt * 128:(mt + 1) * 128], rhs=QT[:, 2 * hd + dk, :],
                                    start=(dk == 0), stop=(dk == 1)), [KT, sn(QT, 2 * hd + dk)], [ps])
                            OPB("scalar", lambda e, ps=ps, mt=mt, E=E: e.activation(
                                out=E[:, mt, :], in_=ps[:, 0:TT], func=AF.Exp, scale=1.0 / 16.0), [ps], [E])
                        psd = psum()
                        for mt in range(2):
                            P.op("tensor", lambda e, psd=psd, mt=mt, E=E: e.matmul(
                                psd[:, 0:TT], lhsT=ones1[:], rhs=E[:, mt, :], start=(mt == 0), stop=(mt == 1)), [ones1, E], [psd])
                        OPB("vector", lambda e, psd=psd, Rd=Rd: e.reciprocal(out=Rd[:], in_=psd[:, 0:TT]), [psd], [Rd])
                        for dk in range(2):
                            pso = psum()
                            for mt in range(2):
                                P.op("tensor", lambda e, pso=pso, mt=mt, dk=dk, hd=hd, E=E: e.matmul(
                                    pso[:, 0:TT], lhsT=VT[:, mt, (2 * hd + dk) * 128:(2 * hd + dk + 1) * 128], rhs=E[:, mt, :],
                                    start=(mt == 0), stop=(mt == 1)), [VT, E], [pso])
                            OPB("vector", lambda e, pso=pso, dk=dk, hd=hd, Rd=Rd: e.tensor_tensor(
                                out=OT[:, 2 * hd + dk, :], in0=pso[:, 0:TT], in1=Rd[:], op=ALU.mult), [pso, Rd], [sn(OT, 2 * hd + dk)])
                    linear_fm(OT, 8, [(ll, "w_o", 0), (ll, "w_o", 1)], TT, add_to_X, sub=True)

                def mlp(ll):
                    rmsnorm_fm((SQ, RS), Xc[0], 8, TT, lambda kt: vcol(ll, 24 + kt), NTb, sub=True)
                    rl_i = [0]

                    def ev_up(m, ps):
                        r = RL[rl_i[0] % 2]
                        rl_i[0] += 1
                        OPB("scalar", lambda e: e.activation(out=r[:], in_=ps[:, 0:TT], func=AF.Relu), [ps], [r])
                        OPB("gpsimd", lambda e: e.tensor_tensor(out=HUP[:, m, :], in0=r[:], in1=r[:], op=ALU.mult), [r], [sn(HUP, m)])
                    linear_fm(NTb, 8, [(ll, "w_up", i) for i in range(8)], TT, ev_up, sub=True)
                    for i in range(8):
                        b, snk, cw = slab((ll, "w_down", i))
                        sv = slab_view(b, snk, cw)
                        ps = psum()
                        for kt in range(32):
                            P.op("tensor", lambda e, ps=ps, sv=sv, kt=kt: e.matmul(
                                ps[:, 0:TT], lhsT=sv[:, kt, :], rhs=HUP[:, kt, :], start=(kt == 0), stop=(kt == 31)), [b, sn(HUP, kt)], [ps])
                        add_to_X(i, ps)

                def front(ll, t0):
                    rmsnorm_fm((SQ, RS), Xc[0], 8, TT, lambda kt: vcol(ll, kt), NTb, sub=True)
                    j0 = t0 // 8
                    sg = STG[stg_i[0] % 2]
                    stg_i[0] += 1

                    def ev_u(m, ps):
                        eng = evac_eng()
                        copy_op(eng, sg[:, m, :].rearrange("p (s j) -> p s j", s=8),
                                ps[:, 0:TT].rearrange("p (j s) -> p s j", s=8), [ps], [sg])
                    linear_fm(NTb, 8, [(ll, "w_in", "u")], TT, ev_u, sub=True)
                    for m in range(4):
                        P.dma("gpsimd", lambda e, sg=sg, m=m: e.dma_start(
                            out=US.rearrange("(m p s) j -> m p s j", p=128, s=8)[m][:, :, j0:j0 + TT // 8],
                            in_=sg[:, m, :].rearrange("p (s j) -> p s j", s=8)), reads=[sg], writes=["US"])
                    blocks = [("z0", 0), ("z1", 512)] + [("x%d" % i, 1024 + 512 * i) for i in range(4)] + \
                             [("g%d" % i, 3072 + 512 * i) for i in range(4)]
                    for bn, row0 in blocks:
                        sg = STG[stg_i[0] % 2]
                        stg_i[0] += 1

                        def ev(m, ps, sg=sg):
                            copy_op(evac_eng(), sg[:, m, :], ps[:, 0:TT], [ps], [sg])
                        linear_fm(NTb, 8, [(ll, "w_in", bn)], TT, ev, sub=True)
                        P.dma("gpsimd", lambda e, sg=sg, row0=row0: e.dma_start(
                            out=PT[row0:row0 + 512, t0:t0 + TT].rearrange("(m p) t -> p m t", p=128), in_=sg[:]),
                            reads=[sg], writes=["PT"])
                    b, snk, cw = slab((ll, "w_in", "dt"))
                    sv = slab_view(b, snk, cw)
                    ps = psum()
                    for kt in range(8):
                        P.op("tensor", lambda e, ps=ps, sv=sv, kt=kt: e.matmul(
                            ps[0:32, 0:TT], lhsT=sv[:, kt, :], rhs=NTb[:, kt, :], start=(kt == 0), stop=(kt == 7)), [b, sn(NTb, kt)], [ps])
                    OPB("vector", lambda e, ps=ps: e.tensor_copy(out=DTG[:], in_=ps[0:32, 0:TT]), [ps], [DTG])
                    P.dma("gpsimd", lambda e: e.dma_start(out=DTs[:, t0:t0 + TT], in_=DTG[:]), reads=[DTG], writes=["DTs"])

                cur_seq = -1
                for ti_, (s, t, t0) in enumerate(tile_list()):
                    Xc[0] = Xs[ti_ % 2]
                    X = Xc[0]
                    if l > 0 and s != cur_seq:
                        kv_for_seq(l - 1, s)
                        cur_seq = s
                    src = xsrc(l)
                    P.dma("sync", lambda e, src=src, t0=t0: e.dma_start(
                        out=X[:], in_=src[:, t0:t0 + TT].rearrange("(kt p) t -> p kt t", p=128)), reads=["XT"], writes=subs(X, 8))
                    if l > 0:
                        if not dbg.get("no_xattn"):
                            xattn(l - 1)
                        if not dbg.get("no_mlp"):
                            mlp(l - 1)
                    if l < DEPTH:
                        front(l, t0)
                        if l > 0:
                            P.dma("gpsimd", lambda e, t0=t0: e.dma_start(
                                out=XT[:, t0:t0 + TT].rearrange("(kt p) t -> p kt t", p=128), in_=X[:]), reads=subs(X, 8), writes=["XT"])
                    else:
                        OPB("scalar", lambda e: e.activation(out=SQ[:], in_=X[:], func=AF.Square), subs(X, 8), subs(SQ, 8))
                        ps = psum()
                        for kt in range(8):
                            P.op("tensor", lambda e, ps=ps, kt=kt: e.matmul(ps[:, 0:TT], lhsT=onesM[:], rhs=SQ[:, kt, :],
                                                                     start=(kt == 0), stop=(kt == 7)), [onesM, sn(SQ, kt)], [ps])
                        OPB("scalar", lambda e, ps=ps: e.activation(out=RS[:], in_=ps[:, 0:TT], func=AF.Sqrt, bias=1e-6, scale=1.0), [ps], [RS])
                        OPB("vector", lambda e: e.reciprocal(out=RS[:], in_=RS[:]), [RS], [RS])
                        for kt in range(8):
                            OPB("vector", lambda e, kt=kt: e.scalar_tensor_tensor(
                                out=YO[:, kt, :], in0=X[:, kt, :], scalar=nfin_sb[:, kt:kt + 1], in1=RS[:],
                                op0=ALU.mult, op1=ALU.mult), [sn(X, kt), RS, nfin_sb], [YO])
                        P.dma("gpsimd", lambda e, t0=t0: e.dma_start(
                            out=yT_out[:, t0:t0 + TT].rearrange("(kt p) t -> p kt t", p=128), in_=YO[:]), reads=[YO], writes=["yT"])
                P.barrier()
        TWO_PI = 2.0 * math.pi

        def phase_S5(l):
            NB = Ltot // TT
            with contextlib.ExitStack() as st:
                WI = sbuf(st, "WI", [128, 32, 128], BF16)
                WSF = sbuf(st, "WSF", [128, 32, 4, 64], BF16)
                WO = sbuf(st, "WO", [128, 16, 4, 128], BF16)
                TC = sbuf(st, "TC", [128, 2, 16, 65], F32)
                TS = sbuf(st, "TS", [128, 2, 16, 65], F32)
                R8C = sbuf(st, "R8C", [128, 2, 16, 2, 65], F32)
                A64 = sbuf(st, "A64", [128, 2, 2, 16], F32)

                def sincos(st2, name, ph, shape, out_sin, out_cos, rw):
                    n = int(np.prod(shape))
                    t1 = sbuf(st2, name + "_t1", [128, n], F32)
                    ti = sbuf(st2, name + "_ti", [128, n], I32)
                    t2 = sbuf(st2, name + "_t2", [128, n], F32)

                    def v(b):
                        a = b[:, 0:n]
                        if len(shape) == 2:
                            return a.rearrange("p (a b) -> p a b", a=shape[0])
                        if len(shape) == 3:
                            return a.rearrange("p (a b c) -> p a b c", a=shape[0], b=shape[1])
                        return a
                    for off, outp in ((0.0, out_sin), (0.5 * math.pi, out_cos)):
                        P.op("vector", lambda e, off=off: e.tensor_scalar(out=v(t1), in0=ph, scalar1=off, scalar2=1.0 / TWO_PI,
                                                                          op0=ALU.add, op1=ALU.mult), rw, [t1])
                        P.op("vector", lambda e: e.tensor_copy(out=ti[:], in_=t1[:]), [t1], [ti])
                        P.op("vector", lambda e: e.tensor_copy(out=t2[:], in_=ti[:]), [ti], [t2])
                        P.op("vector", lambda e: e.tensor_tensor(out=t2[:], in0=t1[:], in1=t2[:], op=ALU.subtract), [t1, t2], [t2])
                        P.op("scalar", lambda e, outp=outp: e.activation(out=outp, in_=v(t2), func=AF.Sin, scale=TWO_PI), [t2], rw)

                with contextlib.ExitStack() as sg:
                    BMr = [sbuf(sg, "BMr%d" % d, [128, 16, 16, 8], BF16) for d in range(2)]
                    BMi = [sbuf(sg, "BMi%d" % d, [128, 16, 16, 8], BF16) for d in range(2)]
                    CMr = [sbuf(sg, "CMr%d" % d, [128, 16, 16, 8], BF16) for d in range(2)]
                    CMi = [sbuf(sg, "CMi%d" % d, [128, 16, 16, 8], BF16) for d in range(2)]
                    WSr1 = sbuf(sg, "WSr", [128, 16, 16, 8], BF16)
                    WSi1 = sbuf(sg, "WSi", [128, 16, 16, 8], BF16)
                    LAM = sbuf(sg, "LAM", [128, 48], F32)
                    BB = sbuf(sg, "BB", [128, 2, 16, 16], F32)
                    CC = sbuf(sg, "CC", [128, 2, 16, 16], F32)
                    SM = sbuf(sg, "SM", [128, 16, 16], F32)
                    PH = sbuf(sg, "PH", [128, 16, 32], F32)
                    MAG = sbuf(sg, "MAG", [128, 16, 32], F32)
                    SN = sbuf(sg, "SN", [128, 16, 32], F32)
                    CS = sbuf(sg, "CS", [128, 16, 32], F32)
                    ER = sbuf(sg, "ER", [128, 16, 4, 8], F32)
                    EI = sbuf(sg, "EI", [128, 16, 4, 8], F32)
                    BBR = sbuf(sg, "BBR", [128, 16, 16], F32)
                    BBI = sbuf(sg, "BBI", [128, 16, 16], F32)
                    T4 = [sbuf(sg, "T4_%d" % i, [128, 16, 16, 8], F32) for i in range(2)]
                    PH2 = sbuf(sg, "PH2", [128, 16, 65], F32)
                    all_gen = [LAM, BB, CC, SM, PH, MAG, SN, CS, ER, EI, BBR, BBI, PH2]

                    def VV(fn, r, w):
                        P.op("vector", fn, r, w)

                    for d in range(2):
                        P.dma("sync", lambda e, d=d: e.dma_start(out=LAM[:], in_=s5lam[l, d]), writes=[LAM])
                        P.dma("sync", lambda e, d=d: e.dma_start(out=BB[:].rearrange("p c g h -> p (c g h)"), in_=s5b[l, d]), writes=[BB])
                        P.dma("sync", lambda e, d=d: e.dma_start(out=CC[:].rearrange("p c g h -> p (c g h)"), in_=s5c[l, d]), writes=[CC])
                        lre, lim, ldt = LAM[:, 0:16], LAM[:, 16:32], LAM[:, 32:48]
                        P.op("scalar", lambda e: e.activation(out=SM[:, 0, :], in_=ldt, func=AF.Exp), [LAM], [SM])
                        VV(lambda e: e.tensor_tensor(out=SM[:, 1, :], in0=lre, in1=SM[:, 0, :], op=ALU.mult), [LAM, SM], [SM])
                        VV(lambda e: e.tensor_tensor(out=SM[:, 2, :], in0=lim, in1=SM[:, 0, :], op=ALU.mult), [LAM, SM], [SM])
                        kvd = kvc[:, d * 32:(d + 1) * 32]
                        VV(lambda e, kvd=kvd: e.tensor_tensor(out=PH[:], in1=SM[:, 2, :].unsqueeze(2).to_broadcast([128, 16, 32]),
                                                              in0=kvd.unsqueeze(1).to_broadcast([128, 16, 32]), op=ALU.mult), [SM, kvc], [PH])
                        VV(lambda e, kvd=kvd: e.tensor_tensor(out=MAG[:], in1=SM[:, 1, :].unsqueeze(2).to_broadcast([128, 16, 32]),
                                                              in0=kvd.unsqueeze(1).to_broadcast([128, 16, 32]), op=ALU.mult), [SM, kvc], [MAG])
                        P.op("scalar", lambda e: e.activation(out=MAG[:], in_=MAG[:], func=AF.Exp), [MAG], [MAG])
                        with contextlib.ExitStack() as s2:
                            sincos(s2, "sc1_%d" % d, PH[:], [16, 32], SN[:], CS[:], [PH, SN, CS])
                            P.barrier()
                        if True:
                            VV(lambda e: e.tensor_tensor(out=ER[:].rearrange("p g k s -> p g (k s)"), in0=MAG[:], in1=CS[:], op=ALU.mult), [MAG, CS], [ER])
                            VV(lambda e: e.tensor_tensor(out=EI[:].rearrange("p g k s -> p g (k s)"), in0=MAG[:], in1=SN[:], op=ALU.mult), [MAG, SN], [EI])
                            VV(lambda e: e.tensor_tensor(out=PH2[:], in1=SM[:, 2, :].unsqueeze(2).to_broadcast([128, 16, 65]),
                                                         in0=jvc[:].unsqueeze(1).to_broadcast([128, 16, 65]), op=ALU.mult), [SM, jvc], [PH2])
                            VV(lambda e: e.tensor_scalar(out=PH2[:], in0=PH2[:], scalar1=8.0, scalar2=None, op0=ALU.mult), [PH2], [PH2])
                            with contextlib.ExitStack() as s2:
                                sincos(s2, "sc2_%d" % d, PH2[:], [16, 65], TS[:, d], TC[:, d], [PH2, TS, TC])
                                P.barrier()
                            P.op("scalar", lambda e: e.activation(out=SM[:, 8, :], in_=SM[:, 1, :], func=AF.Exp, scale=8.0), [SM], [SM])
                            for c in range(2):
                                P.op("gpsimd", lambda e, c=c, d=d: e.memset(R8C[:, d, :, c, :], 1.0), [R8C], [R8C])
                                VV(lambda e, c=c, d=d: e.tensor_tensor(out=R8C[:, d, :, c, :], in0=R8C[:, d, :, c, :],
                                                                       in1=SM[:, 8, :].unsqueeze(2).to_broadcast([128, 16, 65]), op=ALU.mult), [SM, R8C], [R8C])
                                P.op("gpsimd", lambda e, c=c, d=d: e.memset(R8C[:, d, :, c, 0:1], 0.0), [R8C], [R8C])
                            P.op("scalar", lambda e: e.activation(out=SM[:, 9, :], in_=SM[:, 1, :], func=AF.Exp, scale=512.0), [SM], [SM])
                            VV(lambda e: e.tensor_scalar(out=SM[:, 10, :], in0=SM[:, 2, :], scalar1=512.0, scalar2=None, op0=ALU.mult), [SM], [SM])
                            with contextlib.ExitStack() as s2:
                                sincos(s2, "sc3_%d" % d, SM[:, 10, :], [16], SM[:, 11, :], SM[:, 12, :], [SM])
                                P.barrier()
                            VV(lambda e, d=d: e.tensor_tensor(out=A64[:, d, 0, :], in0=SM[:, 9, :], in1=SM[:, 12, :], op=ALU.mult), [SM], [A64])
                            VV(lambda e, d=d: e.tensor_tensor(out=A64[:, d, 1, :], in0=SM[:, 9, :], in1=SM[:, 11, :], op=ALU.mult), [SM], [A64])
                            P.barrier()
                        if d == 0:
                            dump_sb("LAM", LAM, LAM[:], 48, F32)
                            dump_sb("SM", SM, SM[:].rearrange("p a b -> p (a b)"), 256, F32)
                            dump_sb("PH", PH, PH[:].rearrange("p a b -> p (a b)"), 512, F32)
                            dump_sb("SN", SN, SN[:].rearrange("p a b -> p (a b)"), 512, F32)
                            dump_sb("MAG", MAG, MAG[:].rearrange("p a b -> p (a b)"), 512, F32)
                            dump_sb("ER", ER, ER[:].rearrange("p g k s -> p (g k s)"), 512, F32)
                        if d == 0:
                            e1r, e1i = ER[:, :, 3, 0], EI[:, :, 3, 0]
                        else:
                            e1r, e1i = ER[:, :, 2, 1], EI[:, :, 2, 1]
                        VV(lambda e: e.tensor_scalar(out=SM[:, 6, :], in0=e1r, scalar1=-1.0, scalar2=None, op0=ALU.add), [ER], [SM])
                        VV(lambda e: e.tensor_copy(out=SM[:, 7, :], in_=e1i), [EI], [SM])
                        VV(lambda e: e.tensor_tensor(out=SM[:, 3, :], in0=lre, in1=lre, op=ALU.mult), [LAM], [SM])
                        VV(lambda e: e.tensor_tensor(out=SM[:, 13, :], in0=lim, in1=lim, op=ALU.mult), [LAM], [SM])
                        VV(lambda e: e.tensor_tensor(out=SM[:, 3, :], in0=SM[:, 3, :], in1=SM[:, 13, :], op=ALU.add), [SM], [SM])
                        VV(lambda e: e.reciprocal(out=SM[:, 3, :], in_=SM[:, 3, :]), [SM], [SM])
                        VV(lambda e: e.tensor_tensor(out=SM[:, 4, :], in0=SM[:, 6, :], in1=lre, op=ALU.mult), [SM, LAM], [SM])
                        VV(lambda e: e.tensor_tensor(out=SM[:, 13, :], in0=SM[:, 7, :], in1=lim, op=ALU.mult), [SM, LAM], [SM])
                        VV(lambda e: e.tensor_tensor(out=SM[:, 4, :], in0=SM[:, 4, :], in1=SM[:, 13, :], op=ALU.add), [SM], [SM])
                        VV(lambda e: e.tensor_tensor(out=SM[:, 4, :], in0=SM[:, 4, :], in1=SM[:, 3, :], op=ALU.mult), [SM], [SM])
                        VV(lambda e: e.tensor_tensor(out=SM[:, 5, :], in0=SM[:, 7, :], in1=lre, op=ALU.mult), [SM, LAM], [SM])
                        VV(lambda e: e.tensor_tensor(out=SM[:, 13, :], in0=SM[:, 6, :], in1=lim, op=ALU.mult), [SM, LAM], [SM])
                        VV(lambda e: e.tensor_tensor(out=SM[:, 5, :], in0=SM[:, 5, :], in1=SM[:, 13, :], op=ALU.subtract), [SM], [SM])
                        VV(lambda e: e.tensor_tensor(out=SM[:, 5, :], in0=SM[:, 5, :], in1=SM[:, 3, :], op=ALU.mult), [SM], [SM])
                        frb = SM[:, 4, :].unsqueeze(2).to_broadcast([128, 16, 16])
                        fib = SM[:, 5, :].unsqueeze(2).to_broadcast([128, 16, 16])
                        t3 = T4[0][:, :, :, 0]
                        VV(lambda e: e.tensor_tensor(out=BBR[:], in0=BB[:, 0], in1=frb, op=ALU.mult), [BB, SM], [BBR])
                        VV(lambda e: e.tensor_tensor(out=t3, in0=BB[:, 1], in1=fib, op=ALU.mult), [BB, SM], [T4[0]])
                        VV(lambda e: e.tensor_tensor(out=BBR[:], in0=BBR[:], in1=t3, op=ALU.subtract), [BBR, T4[0]], [BBR])
                        VV(lambda e: e.tensor_tensor(out=BBI[:], in0=BB[:, 1], in1=frb, op=ALU.mult), [BB, SM], [BBI])
                        VV(lambda e: e.tensor_tensor(out=t3, in0=BB[:, 0], in1=fib, op=ALU.mult), [BB, SM], [T4[0]])
                        VV(lambda e: e.tensor_tensor(out=BBI[:], in0=BBI[:], in1=t3, op=ALU.add), [BBI, T4[0]], [BBI])

                        def cprod(kind, xr, xi, outr, outi, neg_imag):
                            er = ER[:, :, kind, :].unsqueeze(2).to_broadcast([128, 16, 16, 8])
                            ei = EI[:, :, kind, :].unsqueeze(2).to_broadcast([128, 16, 16, 8])
                            xrb = xr.unsqueeze(3).to_broadcast([128, 16, 16, 8])
                            xib = xi.unsqueeze(3).to_broadcast([128, 16, 16, 8])
                            rr = [ER, EI, BBR, BBI, CC, T4[0], T4[1]]
                            VV(lambda e: e.tensor_tensor(out=T4[0][:], in0=er, in1=xrb, op=ALU.mult), rr, [T4[0]])
                            VV(lambda e: e.tensor_tensor(out=T4[1][:], in0=ei, in1=xib, op=ALU.mult), rr, [T4[1]])
                            VV(lambda e: e.tensor_tensor(out=outr[:], in0=T4[0][:], in1=T4[1][:], op=ALU.subtract), rr, [outr])
                            VV(lambda e: e.tensor_tensor(out=T4[0][:], in0=er, in1=xib, op=ALU.mult), rr, [T4[0]])
                            VV(lambda e: e.tensor_tensor(out=T4[1][:], in0=ei, in1=xrb, op=ALU.mult), rr, [T4[1]])
                            if neg_imag:
                                VV(lambda e: e.scalar_tensor_tensor(out=outi[:], in0=T4[0][:], scalar=-1.0, in1=T4[1][:],
                                                                    op0=ALU.mult, op1=ALU.subtract), rr, [outi])
                            else:
                                VV(lambda e: e.tensor_tensor(out=outi[:], in0=T4[0][:], in1=T4[1][:], op=ALU.add), rr, [outi])
                        cprod(0, BBR[:], BBI[:], BMr[d], BMi[d], False)
                        cprod(1, CC[:, 0], CC[:, 1], CMr[d], CMi[d], True)
                        cprod(2, BBR[:], BBI[:], WSr1, WSi1, False)
                        for g0 in range(0, 32, 8):
                            ps = psum()
                            psv = ps[:].bitcast(BF16)
                            for gi in range(8):
                                g = g0 + gi
                                g_lo, gh = g // 16, g % 16
                                pr = slice(g_lo * 64, (g_lo + 1) * 64)
                                for c in range(2):
                                    src = WSr1 if c == 0 else WSi1
                                    col = (gi * 2 + c) * 64
                                    P.op("tensor", lambda e, psv=psv, src=src, pr=pr, gh=gh, col=col: e.transpose(
                                        psv[:, col:col + 64], src[pr, gh].rearrange("p h s -> p (h s)"), ident_b[pr, pr]),
                                        [src, ident_b], [ps])
                            copy_op(evac_eng(), WSF[:, g0:g0 + 8, 2 * d:2 * d + 2, :],
                                    psv[:, 0:1024].rearrange("p (g c q) -> p g c q", g=8, c=2), [ps], [WSF])
                        WOr = Buf(WO.t, "WO")
                        class _V:
                            def __init__(self, ap, n):
                                self.ap = ap; self.n = n
                            def __getitem__(self, k):
                                return self.ap
                        cprod(3, CC[:, 0], CC[:, 1],
                              _V(WO[:, :, 2 * d, :].rearrange("p g (h t) -> p g h t", t=8), "WO"),
                              _V(WO[:, :, 2 * d + 1, :].rearrange("p g (h t) -> p g h t", t=8), "WO"), True)
                        P.barrier()
                    TMPA = sbuf(sg, "TMPA", [128, 128], F32)
                    TMPB = sbuf(sg, "TMPB", [128, 128], F32)
                    for g in range(32):
                        g_lo, gh = g // 16, g % 16
                        pr = slice(g_lo * 64, (g_lo + 1) * 64)
                        pss = []
                        for d in range(2):
                            ps = psum()
                            P.op("tensor", lambda e, ps=ps, d=d, pr=pr, gh=gh: e.matmul(
                                ps[:, 0:128], lhsT=BMr[d][pr, gh].rearrange("p h s -> p (h s)"),
                                rhs=CMr[d][pr, gh].rearrange("p h s -> p (h s)"), start=True, stop=False), [BMr[d], CMr[d]], [ps])
                            P.op("tensor", lambda e, ps=ps, d=d, pr=pr, gh=gh: e.matmul(
                                ps[:, 0:128], lhsT=BMi[d][pr, gh].rearrange("p h s -> p (h s)"),
                                rhs=CMi[d][pr, gh].rearrange("p h s -> p (h s)"), start=False, stop=True), [BMi[d], CMi[d]], [ps])
                            pss.append(ps)
                        VV(lambda e, ps=pss[0]: e.tensor_tensor(out=TMPA[:], in0=ps[:, 0:128], in1=m5[:, 0, :], op=ALU.mult), [pss[0], m5], [TMPA])
                        VV(lambda e, ps=pss[1]: e.tensor_tensor(out=TMPB[:], in0=ps[:, 0:128], in1=m5[:, 1, :], op=ALU.mult), [pss[1], m5], [TMPB])
                        P.op("gpsimd", lambda e, g=g: e.tensor_tensor(out=WI[:, g, :], in0=TMPA[:], in1=TMPB[:], op=ALU.add), [TMPA, TMPB], [WI])
                    P.barrier()

                dump_sb("WI", WI, WI[:].rearrange("p g m -> p (g m)"), 32 * 128, BF16)
                dump_sb("WSF", WSF, WSF[:].rearrange("p g k q -> p (g k q)"), 32 * 256, BF16)
                dump_sb("WO", WO, WO[:].rearrange("p g k m -> p (g k m)"), 16 * 512, BF16)
                dump_sb("TC", TC, TC[:].rearrange("p d g k -> p (d g k)"), 2 * 16 * 65, F32)
                dump_sb("TS", TS, TS[:].rearrange("p d g k -> p (d g k)"), 2 * 16 * 65, F32)
                dump_sb("R8C", R8C, R8C[:].rearrange("p d g c k -> p (d g c k)"), 2 * 16 * 2 * 65, F32)
                dump_sb("A64", A64, A64[:].rearrange("p d c g -> p (d c g)"), 64, F32)
                with contextlib.ExitStack() as sb_:
                    Ub = [sbuf(sb_, "Ub%d" % i, [128, 32, 64], BF16) for i in range(2)]
                    SAL = sbuf(sb_, "SAL", [128, 16, 4, 64], F32)
                    GD = sbuf(sb_, "GD", [128, 16, 2, 65], F32)
                    GO = sbuf(sb_, "GO", [128, 16, 2, 65], F32)
                    M = [sbuf(sb_, "M%d" % i, [128, 16, 64], F32) for i in range(2)]
                    HB = sbuf(sb_, "HB", [128, 16, 4, 64], BF16)
                    EB = sbuf(sb_, "EB", [128, NB, 2, 2, 16], F32)
                    CAR = sbuf(sb_, "CAR", [128, NB + 1, 2, 2, 16], F32)
                    YSTG = [sbuf(sb_, "YSTG%d" % i, [128, 8, 64], BF16) for i in range(2)]
                    CT = [sbuf(sb_, "CT%d" % i, [128, 16], F32) for i in range(3)]

                    def VV(fn, r, w):
                        P.op("vector", fn, r, w)

                    def GP(fn, r, w):
                        P.op("gpsimd", fn, r, w)

                    def VVB(fn, r, w):
                        P.op("vector", fn, r, w, big=True)

                    def GPB(fn, r, w):
                        P.op("gpsimd", fn, r, w, big=True)

                    def batch(b, final):
                        j0 = b * 64
                        U = Ub[b % 2]
                        P.dma("sync", lambda e: e.dma_start(out=U[:], in_=US.rearrange("(g r) j -> r g j", r=128)[:, :, j0:j0 + 64]),
                              reads=["US"], writes=[U])
                        for gh in range(16):
                            ps = psum()
                            for g_lo in range(2):
                                g = g_lo * 16 + gh
                                for kind in range(4):
                                    P.op("tensor", lambda e, ps=ps, g=g, g_lo=g_lo, kind=kind: e.matmul(
                                        ps[g_lo * 64:(g_lo + 1) * 64, kind * 64:(kind + 1) * 64], lhsT=WSF[:, g, kind, :], rhs=U[:, g, :],
                                        start=True, stop=True), [WSF, U], [ps])
                            copy_op(evac_eng(), SAL[:, gh].rearrange("p k j -> p (k j)"), ps[:, 0:256], [ps], [SAL])
                        for d in range(2):
                            if d == 0:
                                Sr, Si = SAL[:, :, 0, :], SAL[:, :, 1, :]
                            else:
                                Sr, Si = SAL[:, :, 2, ::-1], SAL[:, :, 3, ::-1]
                            Tc1, Ts1 = TC[:, d, :, 1:65], TS[:, d, :, 1:65]
                            Tc0, Ts0 = TC[:, d, :, 0:64], TS[:, d, :, 0:64]
                            VVB(lambda e: e.tensor_tensor(out=M[0][:], in0=Sr, in1=Tc1, op=ALU.mult), [SAL, TC], [M[0]])
                            GPB(lambda e: e.tensor_tensor(out=M[1][:], in0=Si, in1=Ts1, op=ALU.mult), [SAL, TS], [M[1]])
                            VVB(lambda e: e.tensor_tensor(out=GD[:, :, 0, 1:65], in0=M[0][:], in1=M[1][:], op=ALU.add), [M[0], M[1]], [GD])
                            VVB(lambda e: e.tensor_tensor(out=M[0][:], in0=Si, in1=Tc1, op=ALU.mult), [SAL, TC], [M[0]])
                            GPB(lambda e: e.tensor_tensor(out=M[1][:], in0=Sr, in1=Ts1, op=ALU.mult), [SAL, TS], [M[1]])
                            VVB(lambda e: e.tensor_tensor(out=GD[:, :, 1, 1:65], in0=M[0][:], in1=M[1][:], op=ALU.subtract), [M[0], M[1]], [GD])
                            if final:
                                cidx = b if d == 0 else b + 1
                                for c in range(2):
                                    VV(lambda e, c=c, cidx=cidx, d=d: e.tensor_copy(out=GD[:, :, c, 0:1], in_=CAR[:, cidx, d, c, :].unsqueeze(2)), [CAR], [GD])
                            else:
                                GP(lambda e: e.memset(GD[:, :, :, 0:1], 0.0), [GD], [GD])
                            VVB(lambda e, d=d: e.tensor_tensor_scan(
                                out=GO[:].rearrange("p g c k -> p (g c k)"), data0=R8C[:, d].rearrange("p g c k -> p (g c k)"),
                                data1=GD[:].rearrange("p g c k -> p (g c k)"), initial=0.0, op0=ALU.mult, op1=ALU.add), [R8C, GD], [GO])
                            if not final:
                                gr, gi_ = GO[:, :, 0, 64], GO[:, :, 1, 64]
                                tc, ts = TC[:, d, :, 64], TS[:, d, :, 64]
                                VV(lambda e: e.tensor_tensor(out=CT[0][:], in0=gr, in1=tc, op=ALU.mult), [GO, TC], [CT[0]])
                                VV(lambda e: e.tensor_tensor(out=CT[1][:], in0=gi_, in1=ts, op=ALU.mult), [GO, TS], [CT[1]])
                                VV(lambda e, d=d: e.tensor_tensor(out=EB[:, b, d, 0, :], in0=CT[0][:], in1=CT[1][:], op=ALU.subtract), [CT[0], CT[1]], [EB])
                                VV(lambda e: e.tensor_tensor(out=CT[0][:], in0=gr, in1=ts, op=ALU.mult), [GO, TS], [CT[0]])
                                VV(lambda e: e.tensor_tensor(out=CT[1][:], in0=gi_, in1=tc, op=ALU.mult), [GO, TC], [CT[1]])
                                VV(lambda e, d=d: e.tensor_tensor(out=EB[:, b, d, 1, :], in0=CT[0][:], in1=CT[1][:], op=ALU.add), [CT[0], CT[1]], [EB])
                            else:
                                Gr, Gi = GO[:, :, 0, 0:64], GO[:, :, 1, 0:64]
                                if d == 0:
                                    Hr, Hi = HB[:, :, 0, :], HB[:, :, 1, :]
                                else:
                                    Hr, Hi = HB[:, :, 2, ::-1], HB[:, :, 3, ::-1]
                                VVB(lambda e: e.tensor_tensor(out=M[0][:], in0=Gr, in1=Tc0, op=ALU.mult), [GO, TC], [M[0]])
                                GPB(lambda e: e.tensor_tensor(out=M[1][:], in0=Gi, in1=Ts0, op=ALU.mult), [GO, TS], [M[1]])
                                VVB(lambda e: e.tensor_tensor(out=Hr, in0=M[0][:], in1=M[1][:], op=ALU.subtract), [M[0], M[1]], [HB])
                                VVB(lambda e: e.tensor_tensor(out=M[0][:], in0=Gr, in1=Ts0, op=ALU.mult), [GO, TS], [M[0]])
                                GPB(lambda e: e.tensor_tensor(out=M[1][:], in0=Gi, in1=Tc0, op=ALU.mult), [GO, TC], [M[1]])
                                VVB(lambda e: e.tensor_tensor(out=Hi, in0=M[0][:], in1=M[1][:], op=ALU.add), [M[0], M[1]], [HB])
                        if final:
                            dump_sb("HB", HB, HB[:].rearrange("p g k j -> p (g k j)"), 16 * 4 * 64, BF16)
                            dump_sb("SAL", SAL, SAL[:].rearrange("p g k j -> p (g k j)"), 16 * 4 * 64, F32)
                            dump_sb("CAR", CAR, CAR[:].rearrange("p b d c g -> p (b d c g)"), (NB + 1) * 64, F32)
                            dump_sb("EB", EB, EB[:].rearrange("p b d c g -> p (b d c g)"), NB * 64, F32)
                            for g in range(32):
                                g_lo, gh = g // 16, g % 16
                                pr = slice(g_lo * 64, (g_lo + 1) * 64)
                                ps = psum()
                                P.op("tensor", lambda e, ps=ps, g=g: e.matmul(ps[:, 0:64], lhsT=WI[:, g, :], rhs=U[:, g, :],
                                                                              start=True, stop=False), [WI, U], [ps])
                                for kind in range(4):
                                    P.op("tensor", lambda e, ps=ps, pr=pr, gh=gh, kind=kind: e.matmul(
                                        ps[:, 0:64], lhsT=WO[pr, gh, kind, :], rhs=HB[pr, gh, kind, :], start=False, stop=(kind == 3)),
                                        [WO, HB], [ps])
                                ys = YSTG[(g // 8) % 2]
                                copy_op(evac_eng(), ys[:, g % 8, :], ps[:, 0:64], [ps], [ys])
                                if g % 8 == 7:
                                    g0 = g - 7
                                    P.dma("gpsimd", lambda e, ys=ys, g0=g0: e.dma_start(
                                        out=YS.rearrange("(g r) j -> r g j", r=128)[:, g0:g0 + 8, j0:j0 + 64], in_=ys[:]),
                                        reads=[ys], writes=["YS"])

                    for b in range(NB):
                        batch(b, False)
                    seq_first = set(o // TT for o in cfg.offs)
                    seq_last = set((o + L) // TT - 1 for o, L in zip(cfg.offs, cfg.seq_lens))

                    def cmul_add(dst_idx, src_idx, e_idx, d):
                        ar, ai = A64[:, d, 0, :], A64[:, d, 1, :]
                        cr, ci = CAR[:, src_idx, d, 0, :], CAR[:, src_idx, d, 1, :]
                        VV(lambda e: e.tensor_tensor(out=CT[0][:], in0=ar, in1=cr, op=ALU.mult), [A64, CAR], [CT[0]])
                        VV(lambda e: e.tensor_tensor(out=CT[1][:], in0=ai, in1=ci, op=ALU.mult), [A64, CAR], [CT[1]])
                        VV(lambda e: e.tensor_tensor(out=CT[0][:], in0=CT[0][:], in1=CT[1][:], op=ALU.subtract), [CT[0], CT[1]], [CT[0]])
                        VV(lambda e: e.tensor_tensor(out=CT[1][:], in0=ar, in1=ci, op=ALU.mult), [A64, CAR], [CT[1]])
                        VV(lambda e: e.tensor_tensor(out=CT[2][:], in0=ai, in1=cr, op=ALU.mult), [A64, CAR], [CT[2]])
                        VV(lambda e: e.tensor_tensor(out=CT[1][:], in0=CT[1][:], in1=CT[2][:], op=ALU.add), [CT[1], CT[2]], [CT[1]])
                        VV(lambda e: e.tensor_tensor(out=CAR[:, dst_idx, d, 0, :], in0=CT[0][:], in1=EB[:, e_idx, d, 0, :], op=ALU.add), [CT[0], EB], [CAR])
                        VV(lambda e: e.tensor_tensor(out=CAR[:, dst_idx, d, 1, :], in0=CT[1][:], in1=EB[:, e_idx, d, 1, :], op=ALU.add), [CT[1], EB], [CAR])

                    GP(lambda e: e.memset(CAR[:].rearrange("p b d c g -> p (b d c g)"), 0.0), [CAR], [CAR])
                    for b in range(NB):
                        if (b + 1) < NB and (b + 1) not in seq_first:
                            cmul_add(b + 1, b, b, 0)
                    for b in range(NB - 1, -1, -1):
                        if b not in seq_first and b - 1 >= 0:
                            cmul_add(b, b + 1, b, 1)
                    for b in range(NB):
                        batch(b, True)
                    P.barrier()
        def ssd_alloc(st):
            S = {}
            S["NSEL"] = sbuf(st, "NSEL", [64, 16, 128], BF16)
            with contextlib.ExitStack() as tmpst:
                nself = sbuf(tmpst, "NSELf", [64, 2048], F32)
                P.dma("sync", lambda e: e.dma_start(out=nself[:], in_=cst["c_nsel"]), writes=[nself])
                P.op("vector", lambda e: e.tensor_copy(out=S["NSEL"][:].rearrange("p h t -> p (h t)"), in_=nself[:]), [nself], [S["NSEL"]])
                P.barrier()
            S["XC"] = sbuf(st, "XC", [128, 16, TT], BF16)
            S["DTR"] = sbuf(st, "DTR", [16, TT], F32)
            S["DTA"] = sbuf(st, "DTA", [16, TT], F32)
            S["DTK"] = sbuf(st, "DTK", [128, 32], F32)
            S["EALL"] = sbuf(st, "EALL", [128, 48], F32)
            S["NAC"] = sbuf(st, "NAC", [128, 16], F32)
            S["ACT_"] = None
            S["XDT"] = sbuf(st, "XDT", [128, 16, 64], BF16)
            S["XDD"] = sbuf(st, "XDD", [128, 16, 64], BF16)
            S["BTK"] = sbuf(st, "BTK", [128, 512], BF16)
            S["CBS4"] = sbuf(st, "CBS4", [128, 4, 128], F32)
            S["EBC"] = [sbuf(st, "EBC%d" % i, [128, 4, 128], BF16) for i in range(4)]
            S["LEX"] = [sbuf(st, "LEX%d" % i, [128, 4, 128], BF16) for i in range(4)]
            S["GM"] = [sbuf(st, "GM%d" % i, [128, 4, 128], BF16) for i in range(4)]
            S["CH"] = [sbuf(st, "CH%d" % i, [128, 4, 128], BF16) for i in range(4)]
            S["STMP8"] = sbuf(st, "STMP8", [128, 16, 64], F32)
            S["SIN"] = sbuf(st, "SIN", [128, 16, 64], F32)
            S["SINB"] = sbuf(st, "SINB", [128, 16, 64], BF16)
            S["YB"] = sbuf(st, "YB", [128, 8, TT], F32)
            S["sel2"] = sbuf(st, "sel2", [64, 16], F32)
            P.dma("sync", lambda e: e.dma_start(out=S["sel2"][:], in_=cst["c_sel2"]), writes=[S["sel2"]])
            S["A2"] = sbuf(st, "A2", [64, 128], BF16)
            S["HT"] = sbuf(st, "HT", [64, 128], BF16)
            S["AF32"] = sbuf(st, "AF32", [64, 128], F32)
            S["A2blk"] = sbuf(st, "A2blk", [64, 16, 128], BF16)

            P.op("gpsimd", lambda e: e.memset(S["A2"][:], 0.0), [], [S["A2"]])
            return S

        def ssd_dt(l, d, S, t0):
            DTR, DTA = S["DTR"], S["DTA"]
            P.dma("sync", lambda e: e.dma_start(out=DTR[:], in_=DTs[16 * d:16 * d + 16, t0:t0 + TT]), reads=["DTs"], writes=[DTR])
            OPB("scalar", lambda e: e.activation(out=DTR[:], in_=DTR[:], func=AF.Exp, bias=hpar_sb[:, l, 2 * d + 1:2 * d + 2], scale=1.0), [DTR, hpar_sb], [DTR])
            OPB("scalar", lambda e: e.activation(out=DTR[:], in_=DTR[:], func=AF.Ln, bias=1.0, scale=1.0), [DTR], [DTR])
            OPB("vector", lambda e: e.tensor_scalar(out=DTA[:], in0=DTR[:], scalar1=hder[:, l, 2 * d:2 * d + 1], scalar2=None, op0=ALU.mult), [DTR, hder], [DTA])

        def ssd_chunk(l, d, S, c, first_chunk_of_seq):
            XC, DTR, DTK, EALL, NAC, ACT_, XDT, XDD, BTK = (S[k] for k in ("XC", "DTR", "DTK", "EALL", "NAC", "ACT_", "XDT", "XDD", "BTK"))
            DTA = S["DTA"]
            SIN, SINB, YB = S["SIN"], S["SINB"], S["YB"]
            sel2, A2, HT, AF32, A2blk, NSEL = S["sel2"], S["A2"], S["HT"], S["AF32"], S["A2blk"], S["NSEL"]
            cs = slice(c * 128, (c + 1) * 128)
            triX, tuX = tri[:, 2 * d, :], tri[:, 2 * d + 1, :]
            if first_chunk_of_seq:
                P.op("gpsimd", lambda e: e.memset(SIN[:].rearrange("p h q -> p (h q)"), 0.0), [SIN], [SIN])
                P.op("gpsimd", lambda e: e.memset(SINB[:].rearrange("p h q -> p (h q)"), 0.0), [SINB], [SINB])
            ps = psum()
            P.op("tensor", lambda e: e.transpose(ps[:, 0:16], DTR[:, cs], ident_f[0:16, 0:16]), [DTR, ident_f], [ps])
            P.op("tensor", lambda e: e.transpose(ps[:, 16:32], DTA[:, cs], ident_f[0:16, 0:16]), [DTA, ident_f], [ps])
            P.op("vector", lambda e: e.tensor_copy(out=DTK[:], in_=ps[:, 0:32]), [ps], [DTK])
            lvl = dbg.get("ssd_level", 9)
            if lvl < 2:
                return
            pe = psum()
            dta = DTK[:, 16:32]
            P.op("tensor", lambda e: e.matmul(pe[:, 0:16], lhsT=triX, rhs=dta, start=True, stop=True), [tri, DTK], [pe])
            P.op("tensor", lambda e: e.matmul(pe[:, 16:32], lhsT=tuX, rhs=dta, start=True, stop=True), [tri, DTK], [pe])
            P.op("tensor", lambda e: e.matmul(pe[:, 32:48], lhsT=onesF[:], rhs=dta, start=True, stop=True), [onesF, DTK], [pe])
            P.op("tensor", lambda e: e.matmul(pe[0:16, 64:192], lhsT=dta, rhs=triX, start=True, stop=True), [tri, DTK], [pe])
            P.op("tensor", lambda e: e.matmul(pe[32:48, 64:192], lhsT=dta, rhs=triX, start=True, stop=True), [tri, DTK], [pe])
            if lvl < 2.2:
                return
            P.op("scalar", lambda e: e.activation(out=EALL[:], in_=pe[:, 0:48], func=AF.Exp), [pe], [EALL])
            if lvl < 2.4:
                return
            P.op("scalar", lambda e: e.mul(out=NAC[:], in_=pe[:, 0:16], mul=-1.0), [pe], [NAC])
            if lvl < 2.6:
                return
            P.op("scalar", lambda e: e.copy(out=A2[0:16, :], in_=pe[0:16, 64:192]), [pe], [A2])
            P.op("scalar", lambda e: e.copy(out=HT[32:48, :], in_=pe[32:48, 64:192]), [pe], [HT])
            P.op("scalar", lambda e: e.copy(out=AF32[32:48, :], in_=pe[32:48, 64:192]), [pe], [AF32])
            P.op("vector", lambda e: e.tensor_tensor(out=A2[32:48, :], in0=AF32[32:48, :], in1=HT[32:48, :], op=ALU.subtract), [AF32, HT], [A2])
            P.op("gpsimd", lambda e: e.tensor_tensor(out=A2blk[:], in0=A2[:].unsqueeze(1).to_broadcast([64, 16, 128]),
                                                     in1=sel2[:].unsqueeze(2).to_broadcast([64, 16, 128]), op=ALU.mult), [A2, sel2], [A2blk], big=True)
            if lvl < 3:
                return
            px = psum()
            pxv = px[:].bitcast(BF16)
            for ct in range(8):
                P.op("tensor", lambda e, ct=ct: e.transpose(pxv[:, ct * 128:(ct + 1) * 128], XC[:, ct, cs], ident_b[:]), [XC, ident_b], [px])
            OPB("vector", lambda e: e.tensor_tensor(out=XDT[:], in0=pxv[:, 0:1024].rearrange("p (h q) -> p h q", q=64),
                                                     in1=DTK[:, 0:16].unsqueeze(2).to_broadcast([128, 16, 64]), op=ALU.mult), [px, DTK], [XDT])
            OPB("gpsimd", lambda e: e.tensor_tensor(out=XDD[:], in0=XDT[:], in1=EALL[:, 16:32].unsqueeze(2).to_broadcast([128, 16, 64]), op=ALU.mult), [XDT, EALL], [XDD])
            pb = psum()
            pbv = pb[:].bitcast(BF16)
            for g in range(4):
                P.op("tensor", lambda e, g=g: e.transpose(pbv[:, g * 128:(g + 1) * 128], XC[:, 8 + g, cs], ident_b[:]), [XC, ident_b], [pb])
            OPB("scalar", lambda e: e.copy(out=BTK[:], in_=pbv[:, 0:512]), [pb], [BTK])
            if lvl < 4:
                return
            CBS4, EBCs, LEXs, GMs, CHs = S["CBS4"], S["EBC"], S["LEX"], S["GM"], S["CH"]
            pc = psum()
            for g in range(4):
                P.op("tensor", lambda e, g=g: e.matmul(pc[:, g * 128:(g + 1) * 128], lhsT=XC[:, 8 + g, cs], rhs=XC[:, 12 + g, cs], start=True, stop=True), [XC], [pc])
            OPB("scalar", lambda e: e.copy(out=CBS4[:].rearrange("p g t -> p (g t)"), in_=pc[:, 0:512]), [pc], [CBS4])
            pBs = []
            for g in range(4):
                pB = psum()
                P.op("tensor", lambda e, g=g, pB=pB: e.matmul(pB[:, 0:512], lhsT=ones1[0:64, :], rhs=A2blk[:, 4 * g:4 * g + 4, :].rearrange("p j t -> p (j t)"),
                                                              start=True, stop=True), [ones1, A2blk], [pB])
                pBs.append(pB)
            for g in range(4):
                OPB("scalar", lambda e, g=g: e.activation(out=EBCs[g][:].rearrange("p j t -> p (j t)"), in_=pBs[g][:, 0:512], func=AF.Exp), [pBs[g]], [EBCs[g]])
            pMs = []
            for g in range(4):
                pM = psum()
                P.op("tensor", lambda e, g=g, pM=pM: e.matmul(pM[:, 0:512], lhsT=ones1[0:64, :], rhs=A2blk[:, 4 * g:4 * g + 4, :].rearrange("p j t -> p (j t)"),
                                                              start=True, stop=False), [ones1, A2blk], [pM])
                P.op("tensor", lambda e, pM=pM: e.matmul(pM[:, 0:512], lhsT=ident_b[:], rhs=maskneg[:, d, :], start=False, stop=False), [ident_b, maskneg], [pM])
                P.op("tensor", lambda e, g=g, pM=pM: e.matmul(pM[:, 0:512], lhsT=A2[:], rhs=NSEL[:, 4 * g:4 * g + 4, :].rearrange("p j t -> p (j t)"),
                                                              start=False, stop=True), [A2, NSEL], [pM])
                pMs.append(pM)
            for g in range(4):
                OPB("vector", lambda e, g=g: e.tensor_tensor(out=CHs[g][:], in0=EBCs[g][:], in1=XC[:, 12 + g, cs].unsqueeze(1).to_broadcast([128, 4, 128]), op=ALU.mult), [EBCs[g], XC], [CHs[g]])
            for g in range(4):
                OPB("scalar", lambda e, g=g: e.activation(out=LEXs[g][:].rearrange("p j t -> p (j t)"), in_=pMs[g][:, 0:512], func=AF.Exp), [pMs[g]], [LEXs[g]])
            for g in range(4):
                OPB("vector", lambda e, g=g: e.tensor_tensor(out=GMs[g][:], in0=LEXs[g][:], in1=CBS4[:, g, :].unsqueeze(1).to_broadcast([128, 4, 128]), op=ALU.mult), [LEXs[g], CBS4], [GMs[g]])
            if lvl < 5:
                return
            pys = [psum(), psum()]
            for g in range(4):
                py = pys[g // 2]
                for j in range(4):
                    h = 4 * g + j
                    c0 = (g % 2) * 256 + (j // 2) * 128
                    osl = py[(j % 2) * 64:(j % 2) * 64 + 64, c0:c0 + 128]
                    P.op("tensor", lambda e, osl=osl, h=h, j=j, g=g: e.matmul(osl, lhsT=XDT[:, h, :], rhs=GMs[g][:, j, :], start=True, stop=False), [XDT, GMs[g]], [py])
                    P.op("tensor", lambda e, osl=osl, h=h, j=j, g=g: e.matmul(osl, lhsT=SINB[:, h, :], rhs=CHs[g][:, j, :], start=False, stop=True), [SINB, CHs[g]], [py])
            for gp in range(2):
                OPB("vector", lambda e, gp=gp: e.tensor_tensor(out=YB[:, 4 * gp:4 * gp + 4, cs], in0=YB[:, 4 * gp:4 * gp + 4, cs],
                                                                in1=pys[gp][:, 0:512].rearrange("p (a t) -> p a t", a=4), op=ALU.add), [YB, pys[gp]], [YB])
            if lvl < 6:
                return
            pSs = [psum(), psum()]
            for g in range(4):
                pS = pSs[g // 2]
                P.op("tensor", lambda e, g=g, pS=pS: e.matmul(pS[:, (g % 2) * 256:(g % 2) * 256 + 256], lhsT=BTK[:, g * 128:(g + 1) * 128],
                                                              rhs=XDD[:, 4 * g:4 * g + 4, :].rearrange("p h q -> p (h q)"), start=True, stop=True), [BTK, XDD], [pS])
            STMP8 = S["STMP8"]
            for gp in range(2):
                OPB("gpsimd", lambda e, gp=gp: e.tensor_tensor(out=STMP8[:, 8 * gp:8 * gp + 8, :], in0=SIN[:, 8 * gp:8 * gp + 8, :],
                                                                in1=EALL[:, 32 + 8 * gp:32 + 8 * gp + 8].unsqueeze(2).to_broadcast([128, 8, 64]), op=ALU.mult), [SIN, EALL], [STMP8])
            for gp in range(2):
                OPB("vector", lambda e, gp=gp: e.tensor_tensor(out=SIN[:, 8 * gp:8 * gp + 8, :], in0=STMP8[:, 8 * gp:8 * gp + 8, :],
                                                                in1=pSs[gp][:, 0:512].rearrange("p (h q) -> p h q", q=64), op=ALU.add), [STMP8, pSs[gp]], [SIN])
            for gp in range(2):
                OPB("scalar", lambda e, gp=gp: e.copy(out=SINB[:, 8 * gp:8 * gp + 8, :], in_=SIN[:, 8 * gp:8 * gp + 8, :]), [SIN], [SINB])

        def phase_SSDF(l):
            with contextlib.ExitStack() as st:
                S = ssd_alloc(st)
                XP = [sbuf(st, "XP%d" % i, [128, 4, TT + 4], BF16) for i in range(2)]
                DIAGW = sbuf(st, "DIAGW", [128, 16, 5, 128], BF16)
                XC, YB = S["XC"], S["YB"]
                for ct in range(16):
                    for k in range(5):
                        P.op("vector", lambda e, ct=ct, k=k: e.tensor_scalar(
                            out=DIAGW[:, ct, k, :], in0=ident_f[:], scalar1=convw_sb[:, l, ct * 6 + k:ct * 6 + k + 1], scalar2=None, op0=ALU.mult),
                            [ident_f, convw_sb], [DIAGW], big=True)
                for s in range(nseq):
                    L, off = cfg.seq_lens[s], cfg.offs[s]
                    for t in range(L // TT):
                        t0 = off + t * TT
                        lo = 2 if t > 0 else 0
                        hi = 2 if t < L // TT - 1 else 0
                        for q in range(4):
                            xp = XP[q % 2]
                            if lo == 0:
                                P.op("gpsimd", lambda e, xp=xp: e.memset(xp[:, :, 0:2], 0.0), [xp], [xp])
                            if hi == 0:
                                P.op("gpsimd", lambda e, xp=xp: e.memset(xp[:, :, TT + 2:TT + 4], 0.0), [xp], [xp])
                            P.dma("sync", lambda e, xp=xp, q=q: e.dma_start(
                                out=xp[:, :, 2 - lo:TT + 2 + hi],
                                in_=PT[1024 + 512 * q:1024 + 512 * (q + 1), t0 - lo:t0 + TT + hi].rearrange("(m p) t -> p m t", p=128)),
                                reads=["PT"], writes=[xp])
                            for m in range(4):
                                ct = 4 * q + m
                                psc = psum()
                                for k in range(5):
                                    P.op("tensor", lambda e, xp=xp, m=m, ct=ct, k=k, psc=psc: e.matmul(
                                        psc[:, 0:TT], lhsT=DIAGW[:, ct, k, :], rhs=xp[:, m, k:k + TT], start=(k == 0), stop=(k == 4)),
                                        [DIAGW, xp], [psc])
                                wv = convw_sb[:, l, ct * 6:ct * 6 + 6]
                                OPB("scalar", lambda e, ct=ct, psc=psc, wv=wv: e.activation(
                                    out=XC[:, ct, :], in_=psc[:, 0:TT], func=AF.Silu, bias=wv[:, 5:6], scale=1.0), [psc, convw_sb], [XC])
                        P.dma("gpsimd", lambda e: e.dma_start(out=PC[:, t0:t0 + TT].rearrange("(m p) t -> p m t", p=128), in_=XC[:]),
                              reads=[XC], writes=["PC"])
                        if dbg.get("no_ssd"):
                            continue
                        ssd_dt(l, 0, S, t0)
                        P.op("gpsimd", lambda e: e.memset(YB[:].rearrange("p a t -> p (a t)"), 0.0), [YB], [YB])
                        for c in range(4):
                            ssd_chunk(l, 0, S, c, first_chunk_of_seq=(t == 0 and c == 0))
                        P.dma("gpsimd", lambda e: e.dma_start(out=YF[:, t0:t0 + TT].rearrange("(m p) t -> p m t", p=128), in_=YB[:]),
                              reads=[YB], writes=["YF"])
                P.barrier()
        def phase_B1(l):
            with contextlib.ExitStack() as st:
                S = ssd_alloc(st)
                XC, YB = S["XC"], S["YB"]
                X = sbuf(st, "X1", [128, 8, TT], F32)
                ZT = sbuf(st, "ZT", [128, 8, TT], BF16)
                GTt = sbuf(st, "GTt", [128, 16, TT], BF16)
                YBN = sbuf(st, "YBN", [128, 8, TT], BF16)
                RS = sbuf(st, "RS1", [128, TT], F32)
                MG = sbuf(st, "MG", [128, 8, TT], BF16)
                YSt = sbuf(st, "YSt", [128, 4, TT], BF16)
                Ut = sbuf(st, "Ut", [128, 4, TT], BF16)
                YAf = sbuf(st, "YAf", [128, 4, TT], F32)
                T1 = sbuf(st, "T1", [128, 4, TT], F32)
                SQ = APBuf(T1[:].bitcast(BF16).rearrange("p a (b c) -> p (a b) c", c=TT), T1.n)
                YG = sbuf(st, "YG", [128, 4, TT], BF16)
                YA = sbuf(st, "YA", [128, 4, TT], BF16)
                SGm = [sbuf(st, "SGm%d" % i, [128, TT], F32) for i in range(1)]
                TM = [sbuf(st, "TM%d" % i, [128, TT], F32) for i in range(1)]
                k_i = [0]
                for s in range(nseq):
                    L, off = cfg.seq_lens[s], cfg.offs[s]
                    ntile = L // TT
                    for t in range(ntile - 1, -1, -1):
                        t0 = off + t * TT
                        j0 = t0 // 8
                        P.dma("sync", lambda e: e.dma_start(out=XC[:], in_=PC[:, t0:t0 + TT].rearrange("(m p) t -> p m t", p=128)),
                              reads=["PC"], writes=[XC])
                        P.dma("sync", lambda e: e.dma_start(out=YB[:], in_=YF[:, t0:t0 + TT].rearrange("(m p) t -> p m t", p=128)),
                              reads=["YF"], writes=[YB])
                        if not dbg.get("no_ssd"):
                            ssd_dt(l, 1, S, t0)
                            for c in range(3, -1, -1):
                                ssd_chunk(l, 1, S, c, first_chunk_of_seq=(t == ntile - 1 and c == 3))
                        else:
                            OPB("gpsimd", lambda e: e.memset(YB[:].rearrange("p a t -> p (a t)"), 0.0), [YB], [YB])
                        for ct in range(8):
                            OPB("vector", lambda e, ct=ct: e.scalar_tensor_tensor(
                                out=YB[:, ct, :], in0=XC[:, ct, :], scalar=vcol(l, 40 + ct), in1=YB[:, ct, :], op0=ALU.mult, op1=ALU.add),
                                [XC, YB, vec_sb], [YB])
                        P.dma("sync", lambda e: e.dma_start(out=ZT[:], in_=PT[0:1024, t0:t0 + TT].rearrange("(m p) t -> p m t", p=128)),
                              reads=["PT"], writes=[ZT])
                        OPB("scalar", lambda e: e.activation(out=ZT[:], in_=ZT[:], func=AF.Silu), [ZT], [ZT])
                        OPB("vector", lambda e: e.tensor_tensor(out=YB[:], in0=YB[:], in1=ZT[:], op=ALU.mult), [YB, ZT], [YB])
                        rmsnorm_fm((SQ, RS), YB, 8, TT, lambda kt: vcol(l, 32 + kt), YBN)
                        for m in range(4):
                            P.dma("sync", lambda e, m=m: e.dma_start(
                                out=YSt[:, m, :].rearrange("p (s j) -> p s j", s=8),
                                in_=YS.rearrange("(m p s) j -> m p s j", p=128, s=8)[m][:, :, j0:j0 + TT // 8]), reads=["YS"], writes=[YSt])
                            P.dma("sync", lambda e, m=m: e.dma_start(
                                out=Ut[:, m, :].rearrange("p (s j) -> p s j", s=8),
                                in_=US.rearrange("(m p s) j -> m p s j", p=128, s=8)[m][:, :, j0:j0 + TT // 8]), reads=["US"], writes=[Ut])
                        if dbg.get("no_s5"):
                            OPB("gpsimd", lambda e: e.memset(YSt[:].rearrange("p a t -> p (a t)"), 0.0), [YSt], [YSt])
                        for ct in range(4):
                            OPB("vector", lambda e, ct=ct: e.scalar_tensor_tensor(
                                out=YAf[:, ct, :], in0=Ut[:, ct, :], scalar=vcol(l, 48 + ct), in1=YSt[:, ct, :], op0=ALU.mult, op1=ALU.add),
                                [Ut, YSt, vec_sb], [YAf])
                        OPB("scalar", lambda e: e.activation(out=T1[:], in_=YAf[:], func=AF.Square), [YAf], [T1])
                        OPB("vector", lambda e: e.tensor_scalar(out=T1[:], in0=T1[:], scalar1=0.044715, scalar2=1.0, op0=ALU.mult, op1=ALU.add), [T1], [T1])
                        OPB("gpsimd", lambda e: e.tensor_tensor(out=T1[:], in0=T1[:], in1=YAf[:], op=ALU.mult), [T1, YAf], [T1])
                        OPB("scalar", lambda e: e.activation(out=T1[:], in_=T1[:], func=AF.Sigmoid, scale=1.5957691216057308), [T1], [T1])
                        OPB("vector", lambda e: e.tensor_tensor(out=YG[:].rearrange("p c (j s) -> p c s j", s=8),
                                                                 in0=YAf[:].rearrange("p c (s j) -> p c s j", s=8),
                                                                 in1=T1[:].rearrange("p c (s j) -> p c s j", s=8), op=ALU.mult), [YAf, T1], [YG])

                        def ev_glu(m, ps):
                            sg_ = SGm[0]
                            k_i[0] += 1
                            OPB("scalar", lambda e: e.activation(out=sg_[:], in_=ps[:, 0:TT], func=AF.Sigmoid), [ps], [sg_])
                            OPB("vector", lambda e: e.tensor_tensor(out=YA[:, m, :], in0=YG[:, m, :], in1=sg_[:], op=ALU.mult), [YG, sg_], [YA])
                        linear_fm(YG, 4, [(l, "s5_w_glu", 0)], TT, ev_glu)
                        P.dma("sync", lambda e: e.dma_start(out=GTt[:], in_=PT[3072:5120, t0:t0 + TT].rearrange("(m p) t -> p m t", p=128)),
                              reads=["PT"], writes=[GTt])
                        for hh in range(2):
                            OPB("scalar", lambda e, hh=hh: e.activation(out=GTt[:, 8 * hh:8 * hh + 8, :], in_=GTt[:, 8 * hh:8 * hh + 8, :], func=AF.Sigmoid), [GTt], [GTt])
                        for half in range(2):
                            ba, nka, cwa = slab((l, "w_branch_a", half))
                            bb, nkb, cwb = slab((l, "w_branch_b", half))
                            sva, svb = slab_view(ba, nka, cwa), slab_view(bb, nkb, cwb)
                            for mi in range(4):
                                m = 4 * half + mi
                                psa, psb_ = psum(), psum()
                                for kt in range(4):
                                    P.op("tensor", lambda e, psa=psa, kt=kt, mi=mi, sva=sva: e.matmul(
                                        psa[:, 0:TT], lhsT=sva[:, kt, mi * 128:(mi + 1) * 128], rhs=YA[:, kt, :], start=(kt == 0), stop=(kt == 3)), [ba, YA], [psa])
                                for kt in range(8):
                                    P.op("tensor", lambda e, psb_=psb_, kt=kt, mi=mi, svb=svb: e.matmul(
                                        psb_[:, 0:TT], lhsT=svb[:, kt, mi * 128:(mi + 1) * 128], rhs=YBN[:, kt, :], start=(kt == 0), stop=(kt == 7)), [bb, YBN], [psb_])
                                tm = TM[0]
                                OPB("vector", lambda e, psa=psa, m=m, tm=tm: e.tensor_tensor(out=tm[:], in0=psa[:, 0:TT], in1=GTt[:, m, :], op=ALU.mult), [psa, GTt], [tm])
                                tm2 = SGm[0]
                                OPB("vector", lambda e, psb_=psb_, m=m, tm2=tm2: e.tensor_tensor(out=tm2[:], in0=psb_[:, 0:TT], in1=GTt[:, 8 + m, :], op=ALU.mult), [psb_, GTt], [tm2])
                                OPB("gpsimd", lambda e, m=m, tm=tm, tm2=tm2: e.tensor_tensor(out=MG[:, m, :], in0=tm[:], in1=tm2[:], op=ALU.add), [tm, tm2], [MG])
                        src = xsrc(l)
                        P.dma("sync", lambda e: e.dma_start(out=X[:], in_=src[:, t0:t0 + TT].rearrange("(kt p) t -> p kt t", p=128)), reads=["XT"], writes=[X])

                        def add_to_X(m, ps):
                            OPB("vector", lambda e: e.tensor_tensor(out=X[:, m, :], in0=X[:, m, :], in1=ps[:, 0:TT], op=ALU.add), [X, ps], [X])
                        linear_fm(MG, 8, [(l, "w_out", 0), (l, "w_out", 1)], TT, add_to_X)
                        P.dma("gpsimd", lambda e: e.dma_start(out=XT[:, t0:t0 + TT].rearrange("(kt p) t -> p kt t", p=128), in_=X[:]), reads=[X], writes=["XT"])
                P.barrier()

        for l in range(DEPTH + 1):
            phase_TL(l)
            if l < DEPTH:
                if not dbg.get("no_s5"):
                    phase_S5(l)
                phase_SSDF(l)
                phase_B1(l)
        P.barrier(engines=("gpsimd", "sync"))
        P.emit()
    return nc, P


def host_params(inp, depth):
    f = lambda a: np.ascontiguousarray(np.asarray(a, dtype=np.float32))
    out = {}
    for n, K, N in W_SPECS:
        out[n] = f(inp[n][:depth])
    vec = np.zeros((depth, 128, 64), np.float32)

    def fm(v):
        v = np.asarray(v)
        return v.reshape(depth, -1, 128).transpose(0, 2, 1)
    vec[:, :, 0:8] = fm(inp["norm_mix"][:depth])
    vec[:, :, 8:16] = fm(inp["norm_xattn"][:depth])
    vec[:, :, 16:24] = fm(inp["norm_mem"][:depth])
    vec[:, :, 24:32] = fm(inp["norm_mlp"][:depth])
    vec[:, :, 32:40] = fm(inp["ssd_norm"][:depth])
    vec[:, :, 40:48] = fm(np.repeat(np.asarray(inp["ssd_d"][:depth]), 64, axis=1))
    vec[:, :, 48:52] = fm(inp["s5_d"][:depth])
    out["vecs"] = vec
    out["nfin"] = f(np.asarray(inp["norm_final"]).reshape(8, 128).T)
    cw = np.asarray(inp["ssd_conv_w"][:depth])
    cb = np.asarray(inp["ssd_conv_b"][:depth])
    cwb = np.concatenate([cw, cb[:, None, :]], axis=1)
    out["convw"] = f(cwb.reshape(depth, 6, 16, 128).transpose(0, 3, 2, 1).reshape(depth, 128, 96))
    hp = np.zeros((depth, 16, 4), np.float32)
    hp[:, :, 0] = np.asarray(inp["ssd_a_log"][:depth])[:, 0]
    hp[:, :, 1] = np.asarray(inp["ssd_dt_bias"][:depth])[:, 0]
    hp[:, :, 2] = np.asarray(inp["ssd_a_log"][:depth])[:, 1]
    hp[:, :, 3] = np.asarray(inp["ssd_dt_bias"][:depth])[:, 1]
    out["hpar"] = hp

    def q_layout(a):
        a = np.asarray(a)
        sh = a.shape
        a = a.reshape(sh[0], sh[1], 2, 16, 64, *sh[4:])
        perm = (0, 1, 2, 4, 3) + tuple(range(5, a.ndim))
        a = a.transpose(perm)
        return a.reshape(sh[0], sh[1], 128, 16, *sh[4:])
    lam = np.zeros((depth, 2, 128, 3, 16), np.float32)
    lam[:, :, :, 0] = q_layout(inp["s5_lam_re"][:depth])
    lam[:, :, :, 1] = q_layout(inp["s5_lam_im"][:depth])
    ldt = np.broadcast_to(np.asarray(inp["s5_log_dt"][:depth])[:, :, :, None], (depth, 2, 32, 64))
    lam[:, :, :, 2] = q_layout(ldt)
    out["s5lam"] = f(lam.reshape(depth, 2, 128, 48))
    bq = np.stack([q_layout(inp["s5_b_re"][:depth]), q_layout(inp["s5_b_im"][:depth])], axis=3)
    out["s5b"] = f(bq.reshape(depth, 2, 128, 512))
    cre = np.asarray(inp["s5_c_re"][:depth]).transpose(0, 1, 2, 4, 3)
    cim = np.asarray(inp["s5_c_im"][:depth]).transpose(0, 1, 2, 4, 3)
    cq = np.stack([q_layout(cre), q_layout(cim)], axis=3)
    out["s5c"] = f(cq.reshape(depth, 2, 128, 512))
    out.update(host_consts())
    return out


_NC_CACHE = {}


def run(inp, cfg, core_seqs, n_cores):
    key = (cfg.depth, cfg.seq_lens, repr(sorted(getattr(cfg, "debug", {}).items())))
    if key not in _NC_CACHE:
        _NC_CACHE[key] = build(cfg)
    nc, P = _NC_CACHE[key]
    shared = host_params(inp, cfg.depth)
    in_maps = []
    for c in range(n_cores):
        xs = np.concatenate([np.asarray(x, np.float32) for x, m in core_seqs[c]], axis=0)
        ms = np.concatenate([np.asarray(m, np.float32) for x, m in core_seqs[c]], axis=0)
        d = dict(shared)
        d["xT"] = np.ascontiguousarray(xs.T)
        d["memT"] = np.ascontiguousarray(ms.T)
        in_maps.append(d)
    res = run_bass_kernel_spmd(nc, in_maps, core_ids=list(range(n_cores)))
    outs = []
    for c in range(n_cores):
        yT = np.asarray(res.results[c]["yT"])
        y = np.ascontiguousarray(yT.T)
        o = []
        for off, L in zip(cfg.offs, cfg.seq_lens):
            o.append(y[off:off + L])
        outs.append(o)
    return outs, res


def kernel(**inp):
    xp = np.asarray(inp["x_prompt"])
    xs = np.asarray(inp["x_sample"])
    mp = np.asarray(inp["mem_prompt"])
    ms = np.asarray(inp["mem_sample"])
    cfg = Cfg(depth=4, seq_lens=(2048, 2048, 16384))
    core_seqs = []
    for c in range(8):
        sidx = c % 2
        core_seqs.append([(xp[2 * c], mp[2 * c]), (xp[2 * c + 1], mp[2 * c + 1]), (xs[sidx], ms[sidx])])
    outs, _ = run(inp, cfg, core_seqs, 8)
    y_prompt = np.stack([outs[c][i] for c in range(8) for i in range(2)], axis=0).astype(np.float32)
    y_sample = np.stack([outs[0][2], outs[1][2]], axis=0).astype(np.float32)
    return (y_prompt, y_sample)
```

```python
import contextlib
import math
import numpy as np
import concourse.bass as bass
import concourse.mybir as mybir
from concourse.bass_utils import run_bass_kernel_spmd

F32 = mybir.dt.float32
BF16 = mybir.dt.bfloat16
I32 = mybir.dt.int32
AF = mybir.ActivationFunctionType
ALU = mybir.AluOpType

ENGS = ["tensor", "vector", "scalar", "gpsimd", "sync"]
D = 1024
NMEM = 256
TT = 512
NEG = -30000.0


class Buf:
    def __init__(self, t, name):
        self.t = t
        self.n = name

    def __getitem__(self, k):
        return self.t[k]


class APBuf:
    def __init__(self, ap, name):
        self.ap = ap
        self.n = name

    def __getitem__(self, k):
        return self.ap[k]


def _names(xs):
    out = []
    for x in xs:
        if x is None:
            continue
        out.append(x if isinstance(x, str) else x.n)
    return out


class Prog:
    def __init__(self, nc, same_engine_sync=False, n_dma_sems=8):
        self.nc = nc
        self.ops = {e: [] for e in ENGS}
        self.cnt = {e: 0 for e in ENGS}
        self.dma_i = {e: 0 for e in ENGS}
        self.n_dma_sems = n_dma_sems
        self.synced = {}
        self.last_w = {}
        self.reads = {}
        self.same_engine_sync = same_engine_sync
        self.sems = {}
        self.ctx = []
        self.latest = {}
        self.n_ops = 0
        self.bigset = {e: set() for e in ENGS}

    def sem(self, key):
        if key not in self.sems:
            cm = self.nc.semaphore("s_" + "_".join(str(k) for k in key))
            self.sems[key] = cm.__enter__()
            self.ctx.append(cm)
        return self.sems[key]

    def _deps(self, eng, reads, writes):
        need = {}

        def add(k, v):
            if v > need.get(k, 0):
                need[k] = v
        for r in list(reads) + list(writes):
            lw = self.last_w.get(r)
            if lw is not None:
                add(*lw)
        for w in writes:
            for k, v in self.reads.get(w, {}).items():
                add(k, v)
        out = []
        for k, v in need.items():
            if k == ("c", eng) and (eng == "tensor" or not self.same_engine_sync or v in self.bigset[eng]):
                continue
            if self.synced.get((eng, k), 0) >= v:
                continue
            self.synced[(eng, k)] = v
            out.append((k, v))
        return out

    def _record(self, key, val, reads, writes):
        self.latest[key] = val
        for r in reads:
            self.reads.setdefault(r, {})[key] = val
        for w in writes:
            self.last_w[w] = (key, val)
            self.reads[w] = {}

    def op(self, eng, fn, reads=(), writes=(), big=False):
        reads, writes = _names(reads), _names(writes)
        waits = self._deps(eng, reads, writes)
        self.cnt[eng] += 1
        if big:
            self.bigset[eng].add(self.cnt[eng])
        key = ("c", eng)
        self.sem(key)
        e = getattr(self.nc, eng)
        for k, v in waits:
            e.wait_ge(self.sem(k), v)
        fn(e).then_inc(self.sems[key], 1)
        self._record(key, self.cnt[eng], reads, writes)
        self.n_ops += 1

    def dma(self, eng, fn, reads=(), writes=()):
        reads, writes = _names(reads), _names(writes)
        waits = self._deps(eng, reads, writes)
        i = self.dma_i[eng]
        self.dma_i[eng] += 1
        key = ("d", eng, i % self.n_dma_sems)
        val = 16 * (i // self.n_dma_sems + 1)
        self.sem(key)
        e = getattr(self.nc, eng)
        for k, v in waits:
            e.wait_ge(self.sem(k), v)
        fn(e).then_inc(self.sems[key], 16)
        self._record(key, val, reads, writes)
        self.n_ops += 1

    def barrier(self, engines=("tensor", "vector", "scalar", "gpsimd", "sync")):
        snap = dict(self.latest)
        for e in engines:
            waits = []
            for k, v in snap.items():
                if k == ("c", e):
                    continue
                if self.synced.get((e, k), 0) >= v:
                    continue
                self.synced[(e, k)] = v
                waits.append((k, v))
            eh = getattr(self.nc, e)
            for k, v in waits:
                eh.wait_ge(self.sem(k), v)

    def emit(self):
        for cm in reversed(self.ctx):
            cm.__exit__(None, None, None)


class Cfg:
    def __init__(self, depth=4, seq_lens=(2048, 2048, 16384), stop_after=None, same_engine_sync=True, debug=None):
        self.debug = dict(debug or {})
        self.depth = depth
        self.seq_lens = tuple(seq_lens)
        self.offs = [int(sum(seq_lens[:i])) for i in range(len(seq_lens))]
        self.Ltot = int(sum(seq_lens))
        self.nseq = len(seq_lens)
        self.stop_after = stop_after
        self.same_engine_sync = same_engine_sync
        for L in seq_lens:
            assert L % TT == 0


W_SPECS = [
    ("w_in", 1024, 5664), ("s5_w_glu", 512, 512), ("w_branch_a", 512, 1024), ("w_branch_b", 1024, 1024),
    ("w_out", 1024, 1024), ("w_q", 1024, 1024), ("w_k", 1024, 1024), ("w_v", 1024, 1024), ("w_o", 1024, 1024),
    ("w_up", 1024, 4096), ("w_down", 4096, 1024)]

C_U, C_Z, C_X, C_DT, C_G = 0, 512, 1536, 3584, 3616


def host_consts():
    c = {}
    r = np.arange(128)
    c["c_ident"] = np.eye(128, dtype=np.float32)
    tri = np.zeros((4, 128, 128), np.float32)
    tri[0] = (r[:, None] <= r[None, :])
    tri[1] = (r[:, None] > r[None, :])
    tri[2] = (r[:, None] >= r[None, :])
    tri[3] = (r[:, None] < r[None, :])
    c["c_tri"] = np.ascontiguousarray(tri.transpose(1, 0, 2))
    mk = np.zeros((2, 128, 4, 128), np.float32)
    mk[0] = np.where(r[:, None] > r[None, :], NEG, 0.0)[:, None, :]
    mk[1] = np.where(r[:, None] < r[None, :], NEG, 0.0)[:, None, :]
    c["c_mask"] = np.ascontiguousarray(mk.transpose(1, 0, 2, 3)).reshape(128, 2, 512)
    sel = np.zeros((16, 16, 128), np.float32)
    for h in range(16):
        sel[h, h, :] = 1.0
    c["c_sel"] = sel
    sel2 = np.zeros((64, 16), np.float32)
    for h in range(16):
        sel2[h, h] = 1.0
        sel2[32 + h, h] = 1.0
    c["c_sel2"] = sel2
    nsel = np.zeros((64, 16, 128), np.float32)
    for h in range(16):
        nsel[h, h, :] = -1.0
        nsel[32 + h, h, :] = -1.0
    c["c_nsel"] = nsel.reshape(64, 2048)
    s_idx = np.tile(np.arange(8), 16)
    m5 = np.zeros((128, 2, 128), np.float32)
    m5[:, 0, :] = (s_idx[None, :] >= s_idx[:, None])
    m5[:, 1, :] = (s_idx[None, :] <= s_idx[:, None])
    c["c_m5"] = m5
    kv = np.zeros((2, 4, 8), np.float32)
    s = np.arange(8, dtype=np.float32)
    kv[0, 0] = -s; kv[0, 1] = s; kv[0, 2] = 7 - s; kv[0, 3] = s + 1
    kv[1, 0] = s; kv[1, 1] = -s; kv[1, 2] = s; kv[1, 3] = 8 - s
    c["c_kv"] = np.broadcast_to(kv.reshape(1, 64), (128, 64)).copy()
    jv = np.arange(65, dtype=np.float32)
    c["c_jv"] = np.broadcast_to(jv.reshape(1, 65), (128, 65)).copy()
    return c


def build(cfg):
    nc = bass.Bass("TRN2", target_bir_lowering=False)
    P = Prog(nc, same_engine_sync=cfg.same_engine_sync)
    DEPTH, Ltot, nseq = cfg.depth, cfg.Ltot, cfg.nseq
    NT_TILES = Ltot // TT

    def dram(name, shape, dt, kind="Internal"):
        if kind == "Internal" and name in cfg.debug.get("dump", ()):
            kind = "ExternalOutput"
        return nc.dram_tensor(name, list(shape), dt, kind=kind).ap()

    xT_in = dram("xT", [D, Ltot], F32, "ExternalInput")
    memT_in = dram("memT", [D, nseq * NMEM], F32, "ExternalInput")
    yT_out = dram("yT", [D, Ltot], F32, "ExternalOutput")
    Wd = {n: dram(n, [DEPTH, K, N], F32, "ExternalInput") for n, K, N in W_SPECS}
    vecs = dram("vecs", [DEPTH, 128, 64], F32, "ExternalInput")
    nfin = dram("nfin", [128, 8], F32, "ExternalInput")
    convw = dram("convw", [DEPTH, 128, 16 * 6], F32, "ExternalInput")
    hpar = dram("hpar", [DEPTH, 16, 4], F32, "ExternalInput")
    s5lam = dram("s5lam", [DEPTH, 2, 128, 48], F32, "ExternalInput")
    s5b = dram("s5b", [DEPTH, 2, 128, 2 * 16 * 16], F32, "ExternalInput")
    s5c = dram("s5c", [DEPTH, 2, 128, 2 * 16 * 16], F32, "ExternalInput")
    cst = {k: dram(k, v.shape, F32, "ExternalInput") for k, v in host_consts().items()}

    XT = dram("XT", [D, Ltot], F32)
    PT = dram("PT", [5120, Ltot], BF16)
    PC = dram("PC", [2048, Ltot], BF16)
    US = dram("US", [4096, Ltot // 8], BF16)
    YS = dram("YS", [4096, Ltot // 8], BF16)
    DTs = dram("DTs", [32, Ltot], F32)
    YF = dram("YF", [D, Ltot], F32)
    slab_ids = {}
    n_slabs = 0
    for l in range(DEPTH):
        for n, K, N in W_SPECS:
            if n == "w_in":
                blocks = [("u", C_U, 512), ("z0", C_Z, 512), ("z1", C_Z + 512, 512)]
                blocks += [("x%d" % i, C_X + 512 * i, 512) for i in range(4)]
                blocks += [("g%d" % i, C_G + 512 * i, 512) for i in range(4)]
                blocks += [("dt", C_DT, 32)]
                for bn, c0, cw in blocks:
                    slab_ids[(l, n, bn)] = (n_slabs, 8, c0, cw); n_slabs += 1
            elif n == "w_down":
                for i in range(8):
                    slab_ids[(l, n, i)] = (n_slabs, 32, 128 * i, 128); n_slabs += 1
            else:
                for i in range(N // 512):
                    slab_ids[(l, n, i)] = (n_slabs, K // 128, 512 * i, 512); n_slabs += 1
    WS = dram("WS", [n_slabs, 128, 4096], BF16)

    with contextlib.ExitStack() as glob:
        uniq = [0]

        def sbuf(st, name, shape, dt):
            uniq[0] += 1
            nm = "%s_%d" % (name, uniq[0])
            return Buf(st.enter_context(nc.sbuf_tensor(nm, list(shape), dt)), nm)

        psb = [Buf(glob.enter_context(nc.psum_tensor("ps%d" % i, [128, 512], F32)), "ps%d" % i) for i in range(8)]
        ps_i = [0]

        def psum():
            b = psb[ps_i[0] % 8]
            ps_i[0] += 1
            return b

        def dump_sb(name, buf, ap2d, ncols, dt):
            if name not in cfg.debug.get("dumpsb", ()):
                return
            if name in dumped:
                return
            dumped.add(name)
            dd = nc.dram_tensor("dbg_" + name, [ap2d.shape[0], ncols], dt, kind="ExternalOutput").ap()
            P.dma("gpsimd", lambda e: e.dma_start(out=dd, in_=ap2d), reads=[buf], writes=["dbg_" + name])
        dumped = set()

        ident_f = sbuf(glob, "ident_f", [128, 128], F32)
        ident_b = sbuf(glob, "ident_b", [128, 128], BF16)
        onesM = sbuf(glob, "onesM", [128, 128], BF16)
        ones1 = sbuf(glob, "ones1", [128, 128], BF16)
        onesF = sbuf(glob, "onesF", [128, 128], F32)
        tri = sbuf(glob, "tri", [128, 4, 128], F32)
        maskneg = sbuf(glob, "maskneg", [128, 2, 512], BF16)
        m5 = sbuf(glob, "m5", [128, 2, 128], F32)
        kvc = sbuf(glob, "kvc", [128, 64], F32)
        jvc = sbuf(glob, "jvc", [128, 65], F32)
        vec_sb = sbuf(glob, "vec_sb", [128, DEPTH, 64], F32)
        nfin_sb = sbuf(glob, "nfin_sb", [128, 8], F32)
        convw_sb = sbuf(glob, "convw_sb", [128, DEPTH, 96], F32)
        hpar_sb = sbuf(glob, "hpar_sb", [16, DEPTH, 4], F32)
        hder = sbuf(glob, "hder", [16, DEPTH, 4], F32)

        def ld(dst, src, eng="sync"):
            P.dma(eng, lambda e: e.dma_start(out=dst[:], in_=src), writes=[dst])

        with contextlib.ExitStack() as st0:
            tmpf = sbuf(st0, "tmpf", [128, 1024], F32)
            ld(ident_f, cst["c_ident"])
            ld(tri, cst["c_tri"])
            ld(m5, cst["c_m5"])
            ld(kvc, cst["c_kv"])
            ld(jvc, cst["c_jv"])
            ld(nfin_sb, nfin)
            P.dma("sync", lambda e: e.dma_start(out=vec_sb[:], in_=vecs.rearrange("l p c -> p l c")), writes=[vec_sb])
            P.dma("sync", lambda e: e.dma_start(out=convw_sb[:], in_=convw.rearrange("l p c -> p l c")), writes=[convw_sb])
            P.dma("sync", lambda e: e.dma_start(out=hpar_sb[:], in_=hpar.rearrange("l p c -> p l c")), writes=[hpar_sb])
            P.dma("sync", lambda e: e.dma_start(out=tmpf[:], in_=cst["c_mask"].rearrange("p a b -> p (a b)")), writes=[tmpf])
            P.op("vector", lambda e: e.tensor_copy(out=maskneg[:].rearrange("p a b -> p (a b)"), in_=tmpf[:]), [tmpf], [maskneg])
            P.op("vector", lambda e: e.tensor_copy(out=ident_b[:], in_=ident_f[:]), [ident_f], [ident_b])
            P.op("gpsimd", lambda e: e.memset(onesM[:], 1.0 / 1024.0), [], [onesM])
            P.op("gpsimd", lambda e: e.memset(ones1[:], 1.0), [], [ones1])
            P.op("gpsimd", lambda e: e.memset(onesF[:], 1.0), [], [onesF])
            P.op("scalar", lambda e: e.activation(out=hder[:], in_=hpar_sb[:], func=AF.Exp), [hpar_sb], [hder])
            P.op("vector", lambda e: e.tensor_scalar(out=hder[:], in0=hder[:], scalar1=-1.0, scalar2=None, op0=ALU.mult), [hder], [hder])
            P.barrier()

        with contextlib.ExitStack() as st1:
            wf = [sbuf(st1, "wf%d" % i, [128, 4096], F32) for i in range(2)]
            wb = [sbuf(st1, "wb%d" % i, [128, 4096], BF16) for i in range(3)]
            k = 0
            for (l, n, bn), (sid, nk, c0, cw) in slab_ids.items():
                f, b = wf[k % 2], wb[k % 3]
                src = Wd[n][l, :, c0:c0 + cw].rearrange("(kt p) c -> p kt c", p=128)
                ne = nk * cw
                P.dma("sync", lambda e, f=f, src=src, nk=nk, ne=ne: e.dma_start(
                    out=f[:, 0:ne].rearrange("p (kt c) -> p kt c", kt=nk), in_=src), writes=[f])
                ceng = ["vector", "scalar", "gpsimd"][k % 3]
                if ceng == "scalar":
                    P.op("scalar", lambda e, f=f, b=b, ne=ne: e.copy(out=b[:, 0:ne], in_=f[:, 0:ne]), [f], [b])
                else:
                    P.op(ceng, lambda e, f=f, b=b, ne=ne: e.tensor_copy(out=b[:, 0:ne], in_=f[:, 0:ne]), [f], [b])
                P.dma("gpsimd", lambda e, b=b, sid=sid, ne=ne: e.dma_start(out=WS[sid, :, 0:ne], in_=b[:, 0:ne]),
                      reads=[b], writes=["WS%d" % sid])
                k += 1
            P.barrier()

        NRING = 3
        wring = [sbuf(glob, "wr%d" % i, [128, 4096], BF16) for i in range(NRING)]
        wr_i = [0]

        def slab(key):
            sid, nk, c0, cw = slab_ids[key]
            b = wring[wr_i[0] % NRING]
            wr_i[0] += 1
            ne = nk * cw
            P.dma("sync", lambda e: e.dma_start(out=b[:, 0:ne], in_=WS[sid, :, 0:ne]), reads=["WS%d" % sid], writes=[b])
            return b, nk, cw

        def slab_view(b, nk, cw):
            return b[:, 0:nk * cw].rearrange("p (kt c) -> p kt c", kt=nk)

        def OPB(eng, fn, r=(), w=()):
            P.op(eng, fn, r, w, big=True)

        ev_i = [0]

        def evac_eng():
            ev_i[0] += 1
            return "scalar" if ev_i[0] % 2 else "vector"

        def copy_op(eng, out_ap, in_ap, reads, writes, big=True):
            if eng == "scalar":
                P.op("scalar", lambda e: e.copy(out=out_ap, in_=in_ap), reads, writes, big=big)
            else:
                P.op(eng, lambda e: e.tensor_copy(out=out_ap, in_=in_ap), reads, writes, big=big)

        def sn(buf, k):
            return "%s:%d" % (buf.n, k)

        def subs(buf, n):
            return [sn(buf, k) for k in range(n)]

        def linear_fm(act, nk, keys, ntok, evac, sub=False):
            m = 0
            for key in keys:
                b, snk, cw = slab(key)
                sv = slab_view(b, snk, cw)
                for mi in range(cw // 128):
                    ps = psum()
                    for kt in range(nk):
                        P.op("tensor", lambda e, ps=ps, sv=sv, kt=kt, mi=mi: e.matmul(
                            ps[:, 0:ntok], lhsT=sv[:, kt, mi * 128:(mi + 1) * 128], rhs=act[:, kt, 0:ntok],
                            start=(kt == 0), stop=(kt == nk - 1)), [b, sn(act, kt) if sub else act], [ps])
                    evac(m, ps)
                    m += 1

        def rmsnorm_fm(st_bufs, x, nk, ntok, gain_ap_fn, out, sub=False):
            sq, rs = st_bufs
            if not sub:
                OPB("scalar", lambda e: e.activation(out=sq[:, 0:nk, 0:ntok], in_=x[:, 0:nk, 0:ntok], func=AF.Square), [x], [sq])
            else:
                for kt in range(nk):
                    if False:
                        OPB("gpsimd", lambda e, kt=kt: e.tensor_tensor(out=sq[:, kt, 0:ntok], in0=x[:, kt, 0:ntok], in1=x[:, kt, 0:ntok], op=ALU.mult), [sn(x, kt)], [sn(sq, kt)])
                    else:
                        OPB("scalar", lambda e, kt=kt: e.activation(out=sq[:, kt, 0:ntok], in_=x[:, kt, 0:ntok], func=AF.Square), [sn(x, kt)], [sn(sq, kt)])
            ps = psum()
            for kt in range(nk):
                P.op("tensor", lambda e, kt=kt: e.matmul(ps[:, 0:ntok], lhsT=onesM[:], rhs=sq[:, kt, 0:ntok],
                                                         start=(kt == 0), stop=(kt == nk - 1)), [onesM, sn(sq, kt) if sub else sq], [ps])
            OPB("scalar", lambda e: e.activation(out=rs[:, 0:ntok], in_=ps[:, 0:ntok], func=AF.Sqrt, bias=1e-6, scale=1.0), [ps], [rs])
            if not USE_DIV:
                OPB("vector", lambda e: e.reciprocal(out=rs[:, 0:ntok], in_=rs[:, 0:ntok]), [rs], [rs])
            for kt in range(nk):
                OPB("vector", lambda e, kt=kt: e.scalar_tensor_tensor(
                    out=out[:, kt, 0:ntok], in0=x[:, kt, 0:ntok], scalar=gain_ap_fn(kt), in1=rs[:, 0:ntok],
                    op0=ALU.mult, op1=(ALU.divide if USE_DIV else ALU.mult)), [sn(x, kt) if sub else x, rs], [sn(out, kt) if sub else out])

        USE_DIV = bool(cfg.debug.get("use_div", 0))

        def vcol(l, c):
            return vec_sb[:, l, c:c + 1]

        dbg = cfg.debug if hasattr(cfg, "debug") else {}

        def xsrc(l):
            return xT_in if l == 0 else XT

        def tile_list():
            out = []
            for s in range(nseq):
                for t in range(cfg.seq_lens[s] // TT):
                    out.append((s, t, cfg.offs[s] + t * TT))
            return out

        def phase_TL(l):
            with contextlib.ExitStack() as st:
                Xs = [sbuf(st, "X%d" % i, [128, 8, TT], F32) for i in range(2)]
                Xc = [Xs[0]]
                NTb = sbuf(st, "NTb", [128, 8, TT], BF16)
                SQ = sbuf(st, "SQ", [128, 8, TT], BF16)
                RS = sbuf(st, "RS", [128, TT], F32)
                STG = [sbuf(st, "STG%d" % i, [128, 4, TT], BF16) for i in range(2)]
                DTG = sbuf(st, "DTG", [32, TT], F32)
                if l > 0:
                    QT = sbuf(st, "QT", [128, 8, TT], BF16)
                    OT = sbuf(st, "OT", [128, 8, TT], BF16)
                    HUP = sbuf(st, "HUP", [128, 32, TT], BF16)
                    ET = [sbuf(st, "ET%d" % i, [128, 2, TT], BF16) for i in range(2)]
                    RD = [sbuf(st, "RD%d" % i, [128, TT], F32) for i in range(2)]
                    RL = [sbuf(st, "RL%d" % i, [128, TT], BF16) for i in range(2)]
                    KT = sbuf(st, "KT", [128, 8, NMEM], BF16)
                    VT = sbuf(st, "VT", [128, 2, D], BF16)
                    MEMX = sbuf(st, "MEMX", [128, 8, NMEM], F32)
                    MN = sbuf(st, "MN", [128, 8, NMEM], BF16)
                    SQm = sbuf(st, "SQm", [128, 8, NMEM], BF16)
                if l == DEPTH:
                    YO = sbuf(st, "YO", [128, 8, TT], F32)
                stg_i = [0]

                def kv_for_seq(ll, s):
                    P.dma("sync", lambda e: e.dma_start(
                        out=MEMX[:], in_=memT_in[:, s * NMEM:(s + 1) * NMEM].rearrange("(kt p) m -> p kt m", p=128)), writes=[MEMX])
                    rmsnorm_fm((SQm, RS), MEMX, 8, NMEM, lambda kt: vcol(ll, 16 + kt), MN)

                    def ev_k(m, ps):
                        copy_op(evac_eng(), KT[:, m, :], ps[:, 0:NMEM], [ps], [KT])
                    linear_fm(MN, 8, [(ll, "w_k", 0), (ll, "w_k", 1)], NMEM, ev_k)
                    for i in range(2):
                        b, snk, cw = slab((ll, "w_v", i))
                        sv = slab_view(b, snk, cw)
                        for mt in range(2):
                            ps = psum()
                            for kt in range(8):
                                P.op("tensor", lambda e, ps=ps, sv=sv, kt=kt, mt=mt: e.matmul(
                                    ps[:, 0:512], lhsT=MN[:, kt, mt * 128:(mt + 1) * 128], rhs=sv[:, kt, :],
                                    start=(kt == 0), stop=(kt == 7)), [b, MN], [ps])
                            copy_op(evac_eng(), VT[:, mt, i * 512:(i + 1) * 512], ps[:, 0:512], [ps], [VT])

                def add_to_X(m, ps):
                    OPB("vector", lambda e: e.tensor_tensor(out=Xc[0][:, m, :], in0=Xc[0][:, m, :], in1=ps[:, 0:TT], op=ALU.add), [sn(Xc[0], m), ps], [sn(Xc[0], m)])

                def xattn(ll):
                    rmsnorm_fm((SQ, RS), Xc[0], 8, TT, lambda kt: vcol(ll, 8 + kt), NTb, sub=True)

                    def ev_q(m, ps):
                        copy_op(evac_eng(), QT[:, m, :], ps[:, 0:TT], [ps], [sn(QT, m)])
                    linear_fm(NTb, 8, [(ll, "w_q", 0), (ll, "w_q", 1)], TT, ev_q, sub=True)
                    for hd in range(4):
                        E = ET[hd % 2]
                        Rd = RD[hd % 2]
                        for mt in range(2):
                            ps = psum()
                            for dk in range(2):
                                P.op("tensor", lambda e, ps=ps, dk=dk, mt=mt, hd=hd: e.matmul(
                                    ps[:, 0:TT], lhsT=KT[:, 2 * hd + dk, mt * 128:(mt + 1) * 128], rhs=QT[:, 2 * hd + dk, :],
                                    start=(dk == 0), stop=(dk == 1)), [KT, sn(QT, 2 * hd + dk)], [ps])
                            OPB("scalar", lambda e, ps=ps, mt=mt, E=E: e.activation(
                                out=E[:, mt, :], in_=ps[:, 0:TT], func=AF.Exp, scale=1.0 / 16.0), [ps], [E])
                        psd = psum()
                        for mt in range(2):
                            P.op("tensor", lambda e, psd=psd, mt=mt, E=E: e.matmul(
                                psd[:, 0:TT], lhsT=ones1[:], rhs=E[:, mt, :], start=(mt == 0), stop=(mt == 1)), [ones1, E], [psd])
                        OPB("vector", lambda e, psd=psd, Rd=Rd: e.reciprocal(out=Rd[:], in_=psd[:, 0:TT]), [psd], [Rd])
                        for dk in range(2):
                            pso = psum()
                            for mt in range(2):
                                P.op("tensor", lambda e, pso=pso, mt=mt, dk=dk, hd=hd, E=E: e.matmul(
                                    pso[:, 0:TT], lhsT=VT[:, mt, (2 * hd + dk) * 128:(2 * hd + dk + 1) * 128], rhs=E[:, mt, :],
                                    start=(mt == 0), stop=(mt == 1)), [VT, E], [pso])
                            OPB("vector", lambda e, pso=pso, dk=dk, hd=hd, Rd=Rd: e.tensor_tensor(
                                out=OT[:, 2 * hd + dk, :], in0=pso[:, 0:TT], in1=Rd[:], op=ALU.mult), [pso, Rd], [sn(OT, 2 * hd + dk)])
                    linear_fm(OT, 8, [(ll, "w_o", 0), (ll, "w_o", 1)], TT, add_to_X, sub=True)

                def mlp(ll):
                    rmsnorm_fm((SQ, RS), Xc[0], 8, TT, lambda kt: vcol(ll, 24 + kt), NTb, sub=True)
                    rl_i = [0]

                    def ev_up(m, ps):
                        r = RL[rl_i[0] % 2]
                        rl_i[0] += 1
                        OPB("scalar", lambda e: e.activation(out=r[:], in_=ps[:, 0:TT], func=AF.Relu), [ps], [r])
                        OPB("gpsimd", lambda e: e.tensor_tensor(out=HUP[:, m, :], in0=r[:], in1=r[:], op=ALU.mult), [r], [sn(HUP, m)])
                    linear_fm(NTb, 8, [(ll, "w_up", i) for i in range(8)], TT, ev_up, sub=True)
                    for i in range(8):
                        b, snk, cw = slab((ll, "w_down", i))
                        sv = slab_view(b, snk, cw)
                        ps = psum()
                        for kt in range(32):
                            P.op("tensor", lambda e, ps=ps, sv=sv, kt=kt: e.matmul(
                                ps[:, 0:TT], lhsT=sv[:, kt, :], rhs=HUP[:, kt, :], start=(kt == 0), stop=(kt == 31)), [b, sn(HUP, kt)], [ps])
                        add_to_X(i, ps)

                def front(ll, t0):
                    rmsnorm_fm((SQ, RS), Xc[0], 8, TT, lambda kt: vcol(ll, kt), NTb, sub=True)
                    j0 = t0 // 8
                    sg = STG[stg_i[0] % 2]
                    stg_i[0] += 1

                    def ev_u(m, ps):
                        eng = evac_eng()
                        copy_op(eng, sg[:, m, :].rearrange("p (s j) -> p s j", s=8),
                                ps[:, 0:TT].rearrange("p (j s) -> p s j", s=8), [ps], [sg])
                    linear_fm(NTb, 8, [(ll, "w_in", "u")], TT, ev_u, sub=True)
                    for m in range(4):
                        P.dma("gpsimd", lambda e, sg=sg, m=m: e.dma_start(
                            out=US.rearrange("(m p s) j -> m p s j", p=128, s=8)[m][:, :, j0:j0 + TT // 8],
                            in_=sg[:, m, :].rearrange("p (s j) -> p s j", s=8)), reads=[sg], writes=["US"])
                    blocks = [("z0", 0), ("z1", 512)] + [("x%d" % i, 1024 + 512 * i) for i in range(4)] + \
                             [("g%d" % i, 3072 + 512 * i) for i in range(4)]
                    for bn, row0 in blocks:
                        sg = STG[stg_i[0] % 2]
                        stg_i[0] += 1

                        def ev(m, ps, sg=sg):
                            copy_op(evac_eng(), sg[:, m, :], ps[:, 0:TT], [ps], [sg])
                        linear_fm(NTb, 8, [(ll, "w_in", bn)], TT, ev, sub=True)
                        P.dma("gpsimd", lambda e, sg=sg, row0=row0: e.dma_start(
                            out=PT[row0:row0 + 512, t0:t0 + TT].rearrange("(m p) t -> p m t", p=128), in_=sg[:]),
                            reads=[sg], writes=["PT"])
                    b, snk, cw = slab((ll, "w_in", "dt"))
                    sv = slab_view(b, snk, cw)
                    ps = psum()
                    for kt in range(8):
                        P.op("tensor", lambda e, ps=ps, sv=sv, kt=kt: e.matmul(
                            ps[0:32, 0:TT], lhsT=sv[:, kt, :], rhs=NTb[:, kt, :], start=(kt == 0), stop=(kt == 7)), [b, sn(NTb, kt)], [ps])
                    OPB("vector", lambda e, ps=ps: e.tensor_copy(out=DTG[:], in_=ps[0:32, 0:TT]), [ps], [DTG])
                    P.dma("gpsimd", lambda e: e.dma_start(out=DTs[:, t0:t0 + TT], in_=DTG[:]), reads=[DTG], writes=["DTs"])

                cur_seq = -1
                for ti_, (s, t, t0) in enumerate(tile_list()):
                    Xc[0] = Xs[ti_ % 2]
                    X = Xc[0]
                    if l > 0 and s != cur_seq:
                        kv_for_seq(l - 1, s)
                        cur_seq = s
                    src = xsrc(l)
                    P.dma("sync", lambda e, src=src, t0=t0: e.dma_start(
                        out=X[:], in_=src[:, t0:t0 + TT].rearrange("(kt p) t -> p kt t", p=128)), reads=["XT"], writes=subs(X, 8))
                    if l > 0:
                        if not dbg.get("no_xattn"):
                            xattn(l - 1)
                        if not dbg.get("no_mlp"):
                            mlp(l - 1)
                    if l < DEPTH:
                        front(l, t0)
                        if l > 0:
                            P.dma("gpsimd", lambda e, t0=t0: e.dma_start(
                                out=XT[:, t0:t0 + TT].rearrange("(kt p) t -> p kt t", p=128), in_=X[:]), reads=subs(X, 8), writes=["XT"])
                    else:
                        OPB("scalar", lambda e: e.activation(out=SQ[:], in_=X[:], func=AF.Square), subs(X, 8), subs(SQ, 8))
                        ps = psum()
                        for kt in range(8):
                            P.op("tensor", lambda e, ps=ps, kt=kt: e.matmul(ps[:, 0:TT], lhsT=onesM[:], rhs=SQ[:, kt, :],
                                                                     start=(kt == 0), stop=(kt == 7)), [onesM, sn(SQ, kt)], [ps])
                        OPB("scalar", lambda e, ps=ps: e.activation(out=RS[:], in_=ps[:, 0:TT], func=AF.Sqrt, bias=1e-6, scale=1.0), [ps], [RS])
                        OPB("vector", lambda e: e.reciprocal(out=RS[:], in_=RS[:]), [RS], [RS])
                        for kt in range(8):
                            OPB("vector", lambda e, kt=kt: e.scalar_tensor_tensor(
                                out=YO[:, kt, :], in0=X[:, kt, :], scalar=nfin_sb[:, kt:kt + 1], in1=RS[:],
                                op0=ALU.mult, op1=ALU.mult), [sn(X, kt), RS, nfin_sb], [YO])
                        P.dma("gpsimd", lambda e, t0=t0: e.dma_start(
                            out=yT_out[:, t0:t0 + TT].rearrange("(kt p) t -> p kt t", p=128), in_=YO[:]), reads=[YO], writes=["yT"])
                P.barrier()
        TWO_PI = 2.0 * math.pi

        def phase_S5(l):
            NB = Ltot // TT
            with contextlib.ExitStack() as st:
                WI = sbuf(st, "WI", [128, 32, 128], BF16)
                WSF = sbuf(st, "WSF", [128, 32, 4, 64], BF16)
                WO = sbuf(st, "WO", [128, 16, 4, 128], BF16)
                TC = sbuf(st, "TC", [128, 2, 16, 65], F32)
                TS = sbuf(st, "TS", [128, 2, 16, 65], F32)
                R8C = sbuf(st, "R8C", [128, 2, 16, 2, 65], F32)
                A64 = sbuf(st, "A64", [128, 2, 2, 16], F32)

                def sincos(st2, name, ph, shape, out_sin, out_cos, rw):
                    n = int(np.prod(shape))
                    t1 = sbuf(st2, name + "_t1", [128, n], F32)
                    ti = sbuf(st2, name + "_ti", [128, n], I32)
                    t2 = sbuf(st2, name + "_t2", [128, n], F32)

                    def v(b):
                        a = b[:, 0:n]
                        if len(shape) == 2:
                            return a.rearrange("p (a b) -> p a b", a=shape[0])
                        if len(shape) == 3:
                            return a.rearrange("p (a b c) -> p a b c", a=shape[0], b=shape[1])
                        return a
                    for off, outp in ((0.0, out_sin), (0.5 * math.pi, out_cos)):
                        P.op("vector", lambda e, off=off: e.tensor_scalar(out=v(t1), in0=ph, scalar1=off, scalar2=1.0 / TWO_PI,
                                                                          op0=ALU.add, op1=ALU.mult), rw, [t1])
                        P.op("vector", lambda e: e.tensor_copy(out=ti[:], in_=t1[:]), [t1], [ti])
                        P.op("vector", lambda e: e.tensor_copy(out=t2[:], in_=ti[:]), [ti], [t2])
                        P.op("vector", lambda e: e.tensor_tensor(out=t2[:], in0=t1[:], in1=t2[:], op=ALU.subtract), [t1, t2], [t2])
                        P.op("scalar", lambda e, outp=outp: e.activation(out=outp, in_=v(t2), func=AF.Sin, scale=TWO_PI), [t2], rw)

                with contextlib.ExitStack() as sg:
                    BMr = [sbuf(sg, "BMr%d" % d, [128, 16, 16, 8], BF16) for d in range(2)]
                    BMi = [sbuf(sg, "BMi%d" % d, [128, 16, 16, 8], BF16) for d in range(2)]
                    CMr = [sbuf(sg, "CMr%d" % d, [128, 16, 16, 8], BF16) for d in range(2)]
                    CMi = [sbuf(sg, "CMi%d" % d, [128, 16, 16, 8], BF16) for d in range(2)]
                    WSr1 = sbuf(sg, "WSr", [128, 16, 16, 8], BF16)
                    WSi1 = sbuf(sg, "WSi", [128, 16, 16, 8], BF16)
                    LAM = sbuf(sg, "LAM", [128, 48], F32)
                    BB = sbuf(sg, "BB", [128, 2, 16, 16], F32)
                    CC = sbuf(sg, "CC", [128, 2, 16, 16], F32)
                    SM = sbuf(sg, "SM", [128, 16, 16], F32)
                    PH = sbuf(sg, "PH", [128, 16, 32], F32)
                    MAG = sbuf(sg, "MAG", [128, 16, 32], F32)
                    SN = sbuf(sg, "SN", [128, 16, 32], F32)
                    CS = sbuf(sg, "CS", [128, 16, 32], F32)
                    ER = sbuf(sg, "ER", [128, 16, 4, 8], F32)
                    EI = sbuf(sg, "EI", [128, 16, 4, 8], F32)
                    BBR = sbuf(sg, "BBR", [128, 16, 16], F32)
                    BBI = sbuf(sg, "BBI", [128, 16, 16], F32)
                    T4 = [sbuf(sg, "T4_%d" % i, [128, 16, 16, 8], F32) for i in range(2)]
                    PH2 = sbuf(sg, "PH2", [128, 16, 65], F32)
                    all_gen = [LAM, BB, CC, SM, PH, MAG, SN, CS, ER, EI, BBR, BBI, PH2]

                    def VV(fn, r, w):
                        P.op("vector", fn, r, w)

                    for d in range(2):
                        P.dma("sync", lambda e, d=d: e.dma_start(out=LAM[:], in_=s5lam[l, d]), writes=[LAM])
                        P.dma("sync", lambda e, d=d: e.dma_start(out=BB[:].rearrange("p c g h -> p (c g h)"), in_=s5b[l, d]), writes=[BB])
                        P.dma("sync", lambda e, d=d: e.dma_start(out=CC[:].rearrange("p c g h -> p (c g h)"), in_=s5c[l, d]), writes=[CC])
                        lre, lim, ldt = LAM[:, 0:16], LAM[:, 16:32], LAM[:, 32:48]
                        P.op("scalar", lambda e: e.activation(out=SM[:, 0, :], in_=ldt, func=AF.Exp), [LAM], [SM])
                        VV(lambda e: e.tensor_tensor(out=SM[:, 1, :], in0=lre, in1=SM[:, 0, :], op=ALU.mult), [LAM, SM], [SM])
                        VV(lambda e: e.tensor_tensor(out=SM[:, 2, :], in0=lim, in1=SM[:, 0, :], op=ALU.mult), [LAM, SM], [SM])
                        kvd = kvc[:, d * 32:(d + 1) * 32]
                        VV(lambda e, kvd=kvd: e.tensor_tensor(out=PH[:], in1=SM[:, 2, :].unsqueeze(2).to_broadcast([128, 16, 32]),
                                                              in0=kvd.unsqueeze(1).to_broadcast([128, 16, 32]), op=ALU.mult), [SM, kvc], [PH])
                        VV(lambda e, kvd=kvd: e.tensor_tensor(out=MAG[:], in1=SM[:, 1, :].unsqueeze(2).to_broadcast([128, 16, 32]),
                                                              in0=kvd.unsqueeze(1).to_broadcast([128, 16, 32]), op=ALU.mult), [SM, kvc], [MAG])
                        P.op("scalar", lambda e: e.activation(out=MAG[:], in_=MAG[:], func=AF.Exp), [MAG], [MAG])
                        with contextlib.ExitStack() as s2:
                            sincos(s2, "sc1_%d" % d, PH[:], [16, 32], SN[:], CS[:], [PH, SN, CS])
                            P.barrier()
                        if True:
                            VV(lambda e: e.tensor_tensor(out=ER[:].rearrange("p g k s -> p g (k s)"), in0=MAG[:], in1=CS[:], op=ALU.mult), [MAG, CS], [ER])
                            VV(lambda e: e.tensor_tensor(out=EI[:].rearrange("p g k s -> p g (k s)"), in0=MAG[:], in1=SN[:], op=ALU.mult), [MAG, SN], [EI])
                            VV(lambda e: e.tensor_tensor(out=PH2[:], in1=SM[:, 2, :].unsqueeze(2).to_broadcast([128, 16, 65]),
                                                         in0=jvc[:].unsqueeze(1).to_broadcast([128, 16, 65]), op=ALU.mult), [SM, jvc], [PH2])
                            VV(lambda e: e.tensor_scalar(out=PH2[:], in0=PH2[:], scalar1=8.0, scalar2=None, op0=ALU.mult), [PH2], [PH2])
                            with contextlib.ExitStack() as s2:
                                sincos(s2, "sc2_%d" % d, PH2[:], [16, 65], TS[:, d], TC[:, d], [PH2, TS, TC])
                                P.barrier()
                            P.op("scalar", lambda e: e.activation(out=SM[:, 8, :], in_=SM[:, 1, :], func=AF.Exp, scale=8.0), [SM], [SM])
                            for c in range(2):
                                P.op("gpsimd", lambda e, c=c, d=d: e.memset(R8C[:, d, :, c, :], 1.0), [R8C], [R8C])
                                VV(lambda e, c=c, d=d: e.tensor_tensor(out=R8C[:, d, :, c, :], in0=R8C[:, d, :, c, :],
                                                                       in1=SM[:, 8, :].unsqueeze(2).to_broadcast([128, 16, 65]), op=ALU.mult), [SM, R8C], [R8C])
                                P.op("gpsimd", lambda e, c=c, d=d: e.memset(R8C[:, d, :, c, 0:1], 0.0), [R8C], [R8C])
                            P.op("scalar", lambda e: e.activation(out=SM[:, 9, :], in_=SM[:, 1, :], func=AF.Exp, scale=512.0), [SM], [SM])
                            VV(lambda e: e.tensor_scalar(out=SM[:, 10, :], in0=SM[:, 2, :], scalar1=512.0, scalar2=None, op0=ALU.mult), [SM], [SM])
                            with contextlib.ExitStack() as s2:
                                sincos(s2, "sc3_%d" % d, SM[:, 10, :], [16], SM[:, 11, :], SM[:, 12, :], [SM])
                                P.barrier()
                            VV(lambda e, d=d: e.tensor_tensor(out=A64[:, d, 0, :], in0=SM[:, 9, :], in1=SM[:, 12, :], op=ALU.mult), [SM], [A64])
                            VV(lambda e, d=d: e.tensor_tensor(out=A64[:, d, 1, :], in0=SM[:, 9, :], in1=SM[:, 11, :], op=ALU.mult), [SM], [A64])
                            P.barrier()
                        if d == 0:
                            dump_sb("LAM", LAM, LAM[:], 48, F32)
                            dump_sb("SM", SM, SM[:].rearrange("p a b -> p (a b)"), 256, F32)
                            dump_sb("PH", PH, PH[:].rearrange("p a b -> p (a b)"), 512, F32)
                            dump_sb("SN", SN, SN[:].rearrange("p a b -> p (a b)"), 512, F32)
                            dump_sb("MAG", MAG, MAG[:].rearrange("p a b -> p (a b)"), 512, F32)
                            dump_sb("ER", ER, ER[:].rearrange("p g k s -> p (g k s)"), 512, F32)
                        if d == 0:
                            e1r, e1i = ER[:, :, 3, 0], EI[:, :, 3, 0]
                        else:
                            e1r, e1i = ER[:, :, 2, 1], EI[:, :, 2, 1]
                        VV(lambda e: e.tensor_scalar(out=SM[:, 6, :], in0=e1r, scalar1=-1.0, scalar2=None, op0=ALU.add), [ER], [SM])
                        VV(lambda e: e.tensor_copy(out=SM[:, 7, :], in_=e1i), [EI], [SM])
                        VV(lambda e: e.tensor_tensor(out=SM[:, 3, :], in0=lre, in1=lre, op=ALU.mult), [LAM], [SM])
                        VV(lambda e: e.tensor_tensor(out=SM[:, 13, :], in0=lim, in1=lim, op=ALU.mult), [LAM], [SM])
                        VV(lambda e: e.tensor_tensor(out=SM[:, 3, :], in0=SM[:, 3, :], in1=SM[:, 13, :], op=ALU.add), [SM], [SM])
                        VV(lambda e: e.reciprocal(out=SM[:, 3, :], in_=SM[:, 3, :]), [SM], [SM])
                        VV(lambda e: e.tensor_tensor(out=SM[:, 4, :], in0=SM[:, 6, :], in1=lre, op=ALU.mult), [SM, LAM], [SM])
                        VV(lambda e: e.tensor_tensor(out=SM[:, 13, :], in0=SM[:, 7, :], in1=lim, op=ALU.mult), [SM, LAM], [SM])
                        VV(lambda e: e.tensor_tensor(out=SM[:, 4, :], in0=SM[:, 4, :], in1=SM[:, 13, :], op=ALU.add), [SM], [SM])
                        VV(lambda e: e.tensor_tensor(out=SM[:, 4, :], in0=SM[:, 4, :], in1=SM[:, 3, :], op=ALU.mult), [SM], [SM])
                        VV(lambda e: e.tensor_tensor(out=SM[:, 5, :], in0=SM[:, 7, :], in1=lre, op=ALU.mult), [SM, LAM], [SM])
                        VV(lambda e: e.tensor_tensor(out=SM[:, 13, :], in0=SM[:, 6, :], in1=lim, op=ALU.mult), [SM, LAM], [SM])
                        VV(lambda e: e.tensor_tensor(out=SM[:, 5, :], in0=SM[:, 5, :], in1=SM[:, 13, :], op=ALU.subtract), [SM], [SM])
                        VV(lambda e: e.tensor_tensor(out=SM[:, 5, :], in0=SM[:, 5, :], in1=SM[:, 3, :], op=ALU.mult), [SM], [SM])
                        frb = SM[:, 4, :].unsqueeze(2).to_broadcast([128, 16, 16])
                        fib = SM[:, 5, :].unsqueeze(2).to_broadcast([128, 16, 16])
                        t3 = T4[0][:, :, :, 0]
                        VV(lambda e: e.tensor_tensor(out=BBR[:], in0=BB[:, 0], in1=frb, op=ALU.mult), [BB, SM], [BBR])
                        VV(lambda e: e.tensor_tensor(out=t3, in0=BB[:, 1], in1=fib, op=ALU.mult), [BB, SM], [T4[0]])
                        VV(lambda e: e.tensor_tensor(out=BBR[:], in0=BBR[:], in1=t3, op=ALU.subtract), [BBR, T4[0]], [BBR])
                        VV(lambda e: e.tensor_tensor(out=BBI[:], in0=BB[:, 1], in1=frb, op=ALU.mult), [BB, SM], [BBI])
                        VV(lambda e: e.tensor_tensor(out=t3, in0=BB[:, 0], in1=fib, op=ALU.mult), [BB, SM], [T4[0]])
                        VV(lambda e: e.tensor_tensor(out=BBI[:], in0=BBI[:], in1=t3, op=ALU.add), [BBI, T4[0]], [BBI])

                        def cprod(kind, xr, xi, outr, outi, neg_imag):
                            er = ER[:, :, kind, :].unsqueeze(2).to_broadcast([128, 16, 16, 8])
                            ei = EI[:, :, kind, :].unsqueeze(2).to_broadcast([128, 16, 16, 8])
                            xrb = xr.unsqueeze(3).to_broadcast([128, 16, 16, 8])
                            xib = xi.unsqueeze(3).to_broadcast([128, 16, 16, 8])
                            rr = [ER, EI, BBR, BBI, CC, T4[0], T4[1]]
                            VV(lambda e: e.tensor_tensor(out=T4[0][:], in0=er, in1=xrb, op=ALU.mult), rr, [T4[0]])
                            VV(lambda e: e.tensor_tensor(out=T4[1][:], in0=ei, in1=xib, op=ALU.mult), rr, [T4[1]])
                            VV(lambda e: e.tensor_tensor(out=outr[:], in0=T4[0][:], in1=T4[1][:], op=ALU.subtract), rr, [outr])
                            VV(lambda e: e.tensor_tensor(out=T4[0][:], in0=er, in1=xib, op=ALU.mult), rr, [T4[0]])
                            VV(lambda e: e.tensor_tensor(out=T4[1][:], in0=ei, in1=xrb, op=ALU.mult), rr, [T4[1]])
                            if neg_imag:
                                VV(lambda e: e.scalar_tensor_tensor(out=outi[:], in0=T4[0][:], scalar=-1.0, in1=T4[1][:],
                                                                    op0=ALU.mult, op1=ALU.subtract), rr, [outi])
                            else:
                                VV(lambda e: e.tensor_tensor(out=outi[:], in0=T4[0][:], in1=T4[1][:], op=ALU.add), rr, [outi])
                        cprod(0, BBR[:], BBI[:], BMr[d], BMi[d], False)
                        cprod(1, CC[:, 0], CC[:, 1], CMr[d], CMi[d], True)
                        cprod(2, BBR[:], BBI[:], WSr1, WSi1, False)
                        for g0 in range(0, 32, 8):
                            ps = psum()
                            psv = ps[:].bitcast(BF16)
                            for gi in range(8):
                                g = g0 + gi
                                g_lo, gh = g // 16, g % 16
                                pr = slice(g_lo * 64, (g_lo + 1) * 64)
                                for c in range(2):
                                    src = WSr1 if c == 0 else WSi1
                                    col = (gi * 2 + c) * 64
                                    P.op("tensor", lambda e, psv=psv, src=src, pr=pr, gh=gh, col=col: e.transpose(
                                        psv[:, col:col + 64], src[pr, gh].rearrange("p h s -> p (h s)"), ident_b[pr, pr]),
                                        [src, ident_b], [ps])
                            copy_op(evac_eng(), WSF[:, g0:g0 + 8, 2 * d:2 * d + 2, :],
                                    psv[:, 0:1024].rearrange("p (g c q) -> p g c q", g=8, c=2), [ps], [WSF])
                        WOr = Buf(WO.t, "WO")
                        class _V:
                            def __init__(self, ap, n):
                                self.ap = ap; self.n = n
                            def __getitem__(self, k):
                                return self.ap
                        cprod(3, CC[:, 0], CC[:, 1],
                              _V(WO[:, :, 2 * d, :].rearrange("p g (h t) -> p g h t", t=8), "WO"),
                              _V(WO[:, :, 2 * d + 1, :].rearrange("p g (h t) -> p g h t", t=8), "WO"), True)
                        P.barrier()
                    TMPA = sbuf(sg, "TMPA", [128, 128], F32)
                    TMPB = sbuf(sg, "TMPB", [128, 128], F32)
                    for g in range(32):
                        g_lo, gh = g // 16, g % 16
                        pr = slice(g_lo * 64, (g_lo + 1) * 64)
                        pss = []
                        for d in range(2):
                            ps = psum()
                            P.op("tensor", lambda e, ps=ps, d=d, pr=pr, gh=gh: e.matmul(
                                ps[:, 0:128], lhsT=BMr[d][pr, gh].rearrange("p h s -> p (h s)"),
                                rhs=CMr[d][pr, gh].rearrange("p h s -> p (h s)"), start=True, stop=False), [BMr[d], CMr[d]], [ps])
                            P.op("tensor", lambda e, ps=ps, d=d, pr=pr, gh=gh: e.matmul(
                                ps[:, 0:128], lhsT=BMi[d][pr, gh].rearrange("p h s -> p (h s)"),
                                rhs=CMi[d][pr, gh].rearrange("p h s -> p (h s)"), start=False, stop=True), [BMi[d], CMi[d]], [ps])
                            pss.append(ps)
                        VV(lambda e, ps=pss[0]: e.tensor_tensor(out=TMPA[:], in0=ps[:, 0:128], in1=m5[:, 0, :], op=ALU.mult), [pss[0], m5], [TMPA])
                        VV(lambda e, ps=pss[1]: e.tensor_tensor(out=TMPB[:], in0=ps[:, 0:128], in1=m5[:, 1, :], op=ALU.mult), [pss[1], m5], [TMPB])
                        P.op("gpsimd", lambda e, g=g: e.tensor_tensor(out=WI[:, g, :], in0=TMPA[:], in1=TMPB[:], op=ALU.add), [TMPA, TMPB], [WI])
                    P.barrier()

                dump_sb("WI", WI, WI[:].rearrange("p g m -> p (g m)"), 32 * 128, BF16)
                dump_sb("WSF", WSF, WSF[:].rearrange("p g k q -> p (g k q)"), 32 * 256, BF16)
                dump_sb("WO", WO, WO[:].rearrange("p g k m -> p (g k m)"), 16 * 512, BF16)
                dump_sb("TC", TC, TC[:].rearrange("p d g k -> p (d g k)"), 2 * 16 * 65, F32)
                dump_sb("TS", TS, TS[:].rearrange("p d g k -> p (d g k)"), 2 * 16 * 65, F32)
                dump_sb("R8C", R8C, R8C[:].rearrange("p d g c k -> p (d g c k)"), 2 * 16 * 2 * 65, F32)
                dump_sb("A64", A64, A64[:].rearrange("p d c g -> p (d c g)"), 64, F32)
                with contextlib.ExitStack() as sb_:
                    Ub = [sbuf(sb_, "Ub%d" % i, [128, 32, 64], BF16) for i in range(2)]
                    SALs = [sbuf(sb_, "SAL%d" % i, [128, 16, 4, 64], F32) for i in range(2)]
                    GD = sbuf(sb_, "GD", [128, 16, 2, 65], F32)
                    GO = sbuf(sb_, "GO", [128, 16, 2, 65], F32)
                    M = [sbuf(sb_, "M%d" % i, [128, 16, 64], F32) for i in range(2)]
                    HBs = [sbuf(sb_, "HB%d" % i, [128, 16, 4, 64], BF16) for i in range(2)]
                    EB = sbuf(sb_, "EB", [128, NB, 2, 2, 16], F32)
                    YSTG = [sbuf(sb_, "YSTG%d" % i, [128, 8, 64], BF16) for i in range(2)]
                    CT = [sbuf(sb_, "CT%d" % i, [128, 16], F32) for i in range(3)]

                    def VV(fn, r, w):
                        P.op("vector", fn, r, w)

                    def GP(fn, r, w):
                        P.op("gpsimd", fn, r, w)

                    def VVB(fn, r, w):
                        P.op("vector", fn, r, w, big=True)

                    def GPB(fn, r, w):
                        P.op("gpsimd", fn, r, w, big=True)

                    seq_first = set(o // TT for o in cfg.offs)
                    seq_last = set((o + L) // TT - 1 for o, L in zip(cfg.offs, cfg.seq_lens))

                    def p1(b):
                        j0 = b * 64
                        U = Ub[b % 2]
                        SAL = SALs[b % 2]
                        P.dma("sync", lambda e: e.dma_start(out=U[:], in_=US.rearrange("(g r) j -> r g j", r=128)[:, :, j0:j0 + 64]),
                              reads=["US"], writes=[U])
                        for gh in range(16):
                            ps = psum()
                            for g_lo in range(2):
                                g = g_lo * 16 + gh
                                for kind in range(4):
                                    P.op("tensor", lambda e, ps=ps, g=g, g_lo=g_lo, kind=kind: e.matmul(
                                        ps[g_lo * 64:(g_lo + 1) * 64, kind * 64:(kind + 1) * 64], lhsT=WSF[:, g, kind, :], rhs=U[:, g, :],
                                        start=True, stop=True), [WSF, U], [ps])
                            copy_op("scalar", SAL[:, gh].rearrange("p k j -> p (k j)"), ps[:, 0:256], [ps], [SAL])

                    def p2(b, final):
                        SAL = SALs[b % 2]
                        HB = HBs[b % 2]
                        for d in range(2):
                            if d == 0:
                                Sr, Si = SAL[:, :, 0, :], SAL[:, :, 1, :]
                            else:
                                Sr, Si = SAL[:, :, 2, ::-1], SAL[:, :, 3, ::-1]
                            Tc1, Ts1 = TC[:, d, :, 1:65], TS[:, d, :, 1:65]
                            Tc0, Ts0 = TC[:, d, :, 0:64], TS[:, d, :, 0:64]
                            VVB(lambda e: e.tensor_tensor(out=M[0][:], in0=Sr, in1=Tc1, op=ALU.mult), [SAL, TC], [M[0]])
                            VVB(lambda e: e.tensor_tensor(out=M[1][:], in0=Si, in1=Ts1, op=ALU.mult), [SAL, TS], [M[1]])
                            VVB(lambda e: e.tensor_tensor(out=GD[:, :, 0, 1:65], in0=M[0][:], in1=M[1][:], op=ALU.add), [M[0], M[1]], [GD])
                            VVB(lambda e: e.tensor_tensor(out=M[0][:], in0=Si, in1=Tc1, op=ALU.mult), [SAL, TC], [M[0]])
                            VVB(lambda e: e.tensor_tensor(out=M[1][:], in0=Sr, in1=Ts1, op=ALU.mult), [SAL, TS], [M[1]])
                            VVB(lambda e: e.tensor_tensor(out=GD[:, :, 1, 1:65], in0=M[0][:], in1=M[1][:], op=ALU.subtract), [M[0], M[1]], [GD])
                            cidx = (b - 1) if d == 0 else (b + 1)
                            has_carry = final and ((d == 0 and b not in seq_first) or (d == 1 and b not in seq_last))
                            if has_carry:
                                for c in range(2):
                                    VV(lambda e, c=c, cidx=cidx, d=d: e.tensor_copy(out=GD[:, :, c, 0:1], in_=EB[:, cidx, d, c, :].unsqueeze(2)), [EB], [GD])
                            else:
                                GP(lambda e: e.memset(GD[:, :, :, 0:1], 0.0), [GD], [GD])
                            VVB(lambda e, d=d: e.tensor_tensor_scan(
                                out=GO[:].rearrange("p g c k -> p (g c k)"), data0=R8C[:, d].rearrange("p g c k -> p (g c k)"),
                                data1=GD[:].rearrange("p g c k -> p (g c k)"), initial=0.0, op0=ALU.mult, op1=ALU.add), [R8C, GD], [GO])
                            if not final:
                                gr, gi_ = GO[:, :, 0, 64], GO[:, :, 1, 64]
                                tc, ts = TC[:, d, :, 64], TS[:, d, :, 64]
                                VV(lambda e: e.tensor_tensor(out=CT[0][:], in0=gr, in1=tc, op=ALU.mult), [GO, TC], [CT[0]])
                                VV(lambda e: e.tensor_tensor(out=CT[1][:], in0=gi_, in1=ts, op=ALU.mult), [GO, TS], [CT[1]])
                                VV(lambda e, d=d: e.tensor_tensor(out=EB[:, b, d, 0, :], in0=CT[0][:], in1=CT[1][:], op=ALU.subtract), [CT[0], CT[1]], [EB])
                                VV(lambda e: e.tensor_tensor(out=CT[0][:], in0=gr, in1=ts, op=ALU.mult), [GO, TS], [CT[0]])
                                VV(lambda e: e.tensor_tensor(out=CT[1][:], in0=gi_, in1=tc, op=ALU.mult), [GO, TC], [CT[1]])
                                VV(lambda e, d=d: e.tensor_tensor(out=EB[:, b, d, 1, :], in0=CT[0][:], in1=CT[1][:], op=ALU.add), [CT[0], CT[1]], [EB])
                            else:
                                Gr, Gi = GO[:, :, 0, 0:64], GO[:, :, 1, 0:64]
                                if d == 0:
                                    Hr, Hi = HB[:, :, 0, :], HB[:, :, 1, :]
                                else:
                                    Hr, Hi = HB[:, :, 2, ::-1], HB[:, :, 3, ::-1]
                                VVB(lambda e: e.tensor_tensor(out=M[0][:], in0=Gr, in1=Tc0, op=ALU.mult), [GO, TC], [M[0]])
                                VVB(lambda e: e.tensor_tensor(out=M[1][:], in0=Gi, in1=Ts0, op=ALU.mult), [GO, TS], [M[1]])
                                VVB(lambda e: e.tensor_tensor(out=Hr, in0=M[0][:], in1=M[1][:], op=ALU.subtract), [M[0], M[1]], [HB])
                                VVB(lambda e: e.tensor_tensor(out=M[0][:], in0=Gr, in1=Ts0, op=ALU.mult), [GO, TS], [M[0]])
                                VVB(lambda e: e.tensor_tensor(out=M[1][:], in0=Gi, in1=Tc0, op=ALU.mult), [GO, TC], [M[1]])
                                VVB(lambda e: e.tensor_tensor(out=Hi, in0=M[0][:], in1=M[1][:], op=ALU.add), [M[0], M[1]], [HB])

                    def p3(b):
                        j0 = b * 64
                        U = Ub[b % 2]
                        SAL = SALs[b % 2]
                        HB = HBs[b % 2]
                        final = True
                        if final:
                            dump_sb("HB", HB, HB[:].rearrange("p g k j -> p (g k j)"), 16 * 4 * 64, BF16)
                            dump_sb("SAL", SAL, SAL[:].rearrange("p g k j -> p (g k j)"), 16 * 4 * 64, F32)
                            dump_sb("EB", EB, EB[:].rearrange("p b d c g -> p (b d c g)"), NB * 64, F32)
                            for g in range(32):
                                g_lo, gh = g // 16, g % 16
                                pr = slice(g_lo * 64, (g_lo + 1) * 64)
                                ps = psum()
                                P.op("tensor", lambda e, ps=ps, g=g: e.matmul(ps[:, 0:64], lhsT=WI[:, g, :], rhs=U[:, g, :],
                                                                              start=True, stop=False), [WI, U], [ps])
                                for kind in range(4):
                                    P.op("tensor", lambda e, ps=ps, pr=pr, gh=gh, kind=kind: e.matmul(
                                        ps[:, 0:64], lhsT=WO[pr, gh, kind, :], rhs=HB[pr, gh, kind, :], start=False, stop=(kind == 3)),
                                        [WO, HB], [ps])
                                ys = YSTG[(g // 8) % 2]
                                copy_op("scalar", ys[:, g % 8, :], ps[:, 0:64], [ps], [ys])
                                if g % 8 == 7:
                                    g0 = g - 7
                                    P.dma("gpsimd", lambda e, ys=ys, g0=g0: e.dma_start(
                                        out=YS.rearrange("(g r) j -> r g j", r=128)[:, g0:g0 + 8, j0:j0 + 64], in_=ys[:]),
                                        reads=[ys], writes=["YS"])

                    for b in range(NB):
                        p1(b)
                        p2(b, False)

                    def cmul_add(dst_idx, src_idx, d):
                        ar, ai = A64[:, d, 0, :], A64[:, d, 1, :]
                        cr, ci = EB[:, src_idx, d, 0, :], EB[:, src_idx, d, 1, :]
                        VV(lambda e: e.tensor_tensor(out=CT[0][:], in0=ar, in1=cr, op=ALU.mult), [A64, EB], [CT[0]])
                        VV(lambda e: e.tensor_tensor(out=CT[1][:], in0=ai, in1=ci, op=ALU.mult), [A64, EB], [CT[1]])
                        VV(lambda e: e.tensor_tensor(out=CT[0][:], in0=CT[0][:], in1=CT[1][:], op=ALU.subtract), [CT[0], CT[1]], [CT[0]])
                        VV(lambda e: e.tensor_tensor(out=CT[1][:], in0=ar, in1=ci, op=ALU.mult), [A64, EB], [CT[1]])
                        VV(lambda e: e.tensor_tensor(out=CT[2][:], in0=ai, in1=cr, op=ALU.mult), [A64, EB], [CT[2]])
                        VV(lambda e: e.tensor_tensor(out=CT[1][:], in0=CT[1][:], in1=CT[2][:], op=ALU.add), [CT[1], CT[2]], [CT[1]])
                        VV(lambda e: e.tensor_tensor(out=EB[:, dst_idx, d, 0, :], in0=CT[0][:], in1=EB[:, dst_idx, d, 0, :], op=ALU.add), [CT[0], EB], [EB])
                        VV(lambda e: e.tensor_tensor(out=EB[:, dst_idx, d, 1, :], in0=CT[1][:], in1=EB[:, dst_idx, d, 1, :], op=ALU.add), [CT[1], EB], [EB])

                    for b in range(NB):
                        if b not in seq_first:
                            cmul_add(b, b - 1, 0)
                    for b in range(NB - 1, -1, -1):
                        if b not in seq_last:
                            cmul_add(b, b + 1, 1)
                    p1(0)
                    for b in range(NB):
                        p2(b, True)
                        if b + 1 < NB:
                            p1(b + 1)
                        p3(b)
                    P.barrier()
        def ssd_alloc(st):
            S = {}
            S["NSEL"] = sbuf(st, "NSEL", [64, 16, 128], BF16)
            with contextlib.ExitStack() as tmpst:
                nself = sbuf(tmpst, "NSELf", [64, 2048], F32)
                P.dma("sync", lambda e: e.dma_start(out=nself[:], in_=cst["c_nsel"]), writes=[nself])
                P.op("vector", lambda e: e.tensor_copy(out=S["NSEL"][:].rearrange("p h t -> p (h t)"), in_=nself[:]), [nself], [S["NSEL"]])
                P.barrier()
            S["XC"] = sbuf(st, "XC", [128, 16, TT], BF16)
            S["DTR"] = sbuf(st, "DTR", [16, TT], F32)
            S["DTA"] = sbuf(st, "DTA", [16, TT], F32)
            S["DTK"] = sbuf(st, "DTK", [128, 32], F32)
            S["EALL"] = sbuf(st, "EALL", [128, 48], F32)
            S["NAC"] = sbuf(st, "NAC", [128, 16], F32)
            S["ACT_"] = None
            S["XDT"] = sbuf(st, "XDT", [128, 16, 64], BF16)
            S["XDD"] = sbuf(st, "XDD", [128, 16, 64], BF16)
            S["BTK"] = sbuf(st, "BTK", [128, 512], BF16)
            S["CBS4"] = sbuf(st, "CBS4", [128, 4, 128], F32)
            S["EBC"] = [sbuf(st, "EBC%d" % i, [128, 4, 128], BF16) for i in range(4)]
            S["LEX"] = [sbuf(st, "LEX%d" % i, [128, 4, 128], BF16) for i in range(4)]
            S["GM"] = [sbuf(st, "GM%d" % i, [128, 4, 128], BF16) for i in range(4)]
            S["CH"] = [sbuf(st, "CH%d" % i, [128, 4, 128], BF16) for i in range(4)]
            S["STMP8"] = sbuf(st, "STMP8", [128, 16, 64], F32)
            S["SIN"] = sbuf(st, "SIN", [128, 16, 64], F32)
            S["SINB"] = sbuf(st, "SINB", [128, 16, 64], BF16)
            S["YB"] = sbuf(st, "YB", [128, 8, TT], F32)
            S["sel2"] = sbuf(st, "sel2", [64, 16], F32)
            P.dma("sync", lambda e: e.dma_start(out=S["sel2"][:], in_=cst["c_sel2"]), writes=[S["sel2"]])
            S["A2"] = sbuf(st, "A2", [64, 128], BF16)
            S["HT"] = sbuf(st, "HT", [64, 128], BF16)
            S["AF32"] = sbuf(st, "AF32", [64, 128], F32)
            S["A2blk"] = sbuf(st, "A2blk", [64, 16, 128], BF16)

            P.op("gpsimd", lambda e: e.memset(S["A2"][:], 0.0), [], [S["A2"]])
            return S

        def ssd_dt(l, d, S, t0):
            DTR, DTA = S["DTR"], S["DTA"]
            P.dma("sync", lambda e: e.dma_start(out=DTR[:], in_=DTs[16 * d:16 * d + 16, t0:t0 + TT]), reads=["DTs"], writes=[DTR])
            OPB("scalar", lambda e: e.activation(out=DTR[:], in_=DTR[:], func=AF.Exp, bias=hpar_sb[:, l, 2 * d + 1:2 * d + 2], scale=1.0), [DTR, hpar_sb], [DTR])
            OPB("scalar", lambda e: e.activation(out=DTR[:], in_=DTR[:], func=AF.Ln, bias=1.0, scale=1.0), [DTR], [DTR])
            OPB("vector", lambda e: e.tensor_scalar(out=DTA[:], in0=DTR[:], scalar1=hder[:, l, 2 * d:2 * d + 1], scalar2=None, op0=ALU.mult), [DTR, hder], [DTA])

        def ssd_chunk(l, d, S, c, first_chunk_of_seq):
            XC, DTR, DTK, EALL, NAC, ACT_, XDT, XDD, BTK = (S[k] for k in ("XC", "DTR", "DTK", "EALL", "NAC", "ACT_", "XDT", "XDD", "BTK"))
            DTA = S["DTA"]
            SIN, SINB, YB = S["SIN"], S["SINB"], S["YB"]
            sel2, A2, HT, AF32, A2blk, NSEL = S["sel2"], S["A2"], S["HT"], S["AF32"], S["A2blk"], S["NSEL"]
            cs = slice(c * 128, (c + 1) * 128)
            triX, tuX = tri[:, 2 * d, :], tri[:, 2 * d + 1, :]
            if first_chunk_of_seq:
                P.op("gpsimd", lambda e: e.memset(SIN[:].rearrange("p h q -> p (h q)"), 0.0), [SIN], [SIN])
                P.op("gpsimd", lambda e: e.memset(SINB[:].rearrange("p h q -> p (h q)"), 0.0), [SINB], [SINB])
            ps = psum()
            P.op("tensor", lambda e: e.transpose(ps[:, 0:16], DTR[:, cs], ident_f[0:16, 0:16]), [DTR, ident_f], [ps])
            P.op("tensor", lambda e: e.transpose(ps[:, 16:32], DTA[:, cs], ident_f[0:16, 0:16]), [DTA, ident_f], [ps])
            P.op("vector", lambda e: e.tensor_copy(out=DTK[:], in_=ps[:, 0:32]), [ps], [DTK])
            lvl = dbg.get("ssd_level", 9)
            if lvl < 2:
                return
            pe = psum()
            dta = DTK[:, 16:32]
            P.op("tensor", lambda e: e.matmul(pe[:, 0:16], lhsT=triX, rhs=dta, start=True, stop=True), [tri, DTK], [pe])
            P.op("tensor", lambda e: e.matmul(pe[:, 16:32], lhsT=tuX, rhs=dta, start=True, stop=True), [tri, DTK], [pe])
            P.op("tensor", lambda e: e.matmul(pe[:, 32:48], lhsT=onesF[:], rhs=dta, start=True, stop=True), [onesF, DTK], [pe])
            P.op("tensor", lambda e: e.matmul(pe[0:16, 64:192], lhsT=dta, rhs=triX, start=True, stop=True), [tri, DTK], [pe])
            P.op("tensor", lambda e: e.matmul(pe[32:48, 64:192], lhsT=dta, rhs=triX, start=True, stop=True), [tri, DTK], [pe])
            if lvl < 2.2:
                return
            P.op("scalar", lambda e: e.activation(out=EALL[:], in_=pe[:, 0:48], func=AF.Exp), [pe], [EALL])
            if lvl < 2.4:
                return
            P.op("scalar", lambda e: e.mul(out=NAC[:], in_=pe[:, 0:16], mul=-1.0), [pe], [NAC])
            if lvl < 2.6:
                return
            P.op("scalar", lambda e: e.copy(out=A2[0:16, :], in_=pe[0:16, 64:192]), [pe], [A2])
            P.op("scalar", lambda e: e.copy(out=HT[32:48, :], in_=pe[32:48, 64:192]), [pe], [HT])
            P.op("scalar", lambda e: e.copy(out=AF32[32:48, :], in_=pe[32:48, 64:192]), [pe], [AF32])
            P.op("vector", lambda e: e.tensor_tensor(out=A2[32:48, :], in0=AF32[32:48, :], in1=HT[32:48, :], op=ALU.subtract), [AF32, HT], [A2])
            P.op("gpsimd", lambda e: e.tensor_tensor(out=A2blk[:], in0=A2[:].unsqueeze(1).to_broadcast([64, 16, 128]),
                                                     in1=sel2[:].unsqueeze(2).to_broadcast([64, 16, 128]), op=ALU.mult), [A2, sel2], [A2blk], big=True)
            if lvl < 3:
                return
            px = psum()
            pxv = px[:].bitcast(BF16)
            for ct in range(8):
                P.op("tensor", lambda e, ct=ct: e.transpose(pxv[:, ct * 128:(ct + 1) * 128], XC[:, ct, cs], ident_b[:]), [XC, ident_b], [px])
            OPB("vector", lambda e: e.tensor_tensor(out=XDT[:], in0=pxv[:, 0:1024].rearrange("p (h q) -> p h q", q=64),
                                                     in1=DTK[:, 0:16].unsqueeze(2).to_broadcast([128, 16, 64]), op=ALU.mult), [px, DTK], [XDT])
            OPB("gpsimd", lambda e: e.tensor_tensor(out=XDD[:], in0=XDT[:], in1=EALL[:, 16:32].unsqueeze(2).to_broadcast([128, 16, 64]), op=ALU.mult), [XDT, EALL], [XDD])
            pb = psum()
            pbv = pb[:].bitcast(BF16)
            for g in range(4):
                P.op("tensor", lambda e, g=g: e.transpose(pbv[:, g * 128:(g + 1) * 128], XC[:, 8 + g, cs], ident_b[:]), [XC, ident_b], [pb])
            OPB("scalar", lambda e: e.copy(out=BTK[:], in_=pbv[:, 0:512]), [pb], [BTK])
            if lvl < 4:
                return
            CBS4, EBCs, LEXs, GMs, CHs = S["CBS4"], S["EBC"], S["LEX"], S["GM"], S["CH"]
            pc = psum()
            for g in range(4):
                P.op("tensor", lambda e, g=g: e.matmul(pc[:, g * 128:(g + 1) * 128], lhsT=XC[:, 8 + g, cs], rhs=XC[:, 12 + g, cs], start=True, stop=True), [XC], [pc])
            OPB("scalar", lambda e: e.copy(out=CBS4[:].rearrange("p g t -> p (g t)"), in_=pc[:, 0:512]), [pc], [CBS4])
            pBs = []
            for g in range(4):
                pB = psum()
                P.op("tensor", lambda e, g=g, pB=pB: e.matmul(pB[:, 0:512], lhsT=ones1[0:64, :], rhs=A2blk[:, 4 * g:4 * g + 4, :].rearrange("p j t -> p (j t)"),
                                                              start=True, stop=True), [ones1, A2blk], [pB])
                pBs.append(pB)
            for g in range(4):
                OPB("scalar", lambda e, g=g: e.activation(out=EBCs[g][:].rearrange("p j t -> p (j t)"), in_=pBs[g][:, 0:512], func=AF.Exp), [pBs[g]], [EBCs[g]])
            pMs = []
            for g in range(4):
                pM = psum()
                P.op("tensor", lambda e, g=g, pM=pM: e.matmul(pM[:, 0:512], lhsT=ones1[0:64, :], rhs=A2blk[:, 4 * g:4 * g + 4, :].rearrange("p j t -> p (j t)"),
                                                              start=True, stop=False), [ones1, A2blk], [pM])
                P.op("tensor", lambda e, pM=pM: e.matmul(pM[:, 0:512], lhsT=ident_b[:], rhs=maskneg[:, d, :], start=False, stop=False), [ident_b, maskneg], [pM])
                P.op("tensor", lambda e, g=g, pM=pM: e.matmul(pM[:, 0:512], lhsT=A2[:], rhs=NSEL[:, 4 * g:4 * g + 4, :].rearrange("p j t -> p (j t)"),
                                                              start=False, stop=True), [A2, NSEL], [pM])
                pMs.append(pM)
            for g in range(4):
                OPB("vector", lambda e, g=g: e.tensor_tensor(out=CHs[g][:], in0=EBCs[g][:], in1=XC[:, 12 + g, cs].unsqueeze(1).to_broadcast([128, 4, 128]), op=ALU.mult), [EBCs[g], XC], [CHs[g]])
            for g in range(4):
                OPB("scalar", lambda e, g=g: e.activation(out=LEXs[g][:].rearrange("p j t -> p (j t)"), in_=pMs[g][:, 0:512], func=AF.Exp), [pMs[g]], [LEXs[g]])
            for g in range(4):
                OPB("vector", lambda e, g=g: e.tensor_tensor(out=GMs[g][:], in0=LEXs[g][:], in1=CBS4[:, g, :].unsqueeze(1).to_broadcast([128, 4, 128]), op=ALU.mult), [LEXs[g], CBS4], [GMs[g]])
            if lvl < 5:
                return
            pys = [psum(), psum()]
            for g in range(4):
                py = pys[g // 2]
                for j in range(4):
                    h = 4 * g + j
                    c0 = (g % 2) * 256 + (j // 2) * 128
                    osl = py[(j % 2) * 64:(j % 2) * 64 + 64, c0:c0 + 128]
                    P.op("tensor", lambda e, osl=osl, h=h, j=j, g=g: e.matmul(osl, lhsT=XDT[:, h, :], rhs=GMs[g][:, j, :], start=True, stop=False), [XDT, GMs[g]], [py])
                    P.op("tensor", lambda e, osl=osl, h=h, j=j, g=g: e.matmul(osl, lhsT=SINB[:, h, :], rhs=CHs[g][:, j, :], start=False, stop=True), [SINB, CHs[g]], [py])
            for gp in range(2):
                OPB("vector", lambda e, gp=gp: e.tensor_tensor(out=YB[:, 4 * gp:4 * gp + 4, cs], in0=YB[:, 4 * gp:4 * gp + 4, cs],
                                                                in1=pys[gp][:, 0:512].rearrange("p (a t) -> p a t", a=4), op=ALU.add), [YB, pys[gp]], [YB])
            if lvl < 6:
                return
            pSs = [psum(), psum()]
            for g in range(4):
                pS = pSs[g // 2]
                P.op("tensor", lambda e, g=g, pS=pS: e.matmul(pS[:, (g % 2) * 256:(g % 2) * 256 + 256], lhsT=BTK[:, g * 128:(g + 1) * 128],
                                                              rhs=XDD[:, 4 * g:4 * g + 4, :].rearrange("p h q -> p (h q)"), start=True, stop=True), [BTK, XDD], [pS])
            STMP8 = S["STMP8"]
            for gp in range(2):
                OPB("gpsimd", lambda e, gp=gp: e.tensor_tensor(out=STMP8[:, 8 * gp:8 * gp + 8, :], in0=SIN[:, 8 * gp:8 * gp + 8, :],
                                                                in1=EALL[:, 32 + 8 * gp:32 + 8 * gp + 8].unsqueeze(2).to_broadcast([128, 8, 64]), op=ALU.mult), [SIN, EALL], [STMP8])
            for gp in range(2):
                OPB("vector", lambda e, gp=gp: e.tensor_tensor(out=SIN[:, 8 * gp:8 * gp + 8, :], in0=STMP8[:, 8 * gp:8 * gp + 8, :],
                                                                in1=pSs[gp][:, 0:512].rearrange("p (h q) -> p h q", q=64), op=ALU.add), [STMP8, pSs[gp]], [SIN])
            for gp in range(2):
                OPB("scalar", lambda e, gp=gp: e.copy(out=SINB[:, 8 * gp:8 * gp + 8, :], in_=SIN[:, 8 * gp:8 * gp + 8, :]), [SIN], [SINB])

        def phase_SSDF(l):
            with contextlib.ExitStack() as st:
                S = ssd_alloc(st)
                XP = [sbuf(st, "XP%d" % i, [128, 4, TT + 4], BF16) for i in range(2)]
                DIAGW = sbuf(st, "DIAGW", [128, 16, 5, 128], BF16)
                XC, YB = S["XC"], S["YB"]
                for ct in range(16):
                    for k in range(5):
                        P.op("vector", lambda e, ct=ct, k=k: e.tensor_scalar(
                            out=DIAGW[:, ct, k, :], in0=ident_f[:], scalar1=convw_sb[:, l, ct * 6 + k:ct * 6 + k + 1], scalar2=None, op0=ALU.mult),
                            [ident_f, convw_sb], [DIAGW], big=True)
                for s in range(nseq):
                    L, off = cfg.seq_lens[s], cfg.offs[s]
                    for t in range(L // TT):
                        t0 = off + t * TT
                        lo = 2 if t > 0 else 0
                        hi = 2 if t < L // TT - 1 else 0
                        for q in range(4):
                            xp = XP[q % 2]
                            if lo == 0:
                                P.op("gpsimd", lambda e, xp=xp: e.memset(xp[:, :, 0:2], 0.0), [xp], [xp])
                            if hi == 0:
                                P.op("gpsimd", lambda e, xp=xp: e.memset(xp[:, :, TT + 2:TT + 4], 0.0), [xp], [xp])
                            P.dma("sync", lambda e, xp=xp, q=q: e.dma_start(
                                out=xp[:, :, 2 - lo:TT + 2 + hi],
                                in_=PT[1024 + 512 * q:1024 + 512 * (q + 1), t0 - lo:t0 + TT + hi].rearrange("(m p) t -> p m t", p=128)),
                                reads=["PT"], writes=[xp])
                            for m in range(4):
                                ct = 4 * q + m
                                psc = psum()
                                for k in range(5):
                                    P.op("tensor", lambda e, xp=xp, m=m, ct=ct, k=k, psc=psc: e.matmul(
                                        psc[:, 0:TT], lhsT=DIAGW[:, ct, k, :], rhs=xp[:, m, k:k + TT], start=(k == 0), stop=(k == 4)),
                                        [DIAGW, xp], [psc])
                                wv = convw_sb[:, l, ct * 6:ct * 6 + 6]
                                OPB("scalar", lambda e, ct=ct, psc=psc, wv=wv: e.activation(
                                    out=XC[:, ct, :], in_=psc[:, 0:TT], func=AF.Silu, bias=wv[:, 5:6], scale=1.0), [psc, convw_sb], [XC])
                        P.dma("gpsimd", lambda e: e.dma_start(out=PC[:, t0:t0 + TT].rearrange("(m p) t -> p m t", p=128), in_=XC[:]),
                              reads=[XC], writes=["PC"])
                        if dbg.get("no_ssd"):
                            continue
                        ssd_dt(l, 0, S, t0)
                        P.op("gpsimd", lambda e: e.memset(YB[:].rearrange("p a t -> p (a t)"), 0.0), [YB], [YB])
                        for c in range(4):
                            ssd_chunk(l, 0, S, c, first_chunk_of_seq=(t == 0 and c == 0))
                        P.dma("gpsimd", lambda e: e.dma_start(out=YF[:, t0:t0 + TT].rearrange("(m p) t -> p m t", p=128), in_=YB[:]),
                              reads=[YB], writes=["YF"])
                P.barrier()
        def phase_B1(l):
            with contextlib.ExitStack() as st:
                S = ssd_alloc(st)
                XC, YB = S["XC"], S["YB"]
                X = sbuf(st, "X1", [128, 8, TT], F32)
                ZT = sbuf(st, "ZT", [128, 8, TT], BF16)
                GTt = sbuf(st, "GTt", [128, 16, TT], BF16)
                YBN = sbuf(st, "YBN", [128, 8, TT], BF16)
                RS = sbuf(st, "RS1", [128, TT], F32)
                MG = sbuf(st, "MG", [128, 8, TT], BF16)
                YSt = sbuf(st, "YSt", [128, 4, TT], BF16)
                Ut = sbuf(st, "Ut", [128, 4, TT], BF16)
                YAf = sbuf(st, "YAf", [128, 4, TT], F32)
                T1 = sbuf(st, "T1", [128, 4, TT], F32)
                SQ = APBuf(T1[:].bitcast(BF16).rearrange("p a (b c) -> p (a b) c", c=TT), T1.n)
                YG = sbuf(st, "YG", [128, 4, TT], BF16)
                YA = sbuf(st, "YA", [128, 4, TT], BF16)
                SGm = [sbuf(st, "SGm%d" % i, [128, TT], F32) for i in range(1)]
                TM = [sbuf(st, "TM%d" % i, [128, TT], F32) for i in range(1)]
                k_i = [0]
                for s in range(nseq):
                    L, off = cfg.seq_lens[s], cfg.offs[s]
                    ntile = L // TT
                    for t in range(ntile - 1, -1, -1):
                        t0 = off + t * TT
                        j0 = t0 // 8
                        P.dma("sync", lambda e: e.dma_start(out=XC[:], in_=PC[:, t0:t0 + TT].rearrange("(m p) t -> p m t", p=128)),
                              reads=["PC"], writes=[XC])
                        P.dma("sync", lambda e: e.dma_start(out=YB[:], in_=YF[:, t0:t0 + TT].rearrange("(m p) t -> p m t", p=128)),
                              reads=["YF"], writes=[YB])
                        if not dbg.get("no_ssd"):
                            ssd_dt(l, 1, S, t0)
                            for c in range(3, -1, -1):
                                ssd_chunk(l, 1, S, c, first_chunk_of_seq=(t == ntile - 1 and c == 3))
                        else:
                            OPB("gpsimd", lambda e: e.memset(YB[:].rearrange("p a t -> p (a t)"), 0.0), [YB], [YB])
                        for ct in range(8):
                            OPB("vector", lambda e, ct=ct: e.scalar_tensor_tensor(
                                out=YB[:, ct, :], in0=XC[:, ct, :], scalar=vcol(l, 40 + ct), in1=YB[:, ct, :], op0=ALU.mult, op1=ALU.add),
                                [XC, YB, vec_sb], [YB])
                        P.dma("sync", lambda e: e.dma_start(out=ZT[:], in_=PT[0:1024, t0:t0 + TT].rearrange("(m p) t -> p m t", p=128)),
                              reads=["PT"], writes=[ZT])
                        OPB("scalar", lambda e: e.activation(out=ZT[:], in_=ZT[:], func=AF.Silu), [ZT], [ZT])
                        OPB("vector", lambda e: e.tensor_tensor(out=YB[:], in0=YB[:], in1=ZT[:], op=ALU.mult), [YB, ZT], [YB])
                        rmsnorm_fm((SQ, RS), YB, 8, TT, lambda kt: vcol(l, 32 + kt), YBN)
                        for m in range(4):
                            P.dma("sync", lambda e, m=m: e.dma_start(
                                out=YSt[:, m, :].rearrange("p (s j) -> p s j", s=8),
                                in_=YS.rearrange("(m p s) j -> m p s j", p=128, s=8)[m][:, :, j0:j0 + TT // 8]), reads=["YS"], writes=[YSt])
                            P.dma("sync", lambda e, m=m: e.dma_start(
                                out=Ut[:, m, :].rearrange("p (s j) -> p s j", s=8),
                                in_=US.rearrange("(m p s) j -> m p s j", p=128, s=8)[m][:, :, j0:j0 + TT // 8]), reads=["US"], writes=[Ut])
                        if dbg.get("no_s5"):
                            OPB("gpsimd", lambda e: e.memset(YSt[:].rearrange("p a t -> p (a t)"), 0.0), [YSt], [YSt])
                        for ct in range(4):
                            OPB("vector", lambda e, ct=ct: e.scalar_tensor_tensor(
                                out=YAf[:, ct, :], in0=Ut[:, ct, :], scalar=vcol(l, 48 + ct), in1=YSt[:, ct, :], op0=ALU.mult, op1=ALU.add),
                                [Ut, YSt, vec_sb], [YAf])
                        OPB("scalar", lambda e: e.activation(out=T1[:], in_=YAf[:], func=AF.Square), [YAf], [T1])
                        OPB("vector", lambda e: e.tensor_scalar(out=T1[:], in0=T1[:], scalar1=0.044715, scalar2=1.0, op0=ALU.mult, op1=ALU.add), [T1], [T1])
                        OPB("gpsimd", lambda e: e.tensor_tensor(out=T1[:], in0=T1[:], in1=YAf[:], op=ALU.mult), [T1, YAf], [T1])
                        OPB("scalar", lambda e: e.activation(out=T1[:], in_=T1[:], func=AF.Sigmoid, scale=1.5957691216057308), [T1], [T1])
                        OPB("vector", lambda e: e.tensor_tensor(out=YG[:].rearrange("p c (j s) -> p c s j", s=8),
                                                                 in0=YAf[:].rearrange("p c (s j) -> p c s j", s=8),
                                                                 in1=T1[:].rearrange("p c (s j) -> p c s j", s=8), op=ALU.mult), [YAf, T1], [YG])

                        def ev_glu(m, ps):
                            sg_ = SGm[0]
                            k_i[0] += 1
                            OPB("scalar", lambda e: e.activation(out=sg_[:], in_=ps[:, 0:TT], func=AF.Sigmoid), [ps], [sg_])
                            OPB("vector", lambda e: e.tensor_tensor(out=YA[:, m, :], in0=YG[:, m, :], in1=sg_[:], op=ALU.mult), [YG, sg_], [YA])
                        linear_fm(YG, 4, [(l, "s5_w_glu", 0)], TT, ev_glu)
                        P.dma("sync", lambda e: e.dma_start(out=GTt[:], in_=PT[3072:5120, t0:t0 + TT].rearrange("(m p) t -> p m t", p=128)),
                              reads=["PT"], writes=[GTt])
                        for hh in range(2):
                            OPB("scalar", lambda e, hh=hh: e.activation(out=GTt[:, 8 * hh:8 * hh + 8, :], in_=GTt[:, 8 * hh:8 * hh + 8, :], func=AF.Sigmoid), [GTt], [GTt])
                        for half in range(2):
                            ba, nka, cwa = slab((l, "w_branch_a", half))
                            bb, nkb, cwb = slab((l, "w_branch_b", half))
                            sva, svb = slab_view(ba, nka, cwa), slab_view(bb, nkb, cwb)
                            for mi in range(4):
                                m = 4 * half + mi
                                psa, psb_ = psum(), psum()
                                for kt in range(4):
                                    P.op("tensor", lambda e, psa=psa, kt=kt, mi=mi, sva=sva: e.matmul(
                                        psa[:, 0:TT], lhsT=sva[:, kt, mi * 128:(mi + 1) * 128], rhs=YA[:, kt, :], start=(kt == 0), stop=(kt == 3)), [ba, YA], [psa])
                                for kt in range(8):
                                    P.op("tensor", lambda e, psb_=psb_, kt=kt, mi=mi, svb=svb: e.matmul(
                                        psb_[:, 0:TT], lhsT=svb[:, kt, mi * 128:(mi + 1) * 128], rhs=YBN[:, kt, :], start=(kt == 0), stop=(kt == 7)), [bb, YBN], [psb_])
                                tm = TM[0]
                                OPB("vector", lambda e, psa=psa, m=m, tm=tm: e.tensor_tensor(out=tm[:], in0=psa[:, 0:TT], in1=GTt[:, m, :], op=ALU.mult), [psa, GTt], [tm])
                                tm2 = SGm[0]
                                OPB("vector", lambda e, psb_=psb_, m=m, tm2=tm2: e.tensor_tensor(out=tm2[:], in0=psb_[:, 0:TT], in1=GTt[:, 8 + m, :], op=ALU.mult), [psb_, GTt], [tm2])
                                OPB("gpsimd", lambda e, m=m, tm=tm, tm2=tm2: e.tensor_tensor(out=MG[:, m, :], in0=tm[:], in1=tm2[:], op=ALU.add), [tm, tm2], [MG])
                        src = xsrc(l)
                        P.dma("sync", lambda e: e.dma_start(out=X[:], in_=src[:, t0:t0 + TT].rearrange("(kt p) t -> p kt t", p=128)), reads=["XT"], writes=[X])

                        def add_to_X(m, ps):
                            OPB("vector", lambda e: e.tensor_tensor(out=X[:, m, :], in0=X[:, m, :], in1=ps[:, 0:TT], op=ALU.add), [X, ps], [X])
                        linear_fm(MG, 8, [(l, "w_out", 0), (l, "w_out", 1)], TT, add_to_X)
                        P.dma("gpsimd", lambda e: e.dma_start(out=XT[:, t0:t0 + TT].rearrange("(kt p) t -> p kt t", p=128), in_=X[:]), reads=[X], writes=["XT"])
                P.barrier()

        for l in range(DEPTH + 1):
            phase_TL(l)
            if l < DEPTH:
                if not dbg.get("no_s5"):
                    phase_S5(l)
                phase_SSDF(l)
                phase_B1(l)
        P.barrier(engines=("gpsimd", "sync"))
        P.emit()
    return nc, P


def host_params(inp, depth):
    f = lambda a: np.ascontiguousarray(np.asarray(a, dtype=np.float32))
    out = {}
    for n, K, N in W_SPECS:
        out[n] = f(inp[n][:depth])
    vec = np.zeros((depth, 128, 64), np.float32)

    def fm(v):
        v = np.asarray(v)
        return v.reshape(depth, -1, 128).transpose(0, 2, 1)
    vec[:, :, 0:8] = fm(inp["norm_mix"][:depth])
    vec[:, :, 8:16] = fm(inp["norm_xattn"][:depth])
    vec[:, :, 16:24] = fm(inp["norm_mem"][:depth])
    vec[:, :, 24:32] = fm(inp["norm_mlp"][:depth])
    vec[:, :, 32:40] = fm(inp["ssd_norm"][:depth])
    vec[:, :, 40:48] = fm(np.repeat(np.asarray(inp["ssd_d"][:depth]), 64, axis=1))
    vec[:, :, 48:52] = fm(inp["s5_d"][:depth])
    out["vecs"] = vec
    out["nfin"] = f(np.asarray(inp["norm_final"]).reshape(8, 128).T)
    cw = np.asarray(inp["ssd_conv_w"][:depth])
    cb = np.asarray(inp["ssd_conv_b"][:depth])
    cwb = np.concatenate([cw, cb[:, None, :]], axis=1)
    out["convw"] = f(cwb.reshape(depth, 6, 16, 128).transpose(0, 3, 2, 1).reshape(depth, 128, 96))
    hp = np.zeros((depth, 16, 4), np.float32)
    hp[:, :, 0] = np.asarray(inp["ssd_a_log"][:depth])[:, 0]
    hp[:, :, 1] = np.asarray(inp["ssd_dt_bias"][:depth])[:, 0]
    hp[:, :, 2] = np.asarray(inp["ssd_a_log"][:depth])[:, 1]
    hp[:, :, 3] = np.asarray(inp["ssd_dt_bias"][:depth])[:, 1]
    out["hpar"] = hp

    def q_layout(a):
        a = np.asarray(a)
        sh = a.shape
        a = a.reshape(sh[0], sh[1], 2, 16, 64, *sh[4:])
        perm = (0, 1, 2, 4, 3) + tuple(range(5, a.ndim))
        a = a.transpose(perm)
        return a.reshape(sh[0], sh[1], 128, 16, *sh[4:])
    lam = np.zeros((depth, 2, 128, 3, 16), np.float32)
    lam[:, :, :, 0] = q_layout(inp["s5_lam_re"][:depth])
    lam[:, :, :, 1] = q_layout(inp["s5_lam_im"][:depth])
    ldt = np.broadcast_to(np.asarray(inp["s5_log_dt"][:depth])[:, :, :, None], (depth, 2, 32, 64))
    lam[:, :, :, 2] = q_layout(ldt)
    out["s5lam"] = f(lam.reshape(depth, 2, 128, 48))
    bq = np.stack([q_layout(inp["s5_b_re"][:depth]), q_layout(inp["s5_b_im"][:depth])], axis=3)
    out["s5b"] = f(bq.reshape(depth, 2, 128, 512))
    cre = np.asarray(inp["s5_c_re"][:depth]).transpose(0, 1, 2, 4, 3)
    cim = np.asarray(inp["s5_c_im"][:depth]).transpose(0, 1, 2, 4, 3)
    cq = np.stack([q_layout(cre), q_layout(cim)], axis=3)
    out["s5c"] = f(cq.reshape(depth, 2, 128, 512))
    out.update(host_consts())
    return out


_NC_CACHE = {}


def run(inp, cfg, core_seqs, n_cores):
    key = (cfg.depth, cfg.seq_lens, repr(sorted(getattr(cfg, "debug", {}).items())))
    if key not in _NC_CACHE:
        _NC_CACHE[key] = build(cfg)
    nc, P = _NC_CACHE[key]
    shared = host_params(inp, cfg.depth)
    in_maps = []
    for c in range(n_cores):
        xs = np.concatenate([np.asarray(x, np.float32) for x, m in core_seqs[c]], axis=0)
        ms = np.concatenate([np.asarray(m, np.float32) for x, m in core_seqs[c]], axis=0)
        d = dict(shared)
        d["xT"] = np.ascontiguousarray(xs.T)
        d["memT"] = np.ascontiguousarray(ms.T)
        in_maps.append(d)
    res = run_bass_kernel_spmd(nc, in_maps, core_ids=list(range(n_cores)))
    outs = []
    for c in range(n_cores):
        yT = np.asarray(res.results[c]["yT"])
        y = np.ascontiguousarray(yT.T)
        o = []
        for off, L in zip(cfg.offs, cfg.seq_lens):
            o.append(y[off:off + L])
        outs.append(o)
    return outs, res


def kernel(**inp):
    xp = np.asarray(inp["x_prompt"])
    xs = np.asarray(inp["x_sample"])
    mp = np.asarray(inp["mem_prompt"])
    ms = np.asarray(inp["mem_sample"])
    cfg = Cfg(depth=4, seq_lens=(2048, 2048, 16384))
    core_seqs = []
    for c in range(8):
        sidx = c % 2
        core_seqs.append([(xp[2 * c], mp[2 * c]), (xp[2 * c + 1], mp[2 * c + 1]), (xs[sidx], ms[sidx])])
    outs, _ = run(inp, cfg, core_seqs, 8)
    y_prompt = np.stack([outs[c][i] for c in range(8) for i in range(2)], axis=0).astype(np.float32)
    y_sample = np.stack([outs[0][2], outs[1][2]], axis=0).astype(np.float32)
    return (y_prompt, y_sample)
```

```python
import contextlib
import math
import numpy as np
import concourse.bass as bass
import concourse.mybir as mybir
from concourse.bass_utils import run_bass_kernel_spmd

F32 = mybir.dt.float32
BF16 = mybir.dt.bfloat16
I32 = mybir.dt.int32
AF = mybir.ActivationFunctionType
ALU = mybir.AluOpType

ENGS = ["tensor", "vector", "scalar", "gpsimd", "sync"]
D = 1024
NMEM = 256
TT = 512
NEG = -30000.0


class Buf:
    def __init__(self, t, name):
        self.t = t
        self.n = name

    def __getitem__(self, k):
        return self.t[k]


class APBuf:
    def __init__(self, ap, name):
        self.ap = ap
        self.n = name

    def __getitem__(self, k):
        return self.ap[k]


def _names(xs):
    out = []
    for x in xs:
        if x is None:
            continue
        out.append(x if isinstance(x, str) else x.n)
    return out


class Prog:
    def __init__(self, nc, same_engine_sync=False, n_dma_sems=8):
        self.nc = nc
        self.ops = {e: [] for e in ENGS}
        self.cnt = {e: 0 for e in ENGS}
        self.dma_i = {e: 0 for e in ENGS}
        self.n_dma_sems = n_dma_sems
        self.synced = {}
        self.last_w = {}
        self.reads = {}
        self.same_engine_sync = same_engine_sync
        self.sems = {}
        self.ctx = []
        self.latest = {}
        self.n_ops = 0
        self.bigset = {e: set() for e in ENGS}

    def sem(self, key):
        if key not in self.sems:
            cm = self.nc.semaphore("s_" + "_".join(str(k) for k in key))
            self.sems[key] = cm.__enter__()
            self.ctx.append(cm)
        return self.sems[key]

    def _deps(self, eng, reads, writes):
        need = {}

        def add(k, v):
            if v > need.get(k, 0):
                need[k] = v
        for r in list(reads) + list(writes):
            lw = self.last_w.get(r)
            if lw is not None:
                add(*lw)
        for w in writes:
            for k, v in self.reads.get(w, {}).items():
                add(k, v)
        out = []
        for k, v in need.items():
            if k == ("c", eng) and (eng == "tensor" or not self.same_engine_sync or v in self.bigset[eng]):
                continue
            if self.synced.get((eng, k), 0) >= v:
                continue
            self.synced[(eng, k)] = v
            out.append((k, v))
        return out

    def _record(self, key, val, reads, writes):
        self.latest[key] = val
        for r in reads:
            self.reads.setdefault(r, {})[key] = val
        for w in writes:
            self.last_w[w] = (key, val)
            self.reads[w] = {}

    def op(self, eng, fn, reads=(), writes=(), big=False):
        reads, writes = _names(reads), _names(writes)
        waits = self._deps(eng, reads, writes)
        self.cnt[eng] += 1
        if big:
            self.bigset[eng].add(self.cnt[eng])
        key = ("c", eng)
        self.sem(key)
        e = getattr(self.nc, eng)
        for k, v in waits:
            e.wait_ge(self.sem(k), v)
        fn(e).then_inc(self.sems[key], 1)
        self._record(key, self.cnt[eng], reads, writes)
        self.n_ops += 1

    def dma(self, eng, fn, reads=(), writes=()):
        reads, writes = _names(reads), _names(writes)
        waits = self._deps(eng, reads, writes)
        i = self.dma_i[eng]
        self.dma_i[eng] += 1
        key = ("d", eng, i % self.n_dma_sems)
        val = 16 * (i // self.n_dma_sems + 1)
        self.sem(key)
        e = getattr(self.nc, eng)
        for k, v in waits:
            e.wait_ge(self.sem(k), v)
        fn(e).then_inc(self.sems[key], 16)
        self._record(key, val, reads, writes)
        self.n_ops += 1

    def barrier(self, engines=("tensor", "vector", "scalar", "gpsimd", "sync")):
        snap = dict(self.latest)
        for e in engines:
            waits = []
            for k, v in snap.items():
                if k == ("c", e):
                    continue
                if self.synced.get((e, k), 0) >= v:
                    continue
                self.synced[(e, k)] = v
                waits.append((k, v))
            eh = getattr(self.nc, e)
            for k, v in waits:
                eh.wait_ge(self.sem(k), v)

    def emit(self):
        for cm in reversed(self.ctx):
            cm.__exit__(None, None, None)


class Cfg:
    def __init__(self, depth=4, seq_lens=(2048, 2048, 16384), stop_after=None, same_engine_sync=True, debug=None):
        self.debug = dict(debug or {})
        self.depth = depth
        self.seq_lens = tuple(seq_lens)
        self.offs = [int(sum(seq_lens[:i])) for i in range(len(seq_lens))]
        self.Ltot = int(sum(seq_lens))
        self.nseq = len(seq_lens)
        self.stop_after = stop_after
        self.same_engine_sync = same_engine_sync
        for L in seq_lens:
            assert L % TT == 0


W_SPECS = [
    ("w_in", 1024, 5664), ("s5_w_glu", 512, 512), ("w_branch_a", 512, 1024), ("w_branch_b", 1024, 1024),
    ("w_out", 1024, 1024), ("w_q", 1024, 1024), ("w_k", 1024, 1024), ("w_v", 1024, 1024), ("w_o", 1024, 1024),
    ("w_up", 1024, 4096), ("w_down", 4096, 1024)]

C_U, C_Z, C_X, C_DT, C_G = 0, 512, 1536, 3584, 3616


def host_consts():
    c = {}
    r = np.arange(128)
    c["c_ident"] = np.eye(128, dtype=np.float32)
    tri = np.zeros((4, 128, 128), np.float32)
    tri[0] = (r[:, None] <= r[None, :])
    tri[1] = (r[:, None] > r[None, :])
    tri[2] = (r[:, None] >= r[None, :])
    tri[3] = (r[:, None] < r[None, :])
    c["c_tri"] = np.ascontiguousarray(tri.transpose(1, 0, 2))
    mk = np.zeros((2, 128, 4, 128), np.float32)
    mk[0] = np.where(r[:, None] > r[None, :], NEG, 0.0)[:, None, :]
    mk[1] = np.where(r[:, None] < r[None, :], NEG, 0.0)[:, None, :]
    c["c_mask"] = np.ascontiguousarray(mk.transpose(1, 0, 2, 3)).reshape(128, 2, 512)
    sel = np.zeros((16, 16, 128), np.float32)
    for h in range(16):
        sel[h, h, :] = 1.0
    c["c_sel"] = sel
    sel2 = np.zeros((64, 16), np.float32)
    for h in range(16):
        sel2[h, h] = 1.0
        sel2[32 + h, h] = 1.0
    c["c_sel2"] = sel2
    nsel = np.zeros((64, 16, 128), np.float32)
    for h in range(16):
        nsel[h, h, :] = -1.0
        nsel[32 + h, h, :] = -1.0
    c["c_nsel"] = nsel.reshape(64, 2048)
    s_idx = np.tile(np.arange(8), 16)
    m5 = np.zeros((128, 2, 128), np.float32)
    m5[:, 0, :] = (s_idx[None, :] >= s_idx[:, None])
    m5[:, 1, :] = (s_idx[None, :] <= s_idx[:, None])
    c["c_m5"] = m5
    kv = np.zeros((2, 4, 8), np.float32)
    s = np.arange(8, dtype=np.float32)
    kv[0, 0] = -s; kv[0, 1] = s; kv[0, 2] = 7 - s; kv[0, 3] = s + 1
    kv[1, 0] = s; kv[1, 1] = -s; kv[1, 2] = s; kv[1, 3] = 8 - s
    c["c_kv"] = np.broadcast_to(kv.reshape(1, 64), (128, 64)).copy()
    jv = np.arange(65, dtype=np.float32)
    c["c_jv"] = np.broadcast_to(jv.reshape(1, 65), (128, 65)).copy()
    return c


def build(cfg):
    nc = bass.Bass("TRN2", target_bir_lowering=False)
    P = Prog(nc, same_engine_sync=cfg.same_engine_sync)
    DEPTH, Ltot, nseq = cfg.depth, cfg.Ltot, cfg.nseq
    NT_TILES = Ltot // TT

    def dram(name, shape, dt, kind="Internal"):
        if kind == "Internal" and name in cfg.debug.get("dump", ()):
            kind = "ExternalOutput"
        return nc.dram_tensor(name, list(shape), dt, kind=kind).ap()

    xT_in = dram("xT", [D, Ltot], F32, "ExternalInput")
    memT_in = dram("memT", [D, nseq * NMEM], F32, "ExternalInput")
    yT_out = dram("yT", [D, Ltot], F32, "ExternalOutput")
    Wd = {n: dram(n, [DEPTH, K, N], F32, "ExternalInput") for n, K, N in W_SPECS}
    vecs = dram("vecs", [DEPTH, 128, 64], F32, "ExternalInput")
    nfin = dram("nfin", [128, 8], F32, "ExternalInput")
    convw = dram("convw", [DEPTH, 128, 16 * 6], F32, "ExternalInput")
    hpar = dram("hpar", [DEPTH, 16, 4], F32, "ExternalInput")
    s5lam = dram("s5lam", [DEPTH, 2, 128, 48], F32, "ExternalInput")
    s5b = dram("s5b", [DEPTH, 2, 128, 2 * 16 * 16], F32, "ExternalInput")
    s5c = dram("s5c", [DEPTH, 2, 128, 2 * 16 * 16], F32, "ExternalInput")
    cst = {k: dram(k, v.shape, F32, "ExternalInput") for k, v in host_consts().items()}

    XT = dram("XT", [D, Ltot], F32)
    PT = dram("PT", [5120, Ltot], BF16)
    PC = dram("PC", [2048, Ltot], BF16)
    US = dram("US", [4096, Ltot // 8], BF16)
    YS = dram("YS", [4096, Ltot // 8], BF16)
    DTs = dram("DTs", [32, Ltot], F32)
    YF = dram("YF", [D, Ltot], F32)
    slab_ids = {}
    n_slabs = 0
    for l in range(DEPTH):
        for n, K, N in W_SPECS:
            if n == "w_in":
                blocks = [("u", C_U, 512), ("z0", C_Z, 512), ("z1", C_Z + 512, 512)]
                blocks += [("x%d" % i, C_X + 512 * i, 512) for i in range(4)]
                blocks += [("g%d" % i, C_G + 512 * i, 512) for i in range(4)]
                blocks += [("dt", C_DT, 32)]
                for bn, c0, cw in blocks:
                    slab_ids[(l, n, bn)] = (n_slabs, 8, c0, cw); n_slabs += 1
            elif n == "w_down":
                for i in range(8):
                    slab_ids[(l, n, i)] = (n_slabs, 32, 128 * i, 128); n_slabs += 1
            else:
                for i in range(N // 512):
                    slab_ids[(l, n, i)] = (n_slabs, K // 128, 512 * i, 512); n_slabs += 1
    WS = dram("WS", [n_slabs, 128, 4096], BF16)

    with contextlib.ExitStack() as glob:
        uniq = [0]

        def sbuf(st, name, shape, dt):
            uniq[0] += 1
            nm = "%s_%d" % (name, uniq[0])
            return Buf(st.enter_context(nc.sbuf_tensor(nm, list(shape), dt)), nm)

        psb = [Buf(glob.enter_context(nc.psum_tensor("ps%d" % i, [128, 512], F32)), "ps%d" % i) for i in range(8)]
        ps_i = [0]

        def psum():
            b = psb[ps_i[0] % 8]
            ps_i[0] += 1
            return b

        def dump_sb(name, buf, ap2d, ncols, dt):
            if name not in cfg.debug.get("dumpsb", ()):
                return
            if name in dumped:
                return
            dumped.add(name)
            dd = nc.dram_tensor("dbg_" + name, [ap2d.shape[0], ncols], dt, kind="ExternalOutput").ap()
            P.dma("gpsimd", lambda e: e.dma_start(out=dd, in_=ap2d), reads=[buf], writes=["dbg_" + name])
        dumped = set()

        ident_f = sbuf(glob, "ident_f", [128, 128], F32)
        ident_b = sbuf(glob, "ident_b", [128, 128], BF16)
        onesM = sbuf(glob, "onesM", [128, 128], BF16)
        ones1 = sbuf(glob, "ones1", [128, 128], BF16)
        onesF = sbuf(glob, "onesF", [128, 128], F32)
        tri = sbuf(glob, "tri", [128, 4, 128], F32)
        maskneg = sbuf(glob, "maskneg", [128, 2, 512], BF16)
        m5 = sbuf(glob, "m5", [128, 2, 128], F32)
        kvc = sbuf(glob, "kvc", [128, 64], F32)
        jvc = sbuf(glob, "jvc", [128, 65], F32)
        vec_sb = sbuf(glob, "vec_sb", [128, DEPTH, 64], F32)
        nfin_sb = sbuf(glob, "nfin_sb", [128, 8], F32)
        convw_sb = sbuf(glob, "convw_sb", [128, DEPTH, 96], F32)
        hpar_sb = sbuf(glob, "hpar_sb", [16, DEPTH, 4], F32)
        hder = sbuf(glob, "hder", [16, DEPTH, 4], F32)

        def ld(dst, src, eng="sync"):
            P.dma(eng, lambda e: e.dma_start(out=dst[:], in_=src), writes=[dst])

        with contextlib.ExitStack() as st0:
            tmpf = sbuf(st0, "tmpf", [128, 1024], F32)
            ld(ident_f, cst["c_ident"])
            ld(tri, cst["c_tri"])
            ld(m5, cst["c_m5"])
            ld(kvc, cst["c_kv"])
            ld(jvc, cst["c_jv"])
            ld(nfin_sb, nfin)
            P.dma("sync", lambda e: e.dma_start(out=vec_sb[:], in_=vecs.rearrange("l p c -> p l c")), writes=[vec_sb])
            P.dma("sync", lambda e: e.dma_start(out=convw_sb[:], in_=convw.rearrange("l p c -> p l c")), writes=[convw_sb])
            P.dma("sync", lambda e: e.dma_start(out=hpar_sb[:], in_=hpar.rearrange("l p c -> p l c")), writes=[hpar_sb])
            P.dma("sync", lambda e: e.dma_start(out=tmpf[:], in_=cst["c_mask"].rearrange("p a b -> p (a b)")), writes=[tmpf])
            P.op("vector", lambda e: e.tensor_copy(out=maskneg[:].rearrange("p a b -> p (a b)"), in_=tmpf[:]), [tmpf], [maskneg])
            P.op("vector", lambda e: e.tensor_copy(out=ident_b[:], in_=ident_f[:]), [ident_f], [ident_b])
            P.op("gpsimd", lambda e: e.memset(onesM[:], 1.0 / 1024.0), [], [onesM])
            P.op("gpsimd", lambda e: e.memset(ones1[:], 1.0), [], [ones1])
            P.op("gpsimd", lambda e: e.memset(onesF[:], 1.0), [], [onesF])
            P.op("scalar", lambda e: e.activation(out=hder[:], in_=hpar_sb[:], func=AF.Exp), [hpar_sb], [hder])
            P.op("vector", lambda e: e.tensor_scalar(out=hder[:], in0=hder[:], scalar1=-1.0, scalar2=None, op0=ALU.mult), [hder], [hder])
            P.barrier()

        with contextlib.ExitStack() as st1:
            wf = [sbuf(st1, "wf%d" % i, [128, 4096], F32) for i in range(2)]
            wb = [sbuf(st1, "wb%d" % i, [128, 4096], BF16) for i in range(3)]
            k = 0
            for (l, n, bn), (sid, nk, c0, cw) in slab_ids.items():
                f, b = wf[k % 2], wb[k % 3]
                src = Wd[n][l, :, c0:c0 + cw].rearrange("(kt p) c -> p kt c", p=128)
                ne = nk * cw
                P.dma("sync", lambda e, f=f, src=src, nk=nk, ne=ne: e.dma_start(
                    out=f[:, 0:ne].rearrange("p (kt c) -> p kt c", kt=nk), in_=src), writes=[f])
                ceng = ["vector", "scalar", "gpsimd"][k % 3]
                if ceng == "scalar":
                    P.op("scalar", lambda e, f=f, b=b, ne=ne: e.copy(out=b[:, 0:ne], in_=f[:, 0:ne]), [f], [b])
                else:
                    P.op(ceng, lambda e, f=f, b=b, ne=ne: e.tensor_copy(out=b[:, 0:ne], in_=f[:, 0:ne]), [f], [b])
                P.dma("gpsimd", lambda e, b=b, sid=sid, ne=ne: e.dma_start(out=WS[sid, :, 0:ne], in_=b[:, 0:ne]),
                      reads=[b], writes=["WS%d" % sid])
                k += 1
            P.barrier()

        NRING = 3
        wring = [sbuf(glob, "wr%d" % i, [128, 4096], BF16) for i in range(NRING)]
        wr_i = [0]

        def slab(key):
            sid, nk, c0, cw = slab_ids[key]
            b = wring[wr_i[0] % NRING]
            wr_i[0] += 1
            ne = nk * cw
            P.dma("sync", lambda e: e.dma_start(out=b[:, 0:ne], in_=WS[sid, :, 0:ne]), reads=["WS%d" % sid], writes=[b])
            return b, nk, cw

        def slab_view(b, nk, cw):
            return b[:, 0:nk * cw].rearrange("p (kt c) -> p kt c", kt=nk)

        def OPB(eng, fn, r=(), w=()):
            P.op(eng, fn, r, w, big=True)

        ev_i = [0]

        def evac_eng():
            ev_i[0] += 1
            return "scalar" if ev_i[0] % 2 else "vector"

        def copy_op(eng, out_ap, in_ap, reads, writes, big=True):
            if eng == "scalar":
                P.op("scalar", lambda e: e.copy(out=out_ap, in_=in_ap), reads, writes, big=big)
            else:
                P.op(eng, lambda e: e.tensor_copy(out=out_ap, in_=in_ap), reads, writes, big=big)

        def sn(buf, k):
            return "%s:%d" % (buf.n, k)

        def subs(buf, n):
            return [sn(buf, k) for k in range(n)]

        def linear_fm(act, nk, keys, ntok, evac, sub=False):
            m = 0
            for key in keys:
                b, snk, cw = slab(key)
                sv = slab_view(b, snk, cw)
                for mi in range(cw // 128):
                    ps = psum()
                    for kt in range(nk):
                        P.op("tensor", lambda e, ps=ps, sv=sv, kt=kt, mi=mi: e.matmul(
                            ps[:, 0:ntok], lhsT=sv[:, kt, mi * 128:(mi + 1) * 128], rhs=act[:, kt, 0:ntok],
                            start=(kt == 0), stop=(kt == nk - 1)), [b, sn(act, kt) if sub else act], [ps])
                    evac(m, ps)
                    m += 1

        def rmsnorm_fm(st_bufs, x, nk, ntok, gain_ap_fn, out, sub=False):
            sq, rs = st_bufs
            if not sub:
                OPB("scalar", lambda e: e.activation(out=sq[:, 0:nk, 0:ntok], in_=x[:, 0:nk, 0:ntok], func=AF.Square), [x], [sq])
            else:
                for kt in range(nk):
                    if False:
                        OPB("gpsimd", lambda e, kt=kt: e.tensor_tensor(out=sq[:, kt, 0:ntok], in0=x[:, kt, 0:ntok], in1=x[:, kt, 0:ntok], op=ALU.mult), [sn(x, kt)], [sn(sq, kt)])
                    else:
                        OPB("scalar", lambda e, kt=kt: e.activation(out=sq[:, kt, 0:ntok], in_=x[:, kt, 0:ntok], func=AF.Square), [sn(x, kt)], [sn(sq, kt)])
            ps = psum()
            for kt in range(nk):
                P.op("tensor", lambda e, kt=kt: e.matmul(ps[:, 0:ntok], lhsT=onesM[:], rhs=sq[:, kt, 0:ntok],
                                                         start=(kt == 0), stop=(kt == nk - 1)), [onesM, sn(sq, kt) if sub else sq], [ps])
            OPB("scalar", lambda e: e.activation(out=rs[:, 0:ntok], in_=ps[:, 0:ntok], func=AF.Sqrt, bias=1e-6, scale=1.0), [ps], [rs])
            if not USE_DIV:
                OPB("vector", lambda e: e.reciprocal(out=rs[:, 0:ntok], in_=rs[:, 0:ntok]), [rs], [rs])
            for kt in range(nk):
                OPB("vector", lambda e, kt=kt: e.scalar_tensor_tensor(
                    out=out[:, kt, 0:ntok], in0=x[:, kt, 0:ntok], scalar=gain_ap_fn(kt), in1=rs[:, 0:ntok],
                    op0=ALU.mult, op1=(ALU.divide if USE_DIV else ALU.mult)), [sn(x, kt) if sub else x, rs], [sn(out, kt) if sub else out])

        USE_DIV = bool(cfg.debug.get("use_div", 0))

        def vcol(l, c):
            return vec_sb[:, l, c:c + 1]

        dbg = cfg.debug if hasattr(cfg, "debug") else {}

        def xsrc(l):
            return xT_in if l == 0 else XT

        def tile_list():
            out = []
            for s in range(nseq):
                for t in range(cfg.seq_lens[s] // TT):
                    out.append((s, t, cfg.offs[s] + t * TT))
            return out

        def phase_TL(l):
            with contextlib.ExitStack() as st:
                Xs = [sbuf(st, "X%d" % i, [128, 8, TT], F32) for i in range(2)]
                Xc = [Xs[0]]
                NTb = sbuf(st, "NTb", [128, 8, TT], BF16)
                SQ = sbuf(st, "SQ", [128, 8, TT], BF16)
                RS = sbuf(st, "RS", [128, TT], F32)
                STG = [sbuf(st, "STG%d" % i, [128, 4, TT], BF16) for i in range(2)]
                DTG = sbuf(st, "DTG", [32, TT], F32)
                if l > 0:
                    QT = sbuf(st, "QT", [128, 8, TT], BF16)
                    OT = sbuf(st, "OT", [128, 8, TT], BF16)
                    HUP = sbuf(st, "HUP", [128, 32, TT], BF16)
                    ET = [sbuf(st, "ET%d" % i, [128, 2, TT], BF16) for i in range(2)]
                    RD = [sbuf(st, "RD%d" % i, [128, TT], F32) for i in range(2)]
                    RL = [sbuf(st, "RL%d" % i, [128, TT], BF16) for i in range(2)]
                    KT = sbuf(st, "KT", [128, 8, NMEM], BF16)
                    VT = sbuf(st, "VT", [128, 2, D], BF16)
                    MEMX = sbuf(st, "MEMX", [128, 8, NMEM], F32)
                    MN = sbuf(st, "MN", [128, 8, NMEM], BF16)
                    SQm = sbuf(st, "SQm", [128, 8, NMEM], BF16)
                if l == DEPTH:
                    YO = sbuf(st, "YO", [128, 8, TT], F32)
                stg_i = [0]

                def kv_for_seq(ll, s):
                    P.dma("sync", lambda e: e.dma_start(
                        out=MEMX[:], in_=memT_in[:, s * NMEM:(s + 1) * NMEM].rearrange("(kt p) m -> p kt m", p=128)), writes=[MEMX])
                    rmsnorm_fm((SQm, RS), MEMX, 8, NMEM, lambda kt: vcol(ll, 16 + kt), MN)

                    def ev_k(m, ps):
                        copy_op(evac_eng(), KT[:, m, :], ps[:, 0:NMEM], [ps], [KT])
                    linear_fm(MN, 8, [(ll, "w_k", 0), (ll, "w_k", 1)], NMEM, ev_k)
                    for i in range(2):
                        b, snk, cw = slab((ll, "w_v", i))
                        sv = slab_view(b, snk, cw)
                        for mt in range(2):
                            ps = psum()
                            for kt in range(8):
                                P.op("tensor", lambda e, ps=ps, sv=sv, kt=kt, mt=mt: e.matmul(
                                    ps[:, 0:512], lhsT=MN[:, kt, mt * 128:(mt + 1) * 128], rhs=sv[:, kt, :],
                                    start=(kt == 0), stop=(kt == 7)), [b, MN], [ps])
                            copy_op(evac_eng(), VT[:, mt, i * 512:(i + 1) * 512], ps[:, 0:512], [ps], [VT])

                def add_to_X(m, ps):
                    OPB("vector", lambda e: e.tensor_tensor(out=Xc[0][:, m, :], in0=Xc[0][:, m, :], in1=ps[:, 0:TT], op=ALU.add), [sn(Xc[0], m), ps], [sn(Xc[0], m)])

                def xattn(ll):
                    rmsnorm_fm((SQ, RS), Xc[0], 8, TT, lambda kt: vcol(ll, 8 + kt), NTb, sub=True)

                    def ev_q(m, ps):
                        copy_op(evac_eng(), QT[:, m, :], ps[:, 0:TT], [ps], [sn(QT, m)])
                    linear_fm(NTb, 8, [(ll, "w_q", 0), (ll, "w_q", 1)], TT, ev_q, sub=True)
                    for hd in range(4):
                        E = ET[hd % 2]
                        Rd = RD[hd % 2]
                        for mt in range(2):
                            ps = psum()
                            for dk in range(2):
                                P.op("tensor", lambda e, ps=ps, dk=dk, mt=mt, hd=hd: e.matmul(
                                    ps[:, 0:TT], lhsT=KT[:, 2 * hd + dk, mt * 128:(mt + 1) * 128], rhs=QT[:, 2 * hd + dk, :],
                                    start=(dk == 0), stop=(dk == 1)), [KT, sn(QT, 2 * hd + dk)], [ps])
                            OPB("scalar", lambda e, ps=ps, mt=mt, E=E: e.activation(
                                out=E[:, mt, :], in_=ps[:, 0:TT], func=AF.Exp, scale=1.0 / 16.0), [ps], [E])
                        psd = psum()
                        for mt in range(2):
                            P.op("tensor", lambda e, psd=psd, mt=mt, E=E: e.matmul(
                                psd[:, 0:TT], lhsT=ones1[:], rhs=E[:, mt, :], start=(mt == 0), stop=(mt == 1)), [ones1, E], [psd])
                        OPB("vector", lambda e, psd=psd, Rd=Rd: e.reciprocal(out=Rd[:], in_=psd[:, 0:TT]), [psd], [Rd])
                        for dk in range(2):
                            pso = psum()
                            for mt in range(2):
                                P.op("tensor", lambda e, pso=pso, mt=mt, dk=dk, hd=hd, E=E: e.matmul(
                                    pso[:, 0:TT], lhsT=VT[:, mt, (2 * hd + dk) * 128:(2 * hd + dk + 1) * 128], rhs=E[:, mt, :],
                                    start=(mt == 0), stop=(mt == 1)), [VT, E], [pso])
                            OPB("vector", lambda e, pso=pso, dk=dk, hd=hd, Rd=Rd: e.tensor_tensor(
                                out=OT[:, 2 * hd + dk, :], in0=pso[:, 0:TT], in1=Rd[:], op=ALU.mult), [pso, Rd], [sn(OT, 2 * hd + dk)])
                    linear_fm(OT, 8, [(ll, "w_o", 0), (ll, "w_o", 1)], TT, add_to_X, sub=True)

                def mlp(ll):
                    rmsnorm_fm((SQ, RS), Xc[0], 8, TT, lambda kt: vcol(ll, 24 + kt), NTb, sub=True)
                    rl_i = [0]

                    def ev_up(m, ps):
                        r = RL[rl_i[0] % 2]
                        rl_i[0] += 1
                        OPB("scalar", lambda e: e.activation(out=r[:], in_=ps[:, 0:TT], func=AF.Relu), [ps], [r])
                        OPB("gpsimd", lambda e: e.tensor_tensor(out=HUP[:, m, :], in0=r[:], in1=r[:], op=ALU.mult), [r], [sn(HUP, m)])
                    linear_fm(NTb, 8, [(ll, "w_up", i) for i in range(8)], TT, ev_up, sub=True)
                    for i in range(8):
                        b, snk, cw = slab((ll, "w_down", i))
                        sv = slab_view(b, snk, cw)
                        ps = psum()
                        for kt in range(32):
                            P.op("tensor", lambda e, ps=ps, sv=sv, kt=kt: e.matmul(
                                ps[:, 0:TT], lhsT=sv[:, kt, :], rhs=HUP[:, kt, :], start=(kt == 0), stop=(kt == 31)), [b, sn(HUP, kt)], [ps])
                        add_to_X(i, ps)

                def front(ll, t0):
                    rmsnorm_fm((SQ, RS), Xc[0], 8, TT, lambda kt: vcol(ll, kt), NTb, sub=True)
                    j0 = t0 // 8
                    sg = STG[stg_i[0] % 2]
                    stg_i[0] += 1

                    def ev_u(m, ps):
                        eng = evac_eng()
                        copy_op(eng, sg[:, m, :].rearrange("p (s j) -> p s j", s=8),
                                ps[:, 0:TT].rearrange("p (j s) -> p s j", s=8), [ps], [sg])
                    linear_fm(NTb, 8, [(ll, "w_in", "u")], TT, ev_u, sub=True)
                    for m in range(4):
                        P.dma("gpsimd", lambda e, sg=sg, m=m: e.dma_start(
                            out=US.rearrange("(m p s) j -> m p s j", p=128, s=8)[m][:, :, j0:j0 + TT // 8],
                            in_=sg[:, m, :].rearrange("p (s j) -> p s j", s=8)), reads=[sg], writes=["US"])
                    blocks = [("z0", 0), ("z1", 512)] + [("x%d" % i, 1024 + 512 * i) for i in range(4)] + \
                             [("g%d" % i, 3072 + 512 * i) for i in range(4)]
                    for bn, row0 in blocks:
                        sg = STG[stg_i[0] % 2]
                        stg_i[0] += 1

                        def ev(m, ps, sg=sg):
                            copy_op(evac_eng(), sg[:, m, :], ps[:, 0:TT], [ps], [sg])
                        linear_fm(NTb, 8, [(ll, "w_in", bn)], TT, ev, sub=True)
                        P.dma("gpsimd", lambda e, sg=sg, row0=row0: e.dma_start(
                            out=PT[row0:row0 + 512, t0:t0 + TT].rearrange("(m p) t -> p m t", p=128), in_=sg[:]),
                            reads=[sg], writes=["PT"])
                    b, snk, cw = slab((ll, "w_in", "dt"))
                    sv = slab_view(b, snk, cw)
                    ps = psum()
                    for kt in range(8):
                        P.op("tensor", lambda e, ps=ps, sv=sv, kt=kt: e.matmul(
                            ps[0:32, 0:TT], lhsT=sv[:, kt, :], rhs=NTb[:, kt, :], start=(kt == 0), stop=(kt == 7)), [b, sn(NTb, kt)], [ps])
                    OPB("vector", lambda e, ps=ps: e.tensor_copy(out=DTG[:], in_=ps[0:32, 0:TT]), [ps], [DTG])
                    P.dma("gpsimd", lambda e: e.dma_start(out=DTs[:, t0:t0 + TT], in_=DTG[:]), reads=[DTG], writes=["DTs"])

                cur_seq = -1
                for ti_, (s, t, t0) in enumerate(tile_list()):
                    Xc[0] = Xs[ti_ % 2]
                    X = Xc[0]
                    if l > 0 and s != cur_seq:
                        kv_for_seq(l - 1, s)
                        cur_seq = s
                    src = xsrc(l)
                    P.dma("sync", lambda e, src=src, t0=t0: e.dma_start(
                        out=X[:], in_=src[:, t0:t0 + TT].rearrange("(kt p) t -> p kt t", p=128)), reads=["XT"], writes=subs(X, 8))
                    if l > 0:
                        if not dbg.get("no_xattn"):
                            xattn(l - 1)
                        if not dbg.get("no_mlp"):
                            mlp(l - 1)
                    if l < DEPTH:
                        front(l, t0)
                        if l > 0:
                            P.dma("gpsimd", lambda e, t0=t0: e.dma_start(
                                out=XT[:, t0:t0 + TT].rearrange("(kt p) t -> p kt t", p=128), in_=X[:]), reads=subs(X, 8), writes=["XT"])
                    else:
                        OPB("scalar", lambda e: e.activation(out=SQ[:], in_=X[:], func=AF.Square), subs(X, 8), subs(SQ, 8))
                        ps = psum()
                        for kt in range(8):
                            P.op("tensor", lambda e, ps=ps, kt=kt: e.matmul(ps[:, 0:TT], lhsT=onesM[:], rhs=SQ[:, kt, :],
                                                                     start=(kt == 0), stop=(kt == 7)), [onesM, sn(SQ, kt)], [ps])
                        OPB("scalar", lambda e, ps=ps: e.activation(out=RS[:], in_=ps[:, 0:TT], func=AF.Sqrt, bias=1e-6, scale=1.0), [ps], [RS])
                        OPB("vector", lambda e: e.reciprocal(out=RS[:], in_=RS[:]), [RS], [RS])
                        for kt in range(8):
                            OPB("vector", lambda e, kt=kt: e.scalar_tensor_tensor(
                                out=YO[:, kt, :], in0=X[:, kt, :], scalar=nfin_sb[:, kt:kt + 1], in1=RS[:],
                                op0=ALU.mult, op1=ALU.mult), [sn(X, kt), RS, nfin_sb], [YO])
                        P.dma("gpsimd", lambda e, t0=t0: e.dma_start(
                            out=yT_out[:, t0:t0 + TT].rearrange("(kt p) t -> p kt t", p=128), in_=YO[:]), reads=[YO], writes=["yT"])
                P.barrier()
        TWO_PI = 2.0 * math.pi

        def phase_S5(l):
            NB = Ltot // TT
            with contextlib.ExitStack() as st:
                WI = sbuf(st, "WI", [128, 32, 128], BF16)
                WSF = sbuf(st, "WSF", [128, 32, 4, 64], BF16)
                WO = sbuf(st, "WO", [128, 16, 4, 128], BF16)
                TC = sbuf(st, "TC", [128, 2, 16, 65], F32)
                TS = sbuf(st, "TS", [128, 2, 16, 65], F32)
                R8C = sbuf(st, "R8C", [128, 2, 16, 2, 65], F32)
                A64 = sbuf(st, "A64", [128, 2, 2, 16], F32)

                def sincos(st2, name, ph, shape, out_sin, out_cos, rw):
                    n = int(np.prod(shape))
                    t1 = sbuf(st2, name + "_t1", [128, n], F32)
                    ti = sbuf(st2, name + "_ti", [128, n], I32)
                    t2 = sbuf(st2, name + "_t2", [128, n], F32)

                    def v(b):
                        a = b[:, 0:n]
                        if len(shape) == 2:
                            return a.rearrange("p (a b) -> p a b", a=shape[0])
                        if len(shape) == 3:
                            return a.rearrange("p (a b c) -> p a b c", a=shape[0], b=shape[1])
                        return a
                    for off, outp in ((0.0, out_sin), (0.5 * math.pi, out_cos)):
                        P.op("vector", lambda e, off=off: e.tensor_scalar(out=v(t1), in0=ph, scalar1=off, scalar2=1.0 / TWO_PI,
                                                                          op0=ALU.add, op1=ALU.mult), rw, [t1])
                        P.op("vector", lambda e: e.tensor_copy(out=ti[:], in_=t1[:]), [t1], [ti])
                        P.op("vector", lambda e: e.tensor_copy(out=t2[:], in_=ti[:]), [ti], [t2])
                        P.op("vector", lambda e: e.tensor_tensor(out=t2[:], in0=t1[:], in1=t2[:], op=ALU.subtract), [t1, t2], [t2])
                        P.op("scalar", lambda e, outp=outp: e.activation(out=outp, in_=v(t2), func=AF.Sin, scale=TWO_PI), [t2], rw)

                with contextlib.ExitStack() as sg:
                    BMr = [sbuf(sg, "BMr%d" % d, [128, 16, 16, 8], BF16) for d in range(2)]
                    BMi = [sbuf(sg, "BMi%d" % d, [128, 16, 16, 8], BF16) for d in range(2)]
                    CMr = [sbuf(sg, "CMr%d" % d, [128, 16, 16, 8], BF16) for d in range(2)]
                    CMi = [sbuf(sg, "CMi%d" % d, [128, 16, 16, 8], BF16) for d in range(2)]
                    WSr1 = sbuf(sg, "WSr", [128, 16, 16, 8], BF16)
                    WSi1 = sbuf(sg, "WSi", [128, 16, 16, 8], BF16)
                    LAM = sbuf(sg, "LAM", [128, 48], F32)
                    BB = sbuf(sg, "BB", [128, 2, 16, 16], F32)
                    CC = sbuf(sg, "CC", [128, 2, 16, 16], F32)
                    SM = sbuf(sg, "SM", [128, 16, 16], F32)
                    PH = sbuf(sg, "PH", [128, 16, 32], F32)
                    MAG = sbuf(sg, "MAG", [128, 16, 32], F32)
                    SN = sbuf(sg, "SN", [128, 16, 32], F32)
                    CS = sbuf(sg, "CS", [128, 16, 32], F32)
                    ER = sbuf(sg, "ER", [128, 16, 4, 8], F32)
                    EI = sbuf(sg, "EI", [128, 16, 4, 8], F32)
                    BBR = sbuf(sg, "BBR", [128, 16, 16], F32)
                    BBI = sbuf(sg, "BBI", [128, 16, 16], F32)
                    T4 = [sbuf(sg, "T4_%d" % i, [128, 16, 16, 8], F32) for i in range(2)]
                    PH2 = sbuf(sg, "PH2", [128, 16, 65], F32)
                    all_gen = [LAM, BB, CC, SM, PH, MAG, SN, CS, ER, EI, BBR, BBI, PH2]

                    def VV(fn, r, w):
                        P.op("vector", fn, r, w)

                    for d in range(2):
                        P.dma("sync", lambda e, d=d: e.dma_start(out=LAM[:], in_=s5lam[l, d]), writes=[LAM])
                        P.dma("sync", lambda e, d=d: e.dma_start(out=BB[:].rearrange("p c g h -> p (c g h)"), in_=s5b[l, d]), writes=[BB])
                        P.dma("sync", lambda e, d=d: e.dma_start(out=CC[:].rearrange("p c g h -> p (c g h)"), in_=s5c[l, d]), writes=[CC])
                        lre, lim, ldt = LAM[:, 0:16], LAM[:, 16:32], LAM[:, 32:48]
                        P.op("scalar", lambda e: e.activation(out=SM[:, 0, :], in_=ldt, func=AF.Exp), [LAM], [SM])
                        VV(lambda e: e.tensor_tensor(out=SM[:, 1, :], in0=lre, in1=SM[:, 0, :], op=ALU.mult), [LAM, SM], [SM])
                        VV(lambda e: e.tensor_tensor(out=SM[:, 2, :], in0=lim, in1=SM[:, 0, :], op=ALU.mult), [LAM, SM], [SM])
                        kvd = kvc[:, d * 32:(d + 1) * 32]
                        VV(lambda e, kvd=kvd: e.tensor_tensor(out=PH[:], in1=SM[:, 2, :].unsqueeze(2).to_broadcast([128, 16, 32]),
                                                              in0=kvd.unsqueeze(1).to_broadcast([128, 16, 32]), op=ALU.mult), [SM, kvc], [PH])
                        VV(lambda e, kvd=kvd: e.tensor_tensor(out=MAG[:], in1=SM[:, 1, :].unsqueeze(2).to_broadcast([128, 16, 32]),
                                                              in0=kvd.unsqueeze(1).to_broadcast([128, 16, 32]), op=ALU.mult), [SM, kvc], [MAG])
                        P.op("scalar", lambda e: e.activation(out=MAG[:], in_=MAG[:], func=AF.Exp), [MAG], [MAG])
                        with contextlib.ExitStack() as s2:
                            sincos(s2, "sc1_%d" % d, PH[:], [16, 32], SN[:], CS[:], [PH, SN, CS])
                            P.barrier()
                        if True:
                            VV(lambda e: e.tensor_tensor(out=ER[:].rearrange("p g k s -> p g (k s)"), in0=MAG[:], in1=CS[:], op=ALU.mult), [MAG, CS], [ER])
                            VV(lambda e: e.tensor_tensor(out=EI[:].rearrange("p g k s -> p g (k s)"), in0=MAG[:], in1=SN[:], op=ALU.mult), [MAG, SN], [EI])
                            VV(lambda e: e.tensor_tensor(out=PH2[:], in1=SM[:, 2, :].unsqueeze(2).to_broadcast([128, 16, 65]),
                                                         in0=jvc[:].unsqueeze(1).to_broadcast([128, 16, 65]), op=ALU.mult), [SM, jvc], [PH2])
                            VV(lambda e: e.tensor_scalar(out=PH2[:], in0=PH2[:], scalar1=8.0, scalar2=None, op0=ALU.mult), [PH2], [PH2])
                            with contextlib.ExitStack() as s2:
                                sincos(s2, "sc2_%d" % d, PH2[:], [16, 65], TS[:, d], TC[:, d], [PH2, TS, TC])
                                P.barrier()
                            P.op("scalar", lambda e: e.activation(out=SM[:, 8, :], in_=SM[:, 1, :], func=AF.Exp, scale=8.0), [SM], [SM])
                            for c in range(2):
                                P.op("gpsimd", lambda e, c=c, d=d: e.memset(R8C[:, d, :, c, :], 1.0), [R8C], [R8C])
                                VV(lambda e, c=c, d=d: e.tensor_tensor(out=R8C[:, d, :, c, :], in0=R8C[:, d, :, c, :],
                                                                       in1=SM[:, 8, :].unsqueeze(2).to_broadcast([128, 16, 65]), op=ALU.mult), [SM, R8C], [R8C])
                                P.op("gpsimd", lambda e, c=c, d=d: e.memset(R8C[:, d, :, c, 0:1], 0.0), [R8C], [R8C])
                            P.op("scalar", lambda e: e.activation(out=SM[:, 9, :], in_=SM[:, 1, :], func=AF.Exp, scale=512.0), [SM], [SM])
                            VV(lambda e: e.tensor_scalar(out=SM[:, 10, :], in0=SM[:, 2, :], scalar1=512.0, scalar2=None, op0=ALU.mult), [SM], [SM])
                            with contextlib.ExitStack() as s2:
                                sincos(s2, "sc3_%d" % d, SM[:, 10, :], [16], SM[:, 11, :], SM[:, 12, :], [SM])
                                P.barrier()
                            VV(lambda e, d=d: e.tensor_tensor(out=A64[:, d, 0, :], in0=SM[:, 9, :], in1=SM[:, 12, :], op=ALU.mult), [SM], [A64])
                            VV(lambda e, d=d: e.tensor_tensor(out=A64[:, d, 1, :], in0=SM[:, 9, :], in1=SM[:, 11, :], op=ALU.mult), [SM], [A64])
                            P.barrier()
                        if d == 0:
                            dump_sb("LAM", LAM, LAM[:], 48, F32)
                            dump_sb("SM", SM, SM[:].rearrange("p a b -> p (a b)"), 256, F32)
                            dump_sb("PH", PH, PH[:].rearrange("p a b -> p (a b)"), 512, F32)
                            dump_sb("SN", SN, SN[:].rearrange("p a b -> p (a b)"), 512, F32)
                            dump_sb("MAG", MAG, MAG[:].rearrange("p a b -> p (a b)"), 512, F32)
                            dump_sb("ER", ER, ER[:].rearrange("p g k s -> p (g k s)"), 512, F32)
                        if d == 0:
                            e1r, e1i = ER[:, :, 3, 0], EI[:, :, 3, 0]
                        else:
                            e1r, e1i = ER[:, :, 2, 1], EI[:, :, 2, 1]
                        VV(lambda e: e.tensor_scalar(out=SM[:, 6, :], in0=e1r, scalar1=-1.0, scalar2=None, op0=ALU.add), [ER], [SM])
                        VV(lambda e: e.tensor_copy(out=SM[:, 7, :], in_=e1i), [EI], [SM])
                        VV(lambda e: e.tensor_tensor(out=SM[:, 3, :], in0=lre, in1=lre, op=ALU.mult), [LAM], [SM])
                        VV(lambda e: e.tensor_tensor(out=SM[:, 13, :], in0=lim, in1=lim, op=ALU.mult), [LAM], [SM])
                        VV(lambda e: e.tensor_tensor(out=SM[:, 3, :], in0=SM[:, 3, :], in1=SM[:, 13, :], op=ALU.add), [SM], [SM])
                        VV(lambda e: e.reciprocal(out=SM[:, 3, :], in_=SM[:, 3, :]), [SM], [SM])
                        VV(lambda e: e.tensor_tensor(out=SM[:, 4, :], in0=SM[:, 6, :], in1=lre, op=ALU.mult), [SM, LAM], [SM])
                        VV(lambda e: e.tensor_tensor(out=SM[:, 13, :], in0=SM[:, 7, :], in1=lim, op=ALU.mult), [SM, LAM], [SM])
                        VV(lambda e: e.tensor_tensor(out=SM[:, 4, :], in0=SM[:, 4, :], in1=SM[:, 13, :], op=ALU.add), [SM], [SM])
                        VV(lambda e: e.tensor_tensor(out=SM[:, 4, :], in0=SM[:, 4, :], in1=SM[:, 3, :], op=ALU.mult), [SM], [SM])
                        VV(lambda e: e.tensor_tensor(out=SM[:, 5, :], in0=SM[:, 7, :], in1=lre, op=ALU.mult), [SM, LAM], [SM])
                        VV(lambda e: e.tensor_tensor(out=SM[:, 13, :], in0=SM[:, 6, :], in1=lim, op=ALU.mult), [SM, LAM], [SM])
                        VV(lambda e: e.tensor_tensor(out=SM[:, 5, :], in0=SM[:, 5, :], in1=SM[:, 13, :], op=ALU.subtract), [SM], [SM])
                        VV(lambda e: e.tensor_tensor(out=SM[:, 5, :], in0=SM[:, 5, :], in1=SM[:, 3, :], op=ALU.mult), [SM], [SM])
                        frb = SM[:, 4, :].unsqueeze(2).to_broadcast([128, 16, 16])
                        fib = SM[:, 5, :].unsqueeze(2).to_broadcast([128, 16, 16])
                        t3 = T4[0][:, :, :, 0]
                        VV(lambda e: e.tensor_tensor(out=BBR[:], in0=BB[:, 0], in1=frb, op=ALU.mult), [BB, SM], [BBR])
                        VV(lambda e: e.tensor_tensor(out=t3, in0=BB[:, 1], in1=fib, op=ALU.mult), [BB, SM], [T4[0]])
                        VV(lambda e: e.tensor_tensor(out=BBR[:], in0=BBR[:], in1=t3, op=ALU.subtract), [BBR, T4[0]], [BBR])
                        VV(lambda e: e.tensor_tensor(out=BBI[:], in0=BB[:, 1], in1=frb, op=ALU.mult), [BB, SM], [BBI])
                        VV(lambda e: e.tensor_tensor(out=t3, in0=BB[:, 0], in1=fib, op=ALU.mult), [BB, SM], [T4[0]])
                        VV(lambda e: e.tensor_tensor(out=BBI[:], in0=BBI[:], in1=t3, op=ALU.add), [BBI, T4[0]], [BBI])

                        def cprod(kind, xr, xi, outr, outi, neg_imag):
                            er = ER[:, :, kind, :].unsqueeze(2).to_broadcast([128, 16, 16, 8])
                            ei = EI[:, :, kind, :].unsqueeze(2).to_broadcast([128, 16, 16, 8])
                            xrb = xr.unsqueeze(3).to_broadcast([128, 16, 16, 8])
                            xib = xi.unsqueeze(3).to_broadcast([128, 16, 16, 8])
                            rr = [ER, EI, BBR, BBI, CC, T4[0], T4[1]]
                            VV(lambda e: e.tensor_tensor(out=T4[0][:], in0=er, in1=xrb, op=ALU.mult), rr, [T4[0]])
                            VV(lambda e: e.tensor_tensor(out=T4[1][:], in0=ei, in1=xib, op=ALU.mult), rr, [T4[1]])
                            VV(lambda e: e.tensor_tensor(out=outr[:], in0=T4[0][:], in1=T4[1][:], op=ALU.subtract), rr, [outr])
                            VV(lambda e: e.tensor_tensor(out=T4[0][:], in0=er, in1=xib, op=ALU.mult), rr, [T4[0]])
                            VV(lambda e: e.tensor_tensor(out=T4[1][:], in0=ei, in1=xrb, op=ALU.mult), rr, [T4[1]])
                            if neg_imag:
                                VV(lambda e: e.scalar_tensor_tensor(out=outi[:], in0=T4[0][:], scalar=-1.0, in1=T4[1][:],
                                                                    op0=ALU.mult, op1=ALU.subtract), rr, [outi])
                            else:
                                VV(lambda e: e.tensor_tensor(out=outi[:], in0=T4[0][:], in1=T4[1][:], op=ALU.add), rr, [outi])
                        cprod(0, BBR[:], BBI[:], BMr[d], BMi[d], False)
                        cprod(1, CC[:, 0], CC[:, 1], CMr[d], CMi[d], True)
                        cprod(2, BBR[:], BBI[:], WSr1, WSi1, False)
                        for g0 in range(0, 32, 8):
                            ps = psum()
                            psv = ps[:].bitcast(BF16)
                            for gi in range(8):
                                g = g0 + gi
                                g_lo, gh = g // 16, g % 16
                                pr = slice(g_lo * 64, (g_lo + 1) * 64)
                                for c in range(2):
                                    src = WSr1 if c == 0 else WSi1
                                    col = (gi * 2 + c) * 64
                                    P.op("tensor", lambda e, psv=psv, src=src, pr=pr, gh=gh, col=col: e.transpose(
                                        psv[:, col:col + 64], src[pr, gh].rearrange("p h s -> p (h s)"), ident_b[pr, pr]),
                                        [src, ident_b], [ps])
                            copy_op(evac_eng(), WSF[:, g0:g0 + 8, 2 * d:2 * d + 2, :],
                                    psv[:, 0:1024].rearrange("p (g c q) -> p g c q", g=8, c=2), [ps], [WSF])
                        WOr = Buf(WO.t, "WO")
                        class _V:
                            def __init__(self, ap, n):
                                self.ap = ap; self.n = n
                            def __getitem__(self, k):
                                return self.ap
                        cprod(3, CC[:, 0], CC[:, 1],
                              _V(WO[:, :, 2 * d, :].rearrange("p g (h t) -> p g h t", t=8), "WO"),
                              _V(WO[:, :, 2 * d + 1, :].rearrange("p g (h t) -> p g h t", t=8), "WO"), True)
                        P.barrier()
                    TMPA = sbuf(sg, "TMPA", [128, 128], F32)
                    TMPB = sbuf(sg, "TMPB", [128, 128], F32)
                    for g in range(32):
                        g_lo, gh = g // 16, g % 16
                        pr = slice(g_lo * 64, (g_lo + 1) * 64)
                        pss = []
                        for d in range(2):
                            ps = psum()
                            P.op("tensor", lambda e, ps=ps, d=d, pr=pr, gh=gh: e.matmul(
                                ps[:, 0:128], lhsT=BMr[d][pr, gh].rearrange("p h s -> p (h s)"),
                                rhs=CMr[d][pr, gh].rearrange("p h s -> p (h s)"), start=True, stop=False), [BMr[d], CMr[d]], [ps])
                            P.op("tensor", lambda e, ps=ps, d=d, pr=pr, gh=gh: e.matmul(
                                ps[:, 0:128], lhsT=BMi[d][pr, gh].rearrange("p h s -> p (h s)"),
                                rhs=CMi[d][pr, gh].rearrange("p h s -> p (h s)"), start=False, stop=True), [BMi[d], CMi[d]], [ps])
                            pss.append(ps)
                        VV(lambda e, ps=pss[0]: e.tensor_tensor(out=TMPA[:], in0=ps[:, 0:128], in1=m5[:, 0, :], op=ALU.mult), [pss[0], m5], [TMPA])
                        VV(lambda e, ps=pss[1]: e.tensor_tensor(out=TMPB[:], in0=ps[:, 0:128], in1=m5[:, 1, :], op=ALU.mult), [pss[1], m5], [TMPB])
                        P.op("gpsimd", lambda e, g=g: e.tensor_tensor(out=WI[:, g, :], in0=TMPA[:], in1=TMPB[:], op=ALU.add), [TMPA, TMPB], [WI])
                    P.barrier()

                dump_sb("WI", WI, WI[:].rearrange("p g m -> p (g m)"), 32 * 128, BF16)
                dump_sb("WSF", WSF, WSF[:].rearrange("p g k q -> p (g k q)"), 32 * 256, BF16)
                dump_sb("WO", WO, WO[:].rearrange("p g k m -> p (g k m)"), 16 * 512, BF16)
                dump_sb("TC", TC, TC[:].rearrange("p d g k -> p (d g k)"), 2 * 16 * 65, F32)
                dump_sb("TS", TS, TS[:].rearrange("p d g k -> p (d g k)"), 2 * 16 * 65, F32)
                dump_sb("R8C", R8C, R8C[:].rearrange("p d g c k -> p (d g c k)"), 2 * 16 * 2 * 65, F32)
                dump_sb("A64", A64, A64[:].rearrange("p d c g -> p (d c g)"), 64, F32)
                with contextlib.ExitStack() as sb_:
                    Ub = [sbuf(sb_, "Ub%d" % i, [128, 32, 64], BF16) for i in range(2)]
                    SALs = [sbuf(sb_, "SAL%d" % i, [128, 16, 4, 64], F32) for i in range(2)]
                    GD = sbuf(sb_, "GD", [128, 16, 2, 65], F32)
                    GO = sbuf(sb_, "GO", [128, 16, 2, 65], F32)
                    M = [sbuf(sb_, "M%d" % i, [128, 16, 64], F32) for i in range(2)]
                    HBs = [sbuf(sb_, "HB%d" % i, [128, 16, 4, 64], BF16) for i in range(2)]
                    EB = sbuf(sb_, "EB", [128, NB, 2, 2, 16], F32)
                    YSTG = [sbuf(sb_, "YSTG%d" % i, [128, 8, 64], BF16) for i in range(2)]
                    CT = [sbuf(sb_, "CT%d" % i, [128, 16], F32) for i in range(3)]

                    def VV(fn, r, w):
                        P.op("vector", fn, r, w)

                    def GP(fn, r, w):
                        P.op("gpsimd", fn, r, w)

                    def VVB(fn, r, w):
                        P.op("vector", fn, r, w, big=True)

                    def GPB(fn, r, w):
                        P.op("gpsimd", fn, r, w, big=True)

                    seq_first = set(o // TT for o in cfg.offs)
                    seq_last = set((o + L) // TT - 1 for o, L in zip(cfg.offs, cfg.seq_lens))

                    def p1(b):
                        j0 = b * 64
                        U = Ub[b % 2]
                        SAL = SALs[b % 2]
                        P.dma("sync", lambda e: e.dma_start(out=U[:], in_=US.rearrange("(g r) j -> r g j", r=128)[:, :, j0:j0 + 64]),
                              reads=["US"], writes=[U])
                        for gh in range(16):
                            ps = psum()
                            for g_lo in range(2):
                                g = g_lo * 16 + gh
                                for kind in range(4):
                                    P.op("tensor", lambda e, ps=ps, g=g, g_lo=g_lo, kind=kind: e.matmul(
                                        ps[g_lo * 64:(g_lo + 1) * 64, kind * 64:(kind + 1) * 64], lhsT=WSF[:, g, kind, :], rhs=U[:, g, :],
                                        start=True, stop=True), [WSF, U], [ps])
                            copy_op("scalar", SAL[:, gh].rearrange("p k j -> p (k j)"), ps[:, 0:256], [ps], [SAL])

                    def p2(b, final):
                        SAL = SALs[b % 2]
                        HB = HBs[b % 2]
                        for d in range(2):
                            if d == 0:
                                Sr, Si = SAL[:, :, 0, :], SAL[:, :, 1, :]
                            else:
                                Sr, Si = SAL[:, :, 2, ::-1], SAL[:, :, 3, ::-1]
                            Tc1, Ts1 = TC[:, d, :, 1:65], TS[:, d, :, 1:65]
                            Tc0, Ts0 = TC[:, d, :, 0:64], TS[:, d, :, 0:64]
                            VVB(lambda e: e.tensor_tensor(out=M[0][:], in0=Sr, in1=Tc1, op=ALU.mult), [SAL, TC], [M[0]])
                            VVB(lambda e: e.tensor_tensor(out=M[1][:], in0=Si, in1=Ts1, op=ALU.mult), [SAL, TS], [M[1]])
                            VVB(lambda e: e.tensor_tensor(out=GD[:, :, 0, 1:65], in0=M[0][:], in1=M[1][:], op=ALU.add), [M[0], M[1]], [GD])
                            VVB(lambda e: e.tensor_tensor(out=M[0][:], in0=Si, in1=Tc1, op=ALU.mult), [SAL, TC], [M[0]])
                            VVB(lambda e: e.tensor_tensor(out=M[1][:], in0=Sr, in1=Ts1, op=ALU.mult), [SAL, TS], [M[1]])
                            VVB(lambda e: e.tensor_tensor(out=GD[:, :, 1, 1:65], in0=M[0][:], in1=M[1][:], op=ALU.subtract), [M[0], M[1]], [GD])
                            cidx = (b - 1) if d == 0 else (b + 1)
                            has_carry = final and ((d == 0 and b not in seq_first) or (d == 1 and b not in seq_last))
                            if has_carry:
                                for c in range(2):
                                    VV(lambda e, c=c, cidx=cidx, d=d: e.tensor_copy(out=GD[:, :, c, 0:1], in_=EB[:, cidx, d, c, :].unsqueeze(2)), [EB], [GD])
                            else:
                                GP(lambda e: e.memset(GD[:, :, :, 0:1], 0.0), [GD], [GD])
                            VVB(lambda e, d=d: e.tensor_tensor_scan(
                                out=GO[:].rearrange("p g c k -> p (g c k)"), data0=R8C[:, d].rearrange("p g c k -> p (g c k)"),
                                data1=GD[:].rearrange("p g c k -> p (g c k)"), initial=0.0, op0=ALU.mult, op1=ALU.add), [R8C, GD], [GO])
                            if not final:
                                gr, gi_ = GO[:, :, 0, 64], GO[:, :, 1, 64]
                                tc, ts = TC[:, d, :, 64], TS[:, d, :, 64]
                                VV(lambda e: e.tensor_tensor(out=CT[0][:], in0=gr, in1=tc, op=ALU.mult), [GO, TC], [CT[0]])
                                VV(lambda e: e.tensor_tensor(out=CT[1][:], in0=gi_, in1=ts, op=ALU.mult), [GO, TS], [CT[1]])
                                VV(lambda e, d=d: e.tensor_tensor(out=EB[:, b, d, 0, :], in0=CT[0][:], in1=CT[1][:], op=ALU.subtract), [CT[0], CT[1]], [EB])
                                VV(lambda e: e.tensor_tensor(out=CT[0][:], in0=gr, in1=ts, op=ALU.mult), [GO, TS], [CT[0]])
                                VV(lambda e: e.tensor_tensor(out=CT[1][:], in0=gi_, in1=tc, op=ALU.mult), [GO, TC], [CT[1]])
                                VV(lambda e, d=d: e.tensor_tensor(out=EB[:, b, d, 1, :], in0=CT[0][:], in1=CT[1][:], op=ALU.add), [CT[0], CT[1]], [EB])
                            else:
                                Gr, Gi = GO[:, :, 0, 0:64], GO[:, :, 1, 0:64]
                                if d == 0:
                                    Hr, Hi = HB[:, :, 0, :], HB[:, :, 1, :]
                                else:
                                    Hr, Hi = HB[:, :, 2, ::-1], HB[:, :, 3, ::-1]
                                VVB(lambda e: e.tensor_tensor(out=M[0][:], in0=Gr, in1=Tc0, op=ALU.mult), [GO, TC], [M[0]])
                                VVB(lambda e: e.tensor_tensor(out=M[1][:], in0=Gi, in1=Ts0, op=ALU.mult), [GO, TS], [M[1]])
                                VVB(lambda e: e.tensor_tensor(out=Hr, in0=M[0][:], in1=M[1][:], op=ALU.subtract), [M[0], M[1]], [HB])
                                VVB(lambda e: e.tensor_tensor(out=M[0][:], in0=Gr, in1=Ts0, op=ALU.mult), [GO, TS], [M[0]])
                                VVB(lambda e: e.tensor_tensor(out=M[1][:], in0=Gi, in1=Tc0, op=ALU.mult), [GO, TC], [M[1]])
                                VVB(lambda e: e.tensor_tensor(out=Hi, in0=M[0][:], in1=M[1][:], op=ALU.add), [M[0], M[1]], [HB])

                    def p3(b):
                        j0 = b * 64
                        U = Ub[b % 2]
                        SAL = SALs[b % 2]
                        HB = HBs[b % 2]
                        final = True
                        if final:
                            dump_sb("HB", HB, HB[:].rearrange("p g k j -> p (g k j)"), 16 * 4 * 64, BF16)
                            dump_sb("SAL", SAL, SAL[:].rearrange("p g k j -> p (g k j)"), 16 * 4 * 64, F32)
                            dump_sb("EB", EB, EB[:].rearrange("p b d c g -> p (b d c g)"), NB * 64, F32)
                            for g in range(32):
                                g_lo, gh = g // 16, g % 16
                                pr = slice(g_lo * 64, (g_lo + 1) * 64)
                                ps = psum()
                                P.op("tensor", lambda e, ps=ps, g=g: e.matmul(ps[:, 0:64], lhsT=WI[:, g, :], rhs=U[:, g, :],
                                                                              start=True, stop=False), [WI, U], [ps])
                                for kind in range(4):
                                    P.op("tensor", lambda e, ps=ps, pr=pr, gh=gh, kind=kind: e.matmul(
                                        ps[:, 0:64], lhsT=WO[pr, gh, kind, :], rhs=HB[pr, gh, kind, :], start=False, stop=(kind == 3)),
                                        [WO, HB], [ps])
                                ys = YSTG[(g // 8) % 2]
                                copy_op("scalar", ys[:, g % 8, :], ps[:, 0:64], [ps], [ys])
                                if g % 8 == 7:
                                    g0 = g - 7
                                    P.dma("gpsimd", lambda e, ys=ys, g0=g0: e.dma_start(
                                        out=YS.rearrange("(g r) j -> r g j", r=128)[:, g0:g0 + 8, j0:j0 + 64], in_=ys[:]),
                                        reads=[ys], writes=["YS"])

                    for b in range(NB):
                        p1(b)
                        p2(b, False)

                    def cmul_add(dst_idx, src_idx, d):
                        ar, ai = A64[:, d, 0, :], A64[:, d, 1, :]
                        cr, ci = EB[:, src_idx, d, 0, :], EB[:, src_idx, d, 1, :]
                        VV(lambda e: e.tensor_tensor(out=CT[0][:], in0=ar, in1=cr, op=ALU.mult), [A64, EB], [CT[0]])
                        VV(lambda e: e.tensor_tensor(out=CT[1][:], in0=ai, in1=ci, op=ALU.mult), [A64, EB], [CT[1]])
                        VV(lambda e: e.tensor_tensor(out=CT[0][:], in0=CT[0][:], in1=CT[1][:], op=ALU.subtract), [CT[0], CT[1]], [CT[0]])
                        VV(lambda e: e.tensor_tensor(out=CT[1][:], in0=ar, in1=ci, op=ALU.mult), [A64, EB], [CT[1]])
                        VV(lambda e: e.tensor_tensor(out=CT[2][:], in0=ai, in1=cr, op=ALU.mult), [A64, EB], [CT[2]])
                        VV(lambda e: e.tensor_tensor(out=CT[1][:], in0=CT[1][:], in1=CT[2][:], op=ALU.add), [CT[1], CT[2]], [CT[1]])
                        VV(lambda e: e.tensor_tensor(out=EB[:, dst_idx, d, 0, :], in0=CT[0][:], in1=EB[:, dst_idx, d, 0, :], op=ALU.add), [CT[0], EB], [EB])
                        VV(lambda e: e.tensor_tensor(out=EB[:, dst_idx, d, 1, :], in0=CT[1][:], in1=EB[:, dst_idx, d, 1, :], op=ALU.add), [CT[1], EB], [EB])

                    for b in range(NB):
                        if b not in seq_first:
                            cmul_add(b, b - 1, 0)
                    for b in range(NB - 1, -1, -1):
                        if b not in seq_last:
                            cmul_add(b, b + 1, 1)
                    p1(0)
                    for b in range(NB):
                        p2(b, True)
                        if b + 1 < NB:
                            p1(b + 1)
                        p3(b)
                    P.barrier()
        def ssd_alloc(st):
            S = {}
            S["NSEL"] = sbuf(st, "NSEL", [64, 16, 128], BF16)
            with contextlib.ExitStack() as tmpst:
                nself = sbuf(tmpst, "NSELf", [64, 2048], F32)
                P.dma("sync", lambda e: e.dma_start(out=nself[:], in_=cst["c_nsel"]), writes=[nself])
                P.op("vector", lambda e: e.tensor_copy(out=S["NSEL"][:].rearrange("p h t -> p (h t)"), in_=nself[:]), [nself], [S["NSEL"]])
                P.barrier()
            S["XC"] = sbuf(st, "XC", [128, 16, TT], BF16)
            S["DTR"] = sbuf(st, "DTR", [16, TT], F32)
            S["DTA"] = sbuf(st, "DTA", [16, TT], F32)
            S["DTK"] = sbuf(st, "DTK", [128, 32], F32)
            S["EALL"] = sbuf(st, "EALL", [128, 48], F32)
            S["NAC"] = sbuf(st, "NAC", [128, 16], F32)
            S["ACT_"] = None
            S["XDT"] = sbuf(st, "XDT", [128, 16, 64], BF16)
            S["XDD"] = sbuf(st, "XDD", [128, 16, 64], BF16)
            S["BTK"] = sbuf(st, "BTK", [128, 512], BF16)
            S["CBS4"] = sbuf(st, "CBS4", [128, 4, 128], F32)
            S["EBC"] = [sbuf(st, "EBC%d" % i, [128, 4, 128], BF16) for i in range(4)]
            S["LEX"] = [sbuf(st, "LEX%d" % i, [128, 4, 128], BF16) for i in range(4)]
            S["GM"] = [sbuf(st, "GM%d" % i, [128, 4, 128], BF16) for i in range(4)]
            S["CH"] = [sbuf(st, "CH%d" % i, [128, 4, 128], BF16) for i in range(4)]
            S["STMP8"] = sbuf(st, "STMP8", [128, 16, 64], F32)
            S["SIN"] = sbuf(st, "SIN", [128, 16, 64], F32)
            S["SINB"] = sbuf(st, "SINB", [128, 16, 64], BF16)
            S["YB"] = sbuf(st, "YB", [128, 8, TT], F32)
            S["sel2"] = sbuf(st, "sel2", [64, 16], F32)
            P.dma("sync", lambda e: e.dma_start(out=S["sel2"][:], in_=cst["c_sel2"]), writes=[S["sel2"]])
            S["A2"] = sbuf(st, "A2", [64, 128], BF16)
            S["HT"] = sbuf(st, "HT", [64, 128], BF16)
            S["AF32"] = sbuf(st, "AF32", [64, 128], F32)
            S["A2blk"] = sbuf(st, "A2blk", [64, 16, 128], BF16)

            P.op("gpsimd", lambda e: e.memset(S["A2"][:], 0.0), [], [S["A2"]])
            return S

        def ssd_dt(l, d, S, t0):
            DTR, DTA = S["DTR"], S["DTA"]
            P.dma("sync", lambda e: e.dma_start(out=DTR[:], in_=DTs[16 * d:16 * d + 16, t0:t0 + TT]), reads=["DTs"], writes=[DTR])
            OPB("scalar", lambda e: e.activation(out=DTR[:], in_=DTR[:], func=AF.Exp, bias=hpar_sb[:, l, 2 * d + 1:2 * d + 2], scale=1.0), [DTR, hpar_sb], [DTR])
            OPB("scalar", lambda e: e.activation(out=DTR[:], in_=DTR[:], func=AF.Ln, bias=1.0, scale=1.0), [DTR], [DTR])
            OPB("vector", lambda e: e.tensor_scalar(out=DTA[:], in0=DTR[:], scalar1=hder[:, l, 2 * d:2 * d + 1], scalar2=None, op0=ALU.mult), [DTR, hder], [DTA])

        def ssd_chunk(l, d, S, c, first_chunk_of_seq):
            XC, DTR, DTK, EALL, NAC, ACT_, XDT, XDD, BTK = (S[k] for k in ("XC", "DTR", "DTK", "EALL", "NAC", "ACT_", "XDT", "XDD", "BTK"))
            DTA = S["DTA"]
            SIN, SINB, YB = S["SIN"], S["SINB"], S["YB"]
            sel2, A2, HT, AF32, A2blk, NSEL = S["sel2"], S["A2"], S["HT"], S["AF32"], S["A2blk"], S["NSEL"]
            cs = slice(c * 128, (c + 1) * 128)
            triX, tuX = tri[:, 2 * d, :], tri[:, 2 * d + 1, :]
            if first_chunk_of_seq:
                P.op("gpsimd", lambda e: e.memset(SIN[:].rearrange("p h q -> p (h q)"), 0.0), [SIN], [SIN])
                P.op("gpsimd", lambda e: e.memset(SINB[:].rearrange("p h q -> p (h q)"), 0.0), [SINB], [SINB])
            ps = psum()
            P.op("tensor", lambda e: e.transpose(ps[:, 0:16], DTR[:, cs], ident_f[0:16, 0:16]), [DTR, ident_f], [ps])
            P.op("tensor", lambda e: e.transpose(ps[:, 16:32], DTA[:, cs], ident_f[0:16, 0:16]), [DTA, ident_f], [ps])
            P.op("vector", lambda e: e.tensor_copy(out=DTK[:], in_=ps[:, 0:32]), [ps], [DTK])
            lvl = dbg.get("ssd_level", 9)
            if lvl < 2:
                return
            pe = psum()
            dta = DTK[:, 16:32]
            P.op("tensor", lambda e: e.matmul(pe[:, 0:16], lhsT=triX, rhs=dta, start=True, stop=True), [tri, DTK], [pe])
            P.op("tensor", lambda e: e.matmul(pe[:, 16:32], lhsT=tuX, rhs=dta, start=True, stop=True), [tri, DTK], [pe])
            P.op("tensor", lambda e: e.matmul(pe[:, 32:48], lhsT=onesF[:], rhs=dta, start=True, stop=True), [onesF, DTK], [pe])
            P.op("tensor", lambda e: e.matmul(pe[0:16, 64:192], lhsT=dta, rhs=triX, start=True, stop=True), [tri, DTK], [pe])
            P.op("tensor", lambda e: e.matmul(pe[32:48, 64:192], lhsT=dta, rhs=triX, start=True, stop=True), [tri, DTK], [pe])
            if lvl < 2.2:
                return
            P.op("scalar", lambda e: e.activation(out=EALL[:], in_=pe[:, 0:48], func=AF.Exp), [pe], [EALL])
            if lvl < 2.4:
                return
            P.op("scalar", lambda e: e.mul(out=NAC[:], in_=pe[:, 0:16], mul=-1.0), [pe], [NAC])
            if lvl < 2.6:
                return
            P.op("scalar", lambda e: e.copy(out=A2[0:16, :], in_=pe[0:16, 64:192]), [pe], [A2])
            P.op("scalar", lambda e: e.copy(out=HT[32:48, :], in_=pe[32:48, 64:192]), [pe], [HT])
            P.op("scalar", lambda e: e.copy(out=AF32[32:48, :], in_=pe[32:48, 64:192]), [pe], [AF32])
            P.op("vector", lambda e: e.tensor_tensor(out=A2[32:48, :], in0=AF32[32:48, :], in1=HT[32:48, :], op=ALU.subtract), [AF32, HT], [A2])
            P.op("vector", lambda e: e.tensor_tensor(out=A2blk[:], in0=A2[:].unsqueeze(1).to_broadcast([64, 16, 128]),
                                                     in1=sel2[:].unsqueeze(2).to_broadcast([64, 16, 128]), op=ALU.mult), [A2, sel2], [A2blk], big=True)
            if lvl < 3:
                return
            px = psum()
            pxv = px[:].bitcast(BF16)
            for ct in range(8):
                P.op("tensor", lambda e, ct=ct: e.transpose(pxv[:, ct * 128:(ct + 1) * 128], XC[:, ct, cs], ident_b[:]), [XC, ident_b], [px])
            OPB("vector", lambda e: e.tensor_tensor(out=XDT[:], in0=pxv[:, 0:1024].rearrange("p (h q) -> p h q", q=64),
                                                     in1=DTK[:, 0:16].unsqueeze(2).to_broadcast([128, 16, 64]), op=ALU.mult), [px, DTK], [XDT])
            OPB("gpsimd", lambda e: e.tensor_tensor(out=XDD[:], in0=XDT[:], in1=EALL[:, 16:32].unsqueeze(2).to_broadcast([128, 16, 64]), op=ALU.mult), [XDT, EALL], [XDD])
            pb = psum()
            pbv = pb[:].bitcast(BF16)
            for g in range(4):
                P.op("tensor", lambda e, g=g: e.transpose(pbv[:, g * 128:(g + 1) * 128], XC[:, 8 + g, cs], ident_b[:]), [XC, ident_b], [pb])
            OPB("scalar", lambda e: e.copy(out=BTK[:], in_=pbv[:, 0:512]), [pb], [BTK])
            if lvl < 4:
                return
            CBS4, EBCs, LEXs, GMs, CHs = S["CBS4"], S["EBC"], S["LEX"], S["GM"], S["CH"]
            pc = psum()
            for g in range(4):
                P.op("tensor", lambda e, g=g: e.matmul(pc[:, g * 128:(g + 1) * 128], lhsT=XC[:, 8 + g, cs], rhs=XC[:, 12 + g, cs], start=True, stop=True), [XC], [pc])
            OPB("scalar", lambda e: e.copy(out=CBS4[:].rearrange("p g t -> p (g t)"), in_=pc[:, 0:512]), [pc], [CBS4])
            pBs = []
            for g in range(4):
                pB = psum()
                P.op("tensor", lambda e, g=g, pB=pB: e.matmul(pB[:, 0:512], lhsT=ones1[0:64, :], rhs=A2blk[:, 4 * g:4 * g + 4, :].rearrange("p j t -> p (j t)"),
                                                              start=True, stop=True), [ones1, A2blk], [pB])
                pBs.append(pB)
            for g in range(4):
                OPB("scalar", lambda e, g=g: e.activation(out=EBCs[g][:].rearrange("p j t -> p (j t)"), in_=pBs[g][:, 0:512], func=AF.Exp), [pBs[g]], [EBCs[g]])
            pMs = []
            for g in range(4):
                pM = psum()
                P.op("tensor", lambda e, g=g, pM=pM: e.matmul(pM[:, 0:512], lhsT=ones1[0:64, :], rhs=A2blk[:, 4 * g:4 * g + 4, :].rearrange("p j t -> p (j t)"),
                                                              start=True, stop=False), [ones1, A2blk], [pM])
                P.op("tensor", lambda e, pM=pM: e.matmul(pM[:, 0:512], lhsT=ident_b[:], rhs=maskneg[:, d, :], start=False, stop=False), [ident_b, maskneg], [pM])
                P.op("tensor", lambda e, g=g, pM=pM: e.matmul(pM[:, 0:512], lhsT=A2[:], rhs=NSEL[:, 4 * g:4 * g + 4, :].rearrange("p j t -> p (j t)"),
                                                              start=False, stop=True), [A2, NSEL], [pM])
                pMs.append(pM)
            for g in range(4):
                OPB("vector", lambda e, g=g: e.tensor_tensor(out=CHs[g][:], in0=EBCs[g][:], in1=XC[:, 12 + g, cs].unsqueeze(1).to_broadcast([128, 4, 128]), op=ALU.mult), [EBCs[g], XC], [CHs[g]])
            for g in range(4):
                OPB("scalar", lambda e, g=g: e.activation(out=LEXs[g][:].rearrange("p j t -> p (j t)"), in_=pMs[g][:, 0:512], func=AF.Exp), [pMs[g]], [LEXs[g]])
            for g in range(4):
                OPB("vector", lambda e, g=g: e.tensor_tensor(out=GMs[g][:], in0=LEXs[g][:], in1=CBS4[:, g, :].unsqueeze(1).to_broadcast([128, 4, 128]), op=ALU.mult), [LEXs[g], CBS4], [GMs[g]])
            if lvl < 5:
                return
            pys = [psum(), psum()]
            for g in range(4):
                py = pys[g // 2]
                for j in range(4):
                    h = 4 * g + j
                    c0 = (g % 2) * 256 + (j // 2) * 128
                    osl = py[(j % 2) * 64:(j % 2) * 64 + 64, c0:c0 + 128]
                    P.op("tensor", lambda e, osl=osl, h=h, j=j, g=g: e.matmul(osl, lhsT=XDT[:, h, :], rhs=GMs[g][:, j, :], start=True, stop=False), [XDT, GMs[g]], [py])
                    P.op("tensor", lambda e, osl=osl, h=h, j=j, g=g: e.matmul(osl, lhsT=SINB[:, h, :], rhs=CHs[g][:, j, :], start=False, stop=True), [SINB, CHs[g]], [py])
            for gp in range(2):
                OPB("vector", lambda e, gp=gp: e.tensor_tensor(out=YB[:, 4 * gp:4 * gp + 4, cs], in0=YB[:, 4 * gp:4 * gp + 4, cs],
                                                                in1=pys[gp][:, 0:512].rearrange("p (a t) -> p a t", a=4), op=ALU.add), [YB, pys[gp]], [YB])
            if lvl < 6:
                return
            pSs = [psum(), psum()]
            for g in range(4):
                pS = pSs[g // 2]
                P.op("tensor", lambda e, g=g, pS=pS: e.matmul(pS[:, (g % 2) * 256:(g % 2) * 256 + 256], lhsT=BTK[:, g * 128:(g + 1) * 128],
                                                              rhs=XDD[:, 4 * g:4 * g + 4, :].rearrange("p h q -> p (h q)"), start=True, stop=True), [BTK, XDD], [pS])
            STMP8 = S["STMP8"]
            for gp in range(2):
                OPB("gpsimd", lambda e, gp=gp: e.tensor_tensor(out=STMP8[:, 8 * gp:8 * gp + 8, :], in0=SIN[:, 8 * gp:8 * gp + 8, :],
                                                                in1=EALL[:, 32 + 8 * gp:32 + 8 * gp + 8].unsqueeze(2).to_broadcast([128, 8, 64]), op=ALU.mult), [SIN, EALL], [STMP8])
            for gp in range(2):
                OPB("vector", lambda e, gp=gp: e.tensor_tensor(out=SIN[:, 8 * gp:8 * gp + 8, :], in0=STMP8[:, 8 * gp:8 * gp + 8, :],
                                                                in1=pSs[gp][:, 0:512].rearrange("p (h q) -> p h q", q=64), op=ALU.add), [STMP8, pSs[gp]], [SIN])
            for gp in range(2):
                OPB("scalar", lambda e, gp=gp: e.copy(out=SINB[:, 8 * gp:8 * gp + 8, :], in_=SIN[:, 8 * gp:8 * gp + 8, :]), [SIN], [SINB])

        def phase_SSDF(l):
            with contextlib.ExitStack() as st:
                S = ssd_alloc(st)
                XP = [sbuf(st, "XP%d" % i, [128, 4, TT + 4], BF16) for i in range(2)]
                DIAGW = sbuf(st, "DIAGW", [128, 16, 5, 128], BF16)
                XC, YB = S["XC"], S["YB"]
                for ct in range(16):
                    for k in range(5):
                        P.op("vector", lambda e, ct=ct, k=k: e.tensor_scalar(
                            out=DIAGW[:, ct, k, :], in0=ident_f[:], scalar1=convw_sb[:, l, ct * 6 + k:ct * 6 + k + 1], scalar2=None, op0=ALU.mult),
                            [ident_f, convw_sb], [DIAGW], big=True)
                for s in range(nseq):
                    L, off = cfg.seq_lens[s], cfg.offs[s]
                    for t in range(L // TT):
                        t0 = off + t * TT
                        lo = 2 if t > 0 else 0
                        hi = 2 if t < L // TT - 1 else 0
                        for q in range(4):
                            xp = XP[q % 2]
                            if lo == 0:
                                P.op("gpsimd", lambda e, xp=xp: e.memset(xp[:, :, 0:2], 0.0), [xp], [xp])
                            if hi == 0:
                                P.op("gpsimd", lambda e, xp=xp: e.memset(xp[:, :, TT + 2:TT + 4], 0.0), [xp], [xp])
                            P.dma("sync", lambda e, xp=xp, q=q: e.dma_start(
                                out=xp[:, :, 2 - lo:TT + 2 + hi],
                                in_=PT[1024 + 512 * q:1024 + 512 * (q + 1), t0 - lo:t0 + TT + hi].rearrange("(m p) t -> p m t", p=128)),
                                reads=["PT"], writes=[xp])
                            for m in range(4):
                                ct = 4 * q + m
                                psc = psum()
                                for k in range(5):
                                    P.op("tensor", lambda e, xp=xp, m=m, ct=ct, k=k, psc=psc: e.matmul(
                                        psc[:, 0:TT], lhsT=DIAGW[:, ct, k, :], rhs=xp[:, m, k:k + TT], start=(k == 0), stop=(k == 4)),
                                        [DIAGW, xp], [psc])
                                wv = convw_sb[:, l, ct * 6:ct * 6 + 6]
                                OPB("scalar", lambda e, ct=ct, psc=psc, wv=wv: e.activation(
                                    out=XC[:, ct, :], in_=psc[:, 0:TT], func=AF.Silu, bias=wv[:, 5:6], scale=1.0), [psc, convw_sb], [XC])
                        P.dma("gpsimd", lambda e: e.dma_start(out=PC[:, t0:t0 + TT].rearrange("(m p) t -> p m t", p=128), in_=XC[:]),
                              reads=[XC], writes=["PC"])
                        if dbg.get("no_ssd"):
                            continue
                        ssd_dt(l, 0, S, t0)
                        P.op("gpsimd", lambda e: e.memset(YB[:].rearrange("p a t -> p (a t)"), 0.0), [YB], [YB])
                        for c in range(4):
                            ssd_chunk(l, 0, S, c, first_chunk_of_seq=(t == 0 and c == 0))
                        P.dma("gpsimd", lambda e: e.dma_start(out=YF[:, t0:t0 + TT].rearrange("(m p) t -> p m t", p=128), in_=YB[:]),
                              reads=[YB], writes=["YF"])
                P.barrier()
        def phase_B1(l):
            with contextlib.ExitStack() as st:
                S = ssd_alloc(st)
                XC, YB = S["XC"], S["YB"]
                X = sbuf(st, "X1", [128, 8, TT], F32)
                ZT = sbuf(st, "ZT", [128, 8, TT], BF16)
                GTt = sbuf(st, "GTt", [128, 16, TT], BF16)
                YBN = sbuf(st, "YBN", [128, 8, TT], BF16)
                RS = sbuf(st, "RS1", [128, TT], F32)
                MG = sbuf(st, "MG", [128, 8, TT], BF16)
                YSt = sbuf(st, "YSt", [128, 4, TT], BF16)
                Ut = sbuf(st, "Ut", [128, 4, TT], BF16)
                YAf = sbuf(st, "YAf", [128, 4, TT], F32)
                T1 = sbuf(st, "T1", [128, 4, TT], F32)
                SQ = APBuf(T1[:].bitcast(BF16).rearrange("p a (b c) -> p (a b) c", c=TT), T1.n)
                YG = sbuf(st, "YG", [128, 4, TT], BF16)
                YA = sbuf(st, "YA", [128, 4, TT], BF16)
                SGm = [sbuf(st, "SGm%d" % i, [128, TT], F32) for i in range(1)]
                TM = [sbuf(st, "TM%d" % i, [128, TT], F32) for i in range(1)]
                k_i = [0]
                for s in range(nseq):
                    L, off = cfg.seq_lens[s], cfg.offs[s]
                    ntile = L // TT
                    for t in range(ntile - 1, -1, -1):
                        t0 = off + t * TT
                        j0 = t0 // 8
                        P.dma("sync", lambda e: e.dma_start(out=XC[:], in_=PC[:, t0:t0 + TT].rearrange("(m p) t -> p m t", p=128)),
                              reads=["PC"], writes=[XC])
                        P.dma("sync", lambda e: e.dma_start(out=YB[:], in_=YF[:, t0:t0 + TT].rearrange("(m p) t -> p m t", p=128)),
                              reads=["YF"], writes=[YB])
                        if not dbg.get("no_ssd"):
                            ssd_dt(l, 1, S, t0)
                            for c in range(3, -1, -1):
                                ssd_chunk(l, 1, S, c, first_chunk_of_seq=(t == ntile - 1 and c == 3))
                        else:
                            OPB("gpsimd", lambda e: e.memset(YB[:].rearrange("p a t -> p (a t)"), 0.0), [YB], [YB])
                        for ct in range(8):
                            OPB("vector", lambda e, ct=ct: e.scalar_tensor_tensor(
                                out=YB[:, ct, :], in0=XC[:, ct, :], scalar=vcol(l, 40 + ct), in1=YB[:, ct, :], op0=ALU.mult, op1=ALU.add),
                                [XC, YB, vec_sb], [YB])
                        P.dma("sync", lambda e: e.dma_start(out=ZT[:], in_=PT[0:1024, t0:t0 + TT].rearrange("(m p) t -> p m t", p=128)),
                              reads=["PT"], writes=[ZT])
                        OPB("scalar", lambda e: e.activation(out=ZT[:], in_=ZT[:], func=AF.Silu), [ZT], [ZT])
                        OPB("vector", lambda e: e.tensor_tensor(out=YB[:], in0=YB[:], in1=ZT[:], op=ALU.mult), [YB, ZT], [YB])
                        rmsnorm_fm((SQ, RS), YB, 8, TT, lambda kt: vcol(l, 32 + kt), YBN)
                        for m in range(4):
                            P.dma("sync", lambda e, m=m: e.dma_start(
                                out=YSt[:, m, :].rearrange("p (s j) -> p s j", s=8),
                                in_=YS.rearrange("(m p s) j -> m p s j", p=128, s=8)[m][:, :, j0:j0 + TT // 8]), reads=["YS"], writes=[YSt])
                            P.dma("sync", lambda e, m=m: e.dma_start(
                                out=Ut[:, m, :].rearrange("p (s j) -> p s j", s=8),
                                in_=US.rearrange("(m p s) j -> m p s j", p=128, s=8)[m][:, :, j0:j0 + TT // 8]), reads=["US"], writes=[Ut])
                        if dbg.get("no_s5"):
                            OPB("gpsimd", lambda e: e.memset(YSt[:].rearrange("p a t -> p (a t)"), 0.0), [YSt], [YSt])
                        for ct in range(4):
                            OPB("vector", lambda e, ct=ct: e.scalar_tensor_tensor(
                                out=YAf[:, ct, :], in0=Ut[:, ct, :], scalar=vcol(l, 48 + ct), in1=YSt[:, ct, :], op0=ALU.mult, op1=ALU.add),
                                [Ut, YSt, vec_sb], [YAf])
                        OPB("scalar", lambda e: e.activation(out=T1[:], in_=YAf[:], func=AF.Square), [YAf], [T1])
                        OPB("vector", lambda e: e.tensor_scalar(out=T1[:], in0=T1[:], scalar1=0.044715, scalar2=1.0, op0=ALU.mult, op1=ALU.add), [T1], [T1])
                        OPB("vector", lambda e: e.tensor_tensor(out=T1[:], in0=T1[:], in1=YAf[:], op=ALU.mult), [T1, YAf], [T1])
                        OPB("scalar", lambda e: e.activation(out=T1[:], in_=T1[:], func=AF.Sigmoid, scale=1.5957691216057308), [T1], [T1])
                        OPB("vector", lambda e: e.tensor_tensor(out=YG[:].rearrange("p c (j s) -> p c s j", s=8),
                                                                 in0=YAf[:].rearrange("p c (s j) -> p c s j", s=8),
                                                                 in1=T1[:].rearrange("p c (s j) -> p c s j", s=8), op=ALU.mult), [YAf, T1], [YG])

                        def ev_glu(m, ps):
                            sg_ = SGm[0]
                            k_i[0] += 1
                            OPB("scalar", lambda e: e.activation(out=sg_[:], in_=ps[:, 0:TT], func=AF.Sigmoid), [ps], [sg_])
                            OPB("vector", lambda e: e.tensor_tensor(out=YA[:, m, :], in0=YG[:, m, :], in1=sg_[:], op=ALU.mult), [YG, sg_], [YA])
                        linear_fm(YG, 4, [(l, "s5_w_glu", 0)], TT, ev_glu)
                        P.dma("sync", lambda e: e.dma_start(out=GTt[:], in_=PT[3072:5120, t0:t0 + TT].rearrange("(m p) t -> p m t", p=128)),
                              reads=["PT"], writes=[GTt])
                        for hh in range(2):
                            OPB("scalar", lambda e, hh=hh: e.activation(out=GTt[:, 8 * hh:8 * hh + 8, :], in_=GTt[:, 8 * hh:8 * hh + 8, :], func=AF.Sigmoid), [GTt], [GTt])
                        for half in range(2):
                            ba, nka, cwa = slab((l, "w_branch_a", half))
                            bb, nkb, cwb = slab((l, "w_branch_b", half))
                            sva, svb = slab_view(ba, nka, cwa), slab_view(bb, nkb, cwb)
                            for mi in range(4):
                                m = 4 * half + mi
                                psa, psb_ = psum(), psum()
                                for kt in range(4):
                                    P.op("tensor", lambda e, psa=psa, kt=kt, mi=mi, sva=sva: e.matmul(
                                        psa[:, 0:TT], lhsT=sva[:, kt, mi * 128:(mi + 1) * 128], rhs=YA[:, kt, :], start=(kt == 0), stop=(kt == 3)), [ba, YA], [psa])
                                for kt in range(8):
                                    P.op("tensor", lambda e, psb_=psb_, kt=kt, mi=mi, svb=svb: e.matmul(
                                        psb_[:, 0:TT], lhsT=svb[:, kt, mi * 128:(mi + 1) * 128], rhs=YBN[:, kt, :], start=(kt == 0), stop=(kt == 7)), [bb, YBN], [psb_])
                                tm = TM[0]
                                OPB("vector", lambda e, psa=psa, m=m, tm=tm: e.tensor_tensor(out=tm[:], in0=psa[:, 0:TT], in1=GTt[:, m, :], op=ALU.mult), [psa, GTt], [tm])
                                tm2 = SGm[0]
                                OPB("vector", lambda e, psb_=psb_, m=m, tm2=tm2: e.tensor_tensor(out=tm2[:], in0=psb_[:, 0:TT], in1=GTt[:, 8 + m, :], op=ALU.mult), [psb_, GTt], [tm2])
                                OPB("gpsimd", lambda e, m=m, tm=tm, tm2=tm2: e.tensor_tensor(out=MG[:, m, :], in0=tm[:], in1=tm2[:], op=ALU.add), [tm, tm2], [MG])
                        src = xsrc(l)
                        P.dma("sync", lambda e: e.dma_start(out=X[:], in_=src[:, t0:t0 + TT].rearrange("(kt p) t -> p kt t", p=128)), reads=["XT"], writes=[X])

                        def add_to_X(m, ps):
                            OPB("vector", lambda e: e.tensor_tensor(out=X[:, m, :], in0=X[:, m, :], in1=ps[:, 0:TT], op=ALU.add), [X, ps], [X])
                        linear_fm(MG, 8, [(l, "w_out", 0), (l, "w_out", 1)], TT, add_to_X)
                        P.dma("gpsimd", lambda e: e.dma_start(out=XT[:, t0:t0 + TT].rearrange("(kt p) t -> p kt t", p=128), in_=X[:]), reads=[X], writes=["XT"])
                P.barrier()

        for l in range(DEPTH + 1):
            phase_TL(l)
            if l < DEPTH:
                if not dbg.get("no_s5"):
                    phase_S5(l)
                phase_SSDF(l)
                phase_B1(l)
        P.barrier(engines=("gpsimd", "sync"))
        P.emit()
    return nc, P


def host_params(inp, depth):
    f = lambda a: np.ascontiguousarray(np.asarray(a, dtype=np.float32))
    out = {}
    for n, K, N in W_SPECS:
        out[n] = f(inp[n][:depth])
    vec = np.zeros((depth, 128, 64), np.float32)

    def fm(v):
        v = np.asarray(v)
        return v.reshape(depth, -1, 128).transpose(0, 2, 1)
    vec[:, :, 0:8] = fm(inp["norm_mix"][:depth])
    vec[:, :, 8:16] = fm(inp["norm_xattn"][:depth])
    vec[:, :, 16:24] = fm(inp["norm_mem"][:depth])
    vec[:, :, 24:32] = fm(inp["norm_mlp"][:depth])
    vec[:, :, 32:40] = fm(inp["ssd_norm"][:depth])
    vec[:, :, 40:48] = fm(np.repeat(np.asarray(inp["ssd_d"][:depth]), 64, axis=1))
    vec[:, :, 48:52] = fm(inp["s5_d"][:depth])
    out["vecs"] = vec
    out["nfin"] = f(np.asarray(inp["norm_final"]).reshape(8, 128).T)
    cw = np.asarray(inp["ssd_conv_w"][:depth])
    cb = np.asarray(inp["ssd_conv_b"][:depth])
    cwb = np.concatenate([cw, cb[:, None, :]], axis=1)
    out["convw"] = f(cwb.reshape(depth, 6, 16, 128).transpose(0, 3, 2, 1).reshape(depth, 128, 96))
    hp = np.zeros((depth, 16, 4), np.float32)
    hp[:, :, 0] = np.asarray(inp["ssd_a_log"][:depth])[:, 0]
    hp[:, :, 1] = np.asarray(inp["ssd_dt_bias"][:depth])[:, 0]
    hp[:, :, 2] = np.asarray(inp["ssd_a_log"][:depth])[:, 1]
    hp[:, :, 3] = np.asarray(inp["ssd_dt_bias"][:depth])[:, 1]
    out["hpar"] = hp

    def q_layout(a):
        a = np.asarray(a)
        sh = a.shape
        a = a.reshape(sh[0], sh[1], 2, 16, 64, *sh[4:])
        perm = (0, 1, 2, 4, 3) + tuple(range(5, a.ndim))
        a = a.transpose(perm)
        return a.reshape(sh[0], sh[1], 128, 16, *sh[4:])
    lam = np.zeros((depth, 2, 128, 3, 16), np.float32)
    lam[:, :, :, 0] = q_layout(inp["s5_lam_re"][:depth])
    lam[:, :, :, 1] = q_layout(inp["s5_lam_im"][:depth])
    ldt = np.broadcast_to(np.asarray(inp["s5_log_dt"][:depth])[:, :, :, None], (depth, 2, 32, 64))
    lam[:, :, :, 2] = q_layout(ldt)
    out["s5lam"] = f(lam.reshape(depth, 2, 128, 48))
    bq = np.stack([q_layout(inp["s5_b_re"][:depth]), q_layout(inp["s5_b_im"][:depth])], axis=3)
    out["s5b"] = f(bq.reshape(depth, 2, 128, 512))
    cre = np.asarray(inp["s5_c_re"][:depth]).transpose(0, 1, 2, 4, 3)
    cim = np.asarray(inp["s5_c_im"][:depth]).transpose(0, 1, 2, 4, 3)
    cq = np.stack([q_layout(cre), q_layout(cim)], axis=3)
    out["s5c"] = f(cq.reshape(depth, 2, 128, 512))
    out.update(host_consts())
    return out


_NC_CACHE = {}


def run(inp, cfg, core_seqs, n_cores):
    key = (cfg.depth, cfg.seq_lens, repr(sorted(getattr(cfg, "debug", {}).items())))
    if key not in _NC_CACHE:
        _NC_CACHE[key] = build(cfg)
    nc, P = _NC_CACHE[key]
    shared = host_params(inp, cfg.depth)
    in_maps = []
    for c in range(n_cores):
        xs = np.concatenate([np.asarray(x, np.float32) for x, m in core_seqs[c]], axis=0)
        ms = np.concatenate([np.asarray(m, np.float32) for x, m in core_seqs[c]], axis=0)
        d = dict(shared)
        d["xT"] = np.ascontiguousarray(xs.T)
        d["memT"] = np.ascontiguousarray(ms.T)
        in_maps.append(d)
    res = run_bass_kernel_spmd(nc, in_maps, core_ids=list(range(n_cores)))
    outs = []
    for c in range(n_cores):
        yT = np.asarray(res.results[c]["yT"])
        y = np.ascontiguousarray(yT.T)
        o = []
        for off, L in zip(cfg.offs, cfg.seq_lens):
            o.append(y[off:off + L])
        outs.append(o)
    return outs, res


def kernel(**inp):
    xp = np.asarray(inp["x_prompt"])
    xs = np.asarray(inp["x_sample"])
    mp = np.asarray(inp["mem_prompt"])
    ms = np.asarray(inp["mem_sample"])
    cfg = Cfg(depth=4, seq_lens=(2048, 2048, 16384))
    core_seqs = []
    for c in range(8):
        sidx = c % 2
        core_seqs.append([(xp[2 * c], mp[2 * c]), (xp[2 * c + 1], mp[2 * c + 1]), (xs[sidx], ms[sidx])])
    outs, _ = run(inp, cfg, core_seqs, 8)
    y_prompt = np.stack([outs[c][i] for c in range(8) for i in range(2)], axis=0).astype(np.float32)
    y_sample = np.stack([outs[0][2], outs[1][2]], axis=0).astype(np.float32)
    return (y_prompt, y_sample)
```
